# Optimizing a Trainium2 kernel written in Bass

```python
import jax, jax.numpy as jnp
from jax import lax
import numpy as np

D_MODEL = 2048
BATCH = 2
SEQ = 8192
DEPTH = 1

HEAD_DIM = 64
ATT_WIDTH = 1024
RWKV_WIDTH = 1024
MIX_WIDTH = ATT_WIDTH + RWKV_WIDTH
N_ATT_HEADS = ATT_WIDTH // HEAD_DIM
N_RWKV_HEADS = RWKV_WIDTH // HEAD_DIM
DIL_PATTERNS = ((128, 1), (512, 4), (2048, 16))
ATT_BLOCK = 128
ROPE_THETA = 10000.0
DECAY_LORA = 64
AAA_LORA = 64
GATE_LORA = 160
NORM_EPS = 1e-5
LNX_EPS = 64e-5
SHIFT_WIDTH = 3 * RWKV_WIDTH + DECAY_LORA + AAA_LORA + GATE_LORA
IN_WIDTH = 4 * ATT_WIDTH + SHIFT_WIDTH + RWKV_WIDTH

kernel_name = 'hymba_rwkv7_dilated_attention_hybrid'


def rms_norm(x, g):
    xf = x.astype(jnp.float32)
    y = xf * lax.rsqrt(jnp.mean(xf * xf, axis=-1, keepdims=True) + NORM_EPS)
    return (y * g.astype(jnp.float32)).astype(x.dtype)


def apply_rope(t, pos):
    inv_freq = ROPE_THETA ** (-jnp.arange(0, HEAD_DIM, 2, dtype=jnp.float32) / HEAD_DIM)
    ang = pos.astype(jnp.float32)[:, None] * inv_freq[None, :]
    cos = jnp.cos(ang)[None, :, None, :]
    sin = jnp.sin(ang)[None, :, None, :]
    t = t.astype(jnp.float32)
    t1, t2 = t[..., : HEAD_DIM // 2], t[..., HEAD_DIM // 2:]
    return jnp.concatenate([t1 * cos - t2 * sin, t2 * cos + t1 * sin], axis=-1)


def dilated_window_attention(q, k, v, dilation, n_back):
    B, S, H, Dh = q.shape
    L = S // dilation
    nb = -(-L // ATT_BLOCK)
    Lp = nb * ATT_BLOCK

    def to_classes(t):
        t = t.reshape(B, L, dilation, H, Dh).transpose(0, 2, 3, 1, 4)
        return jnp.pad(t, ((0, 0), (0, 0), (0, 0), (0, Lp - L), (0, 0)))

    def windows(t):
        t = jnp.pad(t, ((0, 0), (0, 0), (0, 0), (ATT_BLOCK, 0), (0, 0)))
        t = t.reshape(B, dilation, H, nb + 1, ATT_BLOCK, Dh)
        return jnp.concatenate([t[:, :, :, :-1], t[:, :, :, 1:]], axis=4)

    qb = to_classes(q).reshape(B, dilation, H, nb, ATT_BLOCK, Dh)
    kw = windows(to_classes(k))
    vw = windows(to_classes(v))
    s = jnp.einsum('bdhnqc,bdhnkc->bdhnqk', qb, kw)
    qi = jnp.arange(ATT_BLOCK)[:, None]
    ki = jnp.arange(2 * ATT_BLOCK)[None, :]
    rel = ATT_BLOCK + qi - ki
    band = (rel >= 0) & (rel <= n_back)
    blk = jnp.arange(nb)[:, None, None]
    valid = band[None] & ((blk > 0) | (ki[None] >= ATT_BLOCK))
    s = jnp.where(valid, s, -jnp.inf)
    m = jnp.max(s, axis=-1, keepdims=True)
    p = jnp.exp(s - m)
    den = jnp.sum(p, axis=-1)
    o = jnp.einsum('bdhnqk,bdhnkc->bdhnqc', p, vw) / den[..., None]
    lse = m[..., 0] + jnp.log(den)
    o = o.reshape(B, dilation, H, Lp, Dh)[:, :, :, :L].transpose(0, 3, 1, 2, 4).reshape(B, S, H, Dh)
    lse = lse.reshape(B, dilation, H, Lp)[:, :, :, :L].transpose(0, 3, 1, 2).reshape(B, S, H)
    return o, lse


def dilated_mixture_attention(q, k, v, pos):
    B, S, _ = q.shape
    q = apply_rope(q.reshape(B, S, N_ATT_HEADS, HEAD_DIM), pos) * (HEAD_DIM ** -0.5)
    k = apply_rope(k.reshape(B, S, N_ATT_HEADS, HEAD_DIM), pos)
    v = v.reshape(B, S, N_ATT_HEADS, HEAD_DIM)
    outs, lses = [], []
    for window, dilation in DIL_PATTERNS:
        o, lse = dilated_window_attention(q, k, v, dilation, window // dilation)
        outs.append(o)
        lses.append(lse)
    wts = jax.nn.softmax(jnp.stack(lses), axis=0)
    o = jnp.einsum('pbsh,pbshc->bshc', wts, jnp.stack(outs))
    return o.reshape(B, S, ATT_WIDTH)


def rwkv7_recurrence(r, w, k, v, a, b):
    B, S, H, N = r.shape

    def step(state, inp):
        r_t, w_t, k_t, v_t, a_t, b_t = inp
        sa = jnp.einsum('bhij,bhj->bhi', state, a_t)
        state = (state * w_t[:, :, None, :] + sa[..., None] * b_t[:, :, None, :]
                 + v_t[..., None] * k_t[:, :, None, :])
        y = jnp.einsum('bhij,bhj->bhi', state, r_t)
        return state, y

    xs = tuple(t.transpose(1, 0, 2, 3) for t in (r, w, k, v, a, b))
    init = jnp.zeros((B, H, N, N), jnp.float32)
    _, y = lax.scan(step, init, xs)
    return y.transpose(1, 0, 2, 3)


def rwkv7_time_mix(u, z, mu, w0, w2, a0, a2, g2, k_k, k_a, r_k, lnx_g, lnx_b):
    B, S, _ = u.shape
    R = RWKV_WIDTH
    u_prev = jnp.pad(u, ((0, 0), (1, 0), (0, 0)))[:, :S]
    u = u + (u_prev - u) * mu
    r, k, v, wl, al, gl = jnp.split(
        u, [R, 2 * R, 3 * R, 3 * R + DECAY_LORA, 3 * R + DECAY_LORA + AAA_LORA], axis=-1)
    w_log = -jax.nn.softplus(-(w0 + jnp.matmul(jnp.tanh(wl), w2))) - 0.5
    decay = jnp.exp(-jnp.exp(w_log))
    a = jax.nn.sigmoid(a0 + jnp.matmul(al, a2))
    g = jnp.matmul(jax.nn.sigmoid(gl), g2)

    def heads(t):
        return t.reshape(B, S, N_RWKV_HEADS, HEAD_DIM)

    kk = heads(k * k_k)
    kk = kk / jnp.maximum(jnp.sqrt(jnp.sum(kk * kk, axis=-1, keepdims=True)), 1e-12)
    k = k * (1.0 + (a - 1.0) * k_a)
    rh, kh, vh, ah = heads(r), heads(k), heads(v), heads(a)
    y = rwkv7_recurrence(rh, heads(decay), kh, vh, -kk, kk * ah)
    mean = jnp.mean(y, axis=-1, keepdims=True)
    var = jnp.mean(jnp.square(y - mean), axis=-1, keepdims=True)
    y = ((y - mean) * lax.rsqrt(var + LNX_EPS)).reshape(B, S, R) * lnx_g + lnx_b
    bonus = (jnp.sum(rh * kh * r_k, axis=-1, keepdims=True) * vh).reshape(B, S, R)
    return (y + bonus) * g * jax.nn.silu(z)


def setup_inputs(seed: int = 0) -> dict:
    key = jax.random.key(seed)
    ks = jax.random.split(key, 16)
    f32 = jnp.float32
    R = RWKV_WIDTH
    x = jax.random.normal(ks[0], (BATCH, SEQ, D_MODEL), f32)
    norm_g = 1.0 + 0.02 * jax.random.normal(ks[1], (DEPTH, D_MODEL), f32)
    w_in = jax.random.normal(ks[2], (DEPTH, D_MODEL, IN_WIDTH), f32) * D_MODEL ** -0.5
    shift_mu = jax.random.uniform(ks[3], (DEPTH, SHIFT_WIDTH), f32)
    w0 = jnp.linspace(-6.0, -1.0, R, dtype=f32)[None, :] + 0.1 * jax.random.normal(ks[4], (DEPTH, R), f32)
    w2 = jax.random.normal(ks[5], (DEPTH, DECAY_LORA, R), f32) * (0.1 * DECAY_LORA ** -0.5)
    a0 = 0.1 * jax.random.normal(ks[6], (DEPTH, R), f32)
    a2 = jax.random.normal(ks[7], (DEPTH, AAA_LORA, R), f32) * (0.5 * AAA_LORA ** -0.5)
    g2 = jax.random.normal(ks[8], (DEPTH, GATE_LORA, R), f32) * GATE_LORA ** -0.5
    k_k = 0.85 + 0.02 * jax.random.normal(ks[9], (DEPTH, R), f32)
    k_a = 1.0 + 0.02 * jax.random.normal(ks[10], (DEPTH, R), f32)
    r_k = -0.04 + 0.05 * jax.random.normal(ks[11], (DEPTH, N_RWKV_HEADS, HEAD_DIM), f32)
    lnx_g = 1.0 + 0.02 * jax.random.normal(ks[12], (DEPTH, R), f32)
    lnx_b = 0.02 * jax.random.normal(ks[13], (DEPTH, R), f32)
    w_out = jax.random.normal(ks[14], (DEPTH, MIX_WIDTH, D_MODEL), f32) * MIX_WIDTH ** -0.5
    final_g = 1.0 + 0.02 * jax.random.normal(ks[15], (D_MODEL,), f32)
    return {'x': x, 'norm_g': norm_g, 'w_in': w_in, 'shift_mu': shift_mu, 'w0': w0,
            'w2': w2, 'a0': a0, 'a2': a2, 'g2': g2, 'k_k': k_k, 'k_a': k_a, 'r_k': r_k,
            'lnx_g': lnx_g, 'lnx_b': lnx_b, 'w_out': w_out, 'final_g': final_g}


def reference(x, norm_g, w_in, shift_mu, w0, w2, a0, a2, g2, k_k, k_a, r_k,
              lnx_g, lnx_b, w_out, final_g):
    B, S, _ = x.shape
    pos = jnp.arange(S, dtype=jnp.int32)
    A = ATT_WIDTH
    splits = [A, 2 * A, 3 * A, 4 * A, 4 * A + SHIFT_WIDTH]
    for l in range(DEPTH):
        h = rms_norm(x, norm_g[l])
        proj = jnp.matmul(h, w_in[l]).astype(jnp.float32)
        q, k, v, z_att, u_shift, z_rwkv = jnp.split(proj, splits, axis=-1)
        att = dilated_mixture_attention(q, k, v, pos) * jax.nn.silu(z_att)
        rwk = rwkv7_time_mix(u_shift, z_rwkv, shift_mu[l], w0[l], w2[l], a0[l], a2[l],
                             g2[l], k_k[l], k_a[l], r_k[l], lnx_g[l], lnx_b[l])
        mix = jnp.concatenate([att, rwk], axis=-1).astype(x.dtype)
        x = x + jnp.matmul(mix, w_out[l])
    return rms_norm(x, final_g)
```

```python
import contextlib
import types
import numpy as np
import ml_dtypes
import concourse.bass as bass
import concourse.mybir as mybir
from concourse.bass_utils import run_bass_kernel_spmd

F32 = mybir.dt.float32
BF16 = mybir.dt.bfloat16
AF = mybir.ActivationFunctionType
ALU = mybir.AluOpType
NPBF = ml_dtypes.bfloat16

ENGS = ["tensor", "vector", "scalar", "gpsimd", "sync"]

D = 2048
S = 8192
A = 1024
R = 1024
SHIFT = 3 * R + 64 + 64 + 160
TT = 512
NTT = S // TT
CH = 128
NCH = S // CH
DEC = 0.6065306597126334


def _snap(fn):
    if fn is None or fn.__closure__ is None:
        return fn
    cells = []
    for c in fn.__closure__:
        try:
            cells.append(types.CellType(c.cell_contents))
        except ValueError:
            cells.append(c)
    g = types.FunctionType(fn.__code__, fn.__globals__, fn.__name__, fn.__defaults__, tuple(cells))
    g.__kwdefaults__ = fn.__kwdefaults__
    return g


class Tok:
    __slots__ = ("name", "writes", "reads", "sem", "cnt")

    def __init__(self, name=""):
        self.name = name
        self.writes = []
        self.reads = []
        self.sem = None
        self.cnt = 0


class Tl:
    def __init__(self, P, name, shape, dt, psum=False):
        self.t = (P.ps if psum else P.sb)(name, shape, dt)
        self.k = P.tok(name)

    def __getitem__(self, i):
        return self.t[i]


def _k(t):
    return t.k if hasattr(t, "k") else t


class Prog:
    def __init__(self, nc, stack):
        self.nc = nc
        self.stack = stack
        self.ops = {e: [] for e in ENGS}
        self.ecount = {e: 0 for e in ENGS}
        self.waited = {e: {} for e in ENGS}
        self.esem = {e: stack.enter_context(nc.semaphore("s_" + e)) for e in ENGS}
        self.semobj = {("E", e): self.esem[e] for e in ENGS}
        self.nsem = len(ENGS)
        self.toks = []
        self.dtoks = {}
        self.nm = 0

    def sb(self, name, shape, dt, stack=None):
        self.nm += 1
        return (stack or self.stack).enter_context(self.nc.sbuf_tensor("%s_%d" % (name, self.nm), list(shape), dt))

    def ps(self, name, shape, dt=F32, stack=None):
        self.nm += 1
        return (stack or self.stack).enter_context(self.nc.psum_tensor("%s_%d" % (name, self.nm), list(shape), dt))

    def tok(self, name=""):
        t = Tok(name)
        self.toks.append(t)
        return t

    def dtok(self, key):
        if key not in self.dtoks:
            self.dtoks[key] = self.tok(str(key))
        return self.dtoks[key]

    def dsem(self, tok):
        if tok.sem is None:
            tok.sem = self.stack.enter_context(self.nc.semaphore("d%d" % self.nsem))
            self.nsem += 1
            self.semobj[("D", id(tok))] = tok.sem
        return tok.sem

    def op(self, eng, fn, reads=(), writes=(), dma=None):
        fn = _snap(fn)
        reads = [_k(t) for t in reads]
        writes = [_k(t) for t in writes]
        need = []
        for t in reads:
            need += t.writes
        for t in writes:
            need += t.writes
            need += t.reads
        waits = {}
        for (key, val, src) in need:
            if self.waited[eng].get(key, 0) >= val:
                continue
            if waits.get(key, 0) < val:
                waits[key] = val
        for k, v in waits.items():
            self.waited[eng][k] = v
        ev = None
        if fn is not None:
            if dma is None:
                self.ecount[eng] += 1
                ev = (("E", eng), self.ecount[eng], eng)
            else:
                dma = _k(dma)
                self.dsem(dma)
                dma.cnt += 16
                ev = (("D", id(dma)), dma.cnt, None)
            for t in reads:
                t.reads.append(ev)
            for t in writes:
                t.writes = [ev]
                t.reads = []
        self.ops[eng].append((list(waits.items()), fn, ev))

    def barrier(self):
        for e in ENGS:
            self.op(e, None, writes=self.toks)
        for t in self.toks:
            t.writes = []
            t.reads = []

    def emit(self, eng_name, e):
        for waits, fn, ev in self.ops[eng_name]:
            for key, val in waits:
                e.wait_ge(self.semobj[key], val)
            if fn is None:
                continue
            ins = fn(e)
            key, val, src = ev
            ins.then_inc(self.semobj[key], 16 if key[0] == "D" else 1)
        self.ops[eng_name] = []

    def flush(self):
        with self.nc.Block() as block:
            @block.tensor
            def _(e):
                self.emit("tensor", e)

            @block.vector
            def _(e):
                self.emit("vector", e)

            @block.scalar
            def _(e):
                self.emit("scalar", e)

            @block.gpsimd
            def _(e):
                self.emit("gpsimd", e)

            @block.sync
            def _(e):
                self.emit("sync", e)


def _consts():
    ident = np.eye(128, dtype=np.float32)
    perm = np.zeros((128, 128), np.float32)
    for m in range(128):
        p = m + 32 if (m % 64) < 32 else m - 32
        perm[p, m] = 1.0
    bd = np.zeros((128, 128), np.float32)
    bd[:64, :64] = 1.0
    bd[64:, 64:] = 1.0
    ki = np.arange(128)[:, None]
    qi = np.arange(128)[None, :]
    amask = (np.concatenate([(ki <= qi), (qi <= ki)], axis=1).astype(np.float32) - 1.0) * 30000.0
    strict = (ki < qi).astype(np.float32)
    incl = (ki <= qi).astype(np.float32)
    mask4 = np.concatenate([strict, strict, incl, incl], axis=1)
    maskts = (qi < ki).astype(np.float32)
    cb = np.concatenate([ident, perm, bd, amask, mask4, maskts, np.ones((128, 128), np.float32)], axis=1).astype(NPBF)
    ones64 = np.full((128, 64), 1.0 / 64, np.float32)
    rmask = np.ones((128, TT), np.float32)
    rmask[:, ::CH] = 0.0
    onesf = np.ones((128, 64), np.float32)
    cf = np.concatenate([ones64, rmask, onesf], axis=1)
    inv_freq = (10000.0 ** (-np.arange(0, 64, 2, dtype=np.float32) / 64)).astype(np.float32)
    pos = np.arange(S, dtype=np.float32)
    ang = (pos[:, None] * inv_freq[None, :]).astype(np.float32)
    cos = np.cos(ang).astype(np.float32).T
    sin = np.sin(ang).astype(np.float32).T
    cosf = np.tile(cos, (4, 1))
    sinf = np.concatenate([-sin, sin, -sin, sin], axis=0)
    tab = np.stack([cosf, sinf], axis=1).astype(np.float32)
    return cb, cf, tab


CB_ID, CB_PERM, CB_BD, CB_AM, CB_M4, CB_MTS = 0, 128, 256, 384, 640, 1152
CB_ONES = 1280
CB_W = 1408
CF_W = 64 + TT + 64


def build_l1(dbg=False, phases="ACD", ntt=NTT, passes=(0, 1)):
    nc = bass.Bass("TRN2", target_bir_lowering=False)
    xT = nc.dram_tensor("xT", [D, S], F32, kind="ExternalInput").ap()
    wsl = nc.dram_tensor("wsl", [D, 2336], F32, kind="ExternalInput").ap()
    prm = nc.dram_tensor("prm", [128, 64], F32, kind="ExternalInput").ap()
    cm = nc.dram_tensor("cm", [128, 768], F32, kind="ExternalInput").ap()
    cbd = nc.dram_tensor("cb", [128, CB_W], BF16, kind="ExternalInput").ap()
    cfd = nc.dram_tensor("cf", [128, CF_W], F32, kind="ExternalInput").ap()
    tab = nc.dram_tensor("tab", [128, 2, S], F32, kind="ExternalInput").ap()
    mix = nc.dram_tensor("mix", [512, S], F32, kind="ExternalOutput").ap()
    sk = "ExternalOutput" if dbg else "Internal"
    SD = {}
    for n in ["q", "k", "v", "za", "R", "A", "B", "K", "V2"]:
        SD[n] = nc.dram_tensor("S_" + n, [256, S], BF16, kind=sk).ap()
    for n in ["gate", "bonus"]:
        SD[n] = nc.dram_tensor("S_" + n, [256, S], F32, kind=sk).ap()
    SD["pc"] = nc.dram_tensor("S_pc", [256, NCH], F32, kind=sk).ap()

    with contextlib.ExitStack() as st:
        P = Prog(nc, st)
        op = P.op
        prm_t = Tl(P, "prm", [128, 64], F32)
        cb = Tl(P, "cb", [128, CB_W], BF16)
        cf = Tl(P, "cf", [128, CF_W], F32)
        cmf = Tl(P, "cmf", [128, 768], F32)
        cmb = Tl(P, "cmb", [128, 768], BF16)
        op("sync", lambda e: e.dma_start(out=prm_t[:], in_=prm), writes=[prm_t], dma=prm_t)
        op("sync", lambda e: e.dma_start(out=cb[:], in_=cbd), writes=[cb], dma=cb)
        op("sync", lambda e: e.dma_start(out=cf[:], in_=cfd), writes=[cf], dma=cf)
        op("sync", lambda e: e.dma_start(out=cmf[:], in_=cm), writes=[cmf], dma=cmf)
        op("vector", lambda e: e.tensor_copy(out=cmb[:], in_=cmf[:]), reads=[cmf], writes=[cmb])
        ident = cb[:, CB_ID:CB_ID + 128]
        perm = cb[:, CB_PERM:CB_PERM + 128]
        bdm = cb[:, CB_BD:CB_BD + 128]
        rmask = cf[:, 64:64 + TT]

        def pcol(i):
            return prm_t[:, i:i + 1]

        with contextlib.ExitStack() as sa:
            WN = 1312

            def mk(name, shape, dt, psum=False):
                tl = Tl.__new__(Tl)
                tl.t = (P.ps if psum else P.sb)(name, shape, dt, sa)
                tl.k = P.tok(name)
                return tl

            Wb = mk("Wb", [128, 16, WN], BF16)
            ws = [mk("ws", [128, WN], F32) for _ in range(2)]
            xs = [mk("xs", [128, 2, TT], F32) for _ in range(2)]
            sq = [mk("sq", [128, 2, TT], BF16) for _ in range(2)]
            xb = [mk("xb", [128, 16, TT], BF16) for _ in range(2)]
            pj = [mk("pj", [128, TT], F32, True) for _ in range(3)]
            ssp = mk("ssp", [128, TT], F32, True)
            aux = [mk("aux", [128, TT], F32, True) for _ in range(3)]
            auxi = [0]

            def nxaux():
                auxi[0] += 1
                return aux[auxi[0] % 3]

            rstd = mk("rstd", [128, TT], F32)
            tss = mk("tss", [128, TT], F32)
            U = {ct: mk("U%d" % ct, [128, TT + 1], F32) for ct in [8, 9, 10, 11, 12, 13, 16, 17, 18]}
            lastc = {ct: mk("lc%d" % ct, [128, 1], F32) for ct in U}
            for ct in U:
                op("gpsimd", lambda e, ct=ct: e.memset(lastc[ct][:], 0.0), writes=[lastc[ct]])
            dtl = mk("dtl", [128, TT], F32)
            LA = mk("LA", [128, TT], BF16)
            GL0 = mk("GL0", [128, TT], BF16)
            GL1 = mk("GL1", [128, TT], BF16)
            lw = [mk("lw", [128, TT], F32) for _ in range(2)]
            aa = [mk("aa", [128, TT], F32) for _ in range(2)]
            gg = [mk("gg", [128, TT], F32) for _ in range(2)]
            zr = mk("zr", [128, TT], F32)
            kkr = mk("kkr", [128, TT], F32)
            sqb = mk("sqb", [128, TT], BF16)
            sn = mk("sn", [128, TT], F32)
            apt = mk("apt", [128, TT], F32)
            bpt = mk("bpt", [128, TT], F32)
            kpt = mk("kpt", [128, TT], F32)
            rkr = mk("rkr", [128, TT], BF16)
            cum = mk("cum", [128, TT], F32)
            cumx = mk("cumx", [128, TT], F32)
            eP = mk("eP", [128, TT], F32)
            eN = mk("eN", [128, TT], F32)
            ePx = mk("ePx", [128, TT], F32)
            pcs = mk("pcs", [128, 2, NCH], F32)
            stg = {n: mk("st_" + n, [128, TT], BF16) for n in ["R", "A", "B", "K", "V2", "q", "k", "v", "za"]}
            stg["gate"] = mk("st_gate", [128, TT], F32)
            stg["bonus"] = mk("st_bonus", [128, TT], F32)
            cst = mk("cst", [128, 2, TT], F32)
            qf = mk("qf", [128, TT], F32)
            qb = mk("qb", [128, TT], BF16)
            t1 = mk("t1", [128, TT], F32)
            t2 = mk("t2", [128, TT], F32)

            def store(name, c2, tt):
                dst = SD[name][c2 * 128:(c2 + 1) * 128, tt * TT:(tt + 1) * TT]
                op("sync", lambda e: e.dma_start(out=dst, in_=stg[name][:]), reads=[stg[name]],
                   writes=[P.dtok((name, c2, tt))], dma=stg[name])

            xTv = xT.rearrange("(kc p) t -> p kc t", p=128)
            xcnt = [0]

            for pas in passes:
                if pas == 0:
                    c0, ncol = 1024, 1312
                    cts = [16, 17, 18, 8, 10, 12, 14, 9, 11, 13, 15]
                else:
                    c0, ncol = 0, 1024
                    cts = [0, 1, 2, 3, 4, 5, 6, 7]
                for kc in range(16):
                    w = ws[kc % 2]
                    op("sync", lambda e, w=w, kc=kc: e.dma_start(out=w[:, 0:ncol], in_=wsl[kc * 128:(kc + 1) * 128, c0:c0 + ncol]),
                       writes=[w], dma=w)
                    op("vector", lambda e, w=w, kc=kc: e.tensor_scalar(out=Wb[:, kc, 0:ncol], in0=w[:, 0:ncol], scalar1=pcol(kc), scalar2=None, op0=ALU.mult),
                       reads=[w, prm_t], writes=[Wb])

                for tt in range(ntt):
                    tsl = slice(tt * TT, (tt + 1) * TT)
                    xbb = xb[tt % 2]
                    for j in range(8):
                        xi = xcnt[0] % 2
                        xcnt[0] += 1
                        op("sync", lambda e, xi=xi, j=j: e.dma_start(out=xs[xi][:], in_=xTv[:, 2 * j:2 * j + 2, tsl]), writes=[xs[xi]], dma=xs[xi])
                        op("vector", lambda e, xi=xi, j=j: e.tensor_copy(out=xbb[:, 2 * j:2 * j + 2, :], in_=xs[xi][:]), reads=[xs[xi]], writes=[xbb])
                        op("scalar", lambda e, xi=xi: e.activation(out=sq[xi][:], in_=xs[xi][:], func=AF.Square), reads=[xs[xi]], writes=[sq[xi]])

                        def ssmm(e, xi=xi, j=j):
                            for q in range(2):
                                ins = e.matmul(ssp[:], lhsT=cb[:, CB_ONES:CB_ONES + 128], rhs=sq[xi][:, q, :], start=(j == 0 and q == 0), stop=(j == 7 and q == 1))
                            return ins
                        op("tensor", ssmm, reads=[sq[xi], cb], writes=[ssp])
                    op("vector", lambda e: e.tensor_scalar(out=tss[:], in0=ssp[:], scalar1=1.0 / D, scalar2=1e-5, op0=ALU.mult, op1=ALU.add), reads=[ssp], writes=[tss])
                    op("scalar", lambda e: e.activation(out=tss[:], in_=tss[:], func=AF.Sqrt), reads=[tss], writes=[tss])
                    op("vector", lambda e: e.reciprocal(out=rstd[:], in_=tss[:]), reads=[tss], writes=[rstd])
                    if pas == 1:
                        op("sync", lambda e: e.dma_start(out=cst[:], in_=tab[:, :, tsl]), writes=[cst], dma=cst)

                    for ci, ct in enumerate(cts):
                        pp = pj[ci % 3]
                        wc0 = ct * 128 - c0
                        wn = 32 if ct == 18 else 128

                        def proj(e, pp=pp, wc0=wc0, wn=wn):
                            for kc in range(16):
                                ins = e.matmul(pp[0:wn, :], lhsT=Wb[:, kc, wc0:wc0 + wn], rhs=xbb[:, kc, :], start=(kc == 0), stop=(kc == 15))
                            return ins
                        op("tensor", proj, reads=[Wb, xbb], writes=[pp])

                        if ct in U:
                            u = U[ct]
                            lc = lastc[ct]
                            mucol = {8: 16, 9: 17, 10: 18, 11: 19, 12: 20, 13: 21, 16: 22, 17: 23, 18: 24}[ct]
                            op("vector", lambda e, u=u, pp=pp, wn=wn: e.tensor_tensor(out=u[0:wn, 1:TT + 1], in0=pp[0:wn, :], in1=rstd[0:wn, :], op=ALU.mult), reads=[pp, rstd], writes=[u])
                            op("gpsimd", lambda e, u=u, lc=lc, wn=wn: e.tensor_copy(out=u[0:wn, 0:1], in_=lc[0:wn, :]), reads=[lc], writes=[u])
                            op("gpsimd", lambda e, u=u, lc=lc, wn=wn: e.tensor_copy(out=lc[0:wn, :], in_=u[0:wn, TT:TT + 1]), reads=[u], writes=[lc])
                            op("gpsimd", lambda e, u=u, wn=wn: e.tensor_tensor(out=dtl[0:wn, :], in0=u[0:wn, 0:TT], in1=u[0:wn, 1:TT + 1], op=ALU.subtract), reads=[u], writes=[dtl])
                            op("vector", lambda e, u=u, wn=wn, mucol=mucol: e.scalar_tensor_tensor(out=u[0:wn, 1:TT + 1], in0=dtl[0:wn, :], scalar=prm_t[0:wn, mucol:mucol + 1], in1=u[0:wn, 1:TT + 1], op0=ALU.mult, op1=ALU.add), reads=[dtl, u, prm_t], writes=[u])

                        if ct == 16:
                            u = U[16]
                            op("scalar", lambda e, u=u: e.activation(out=LA[0:64, :], in_=u[0:64, 1:TT + 1], func=AF.Tanh), reads=[u], writes=[LA])
                            op("scalar", lambda e, u=u: e.activation(out=LA[64:128, :], in_=u[64:128, 1:TT + 1], func=AF.Copy), reads=[u], writes=[LA])
                            for c2 in range(2):
                                a1 = nxaux()
                                op("tensor", lambda e, a1=a1, c2=c2: e.matmul(a1[:], lhsT=cmb[0:64, c2 * 128:(c2 + 1) * 128], rhs=LA[0:64, :], start=True, stop=True), reads=[cmb, LA], writes=[a1])
                                op("scalar", lambda e, a1=a1, c2=c2: e.activation(out=lw[c2][:], in_=a1[:], func=AF.Sigmoid, bias=pcol(25 + c2), scale=1.0), reads=[a1, prm_t], writes=[lw[c2]])
                                op("gpsimd", lambda e, c2=c2: e.tensor_scalar(out=lw[c2][:], in0=lw[c2][:], scalar1=-DEC, scalar2=None, op0=ALU.mult), reads=[lw[c2]], writes=[lw[c2]])
                                a2 = nxaux()
                                op("tensor", lambda e, a2=a2, c2=c2: e.matmul(a2[:], lhsT=cmb[64:128, c2 * 128:(c2 + 1) * 128], rhs=LA[64:128, :], start=True, stop=True), reads=[cmb, LA], writes=[a2])
                                op("scalar", lambda e, a2=a2, c2=c2: e.activation(out=aa[c2][:], in_=a2[:], func=AF.Sigmoid, bias=pcol(27 + c2), scale=1.0), reads=[a2, prm_t], writes=[aa[c2]])
                        if ct == 17:
                            op("scalar", lambda e: e.activation(out=GL0[:], in_=U[17][:, 1:TT + 1], func=AF.Sigmoid), reads=[U[17]], writes=[GL0])
                        if ct == 18:
                            op("scalar", lambda e: e.activation(out=GL1[0:32, :], in_=U[18][0:32, 1:TT + 1], func=AF.Sigmoid), reads=[U[18]], writes=[GL1])
                            for c2 in range(2):
                                a1 = nxaux()

                                def gmm(e, a1=a1, c2=c2):
                                    e.matmul(a1[:], lhsT=cmb[:, 256 + c2 * 128:256 + (c2 + 1) * 128], rhs=GL0[:], start=True, stop=False)
                                    return e.matmul(a1[:], lhsT=cmb[0:32, 512 + c2 * 128:512 + (c2 + 1) * 128], rhs=GL1[0:32, :], start=False, stop=True)
                                op("tensor", gmm, reads=[cmb, GL0, GL1], writes=[a1])
                                op("scalar", lambda e, a1=a1, c2=c2: e.activation(out=gg[c2][:], in_=a1[:], func=AF.Copy), reads=[a1], writes=[gg[c2]])
                        if ct in (14, 15):
                            c2 = ct - 14
                            op("vector", lambda e, pp=pp: e.tensor_tensor(out=zr[:], in0=pp[:], in1=rstd[:], op=ALU.mult), reads=[pp, rstd], writes=[zr])
                            op("scalar", lambda e: e.activation(out=zr[:], in_=zr[:], func=AF.Silu), reads=[zr], writes=[zr])
                            r_s = U[8 + c2]
                            k_s = U[10 + c2]
                            v_s = U[12 + c2]
                            RS = lambda t: t[:, 1:TT + 1]
                            op("gpsimd", lambda e, c2=c2: e.tensor_tensor(out=stg["gate"][:], in0=gg[c2][:], in1=zr[:], op=ALU.mult), reads=[gg[c2], zr], writes=[stg["gate"]])
                            store("gate", c2, tt)
                            op("vector", lambda e, c2=c2, k_s=k_s: e.tensor_scalar(out=kkr[:], in0=RS(k_s), scalar1=pcol(29 + c2), scalar2=None, op0=ALU.mult), reads=[k_s, prm_t], writes=[kkr])
                            op("scalar", lambda e: e.activation(out=sqb[:], in_=kkr[:], func=AF.Square), reads=[kkr], writes=[sqb])
                            a1 = nxaux()
                            op("tensor", lambda e, a1=a1: e.matmul(a1[:], lhsT=bdm, rhs=sqb[:], start=True, stop=True), reads=[cb, sqb], writes=[a1])
                            op("scalar", lambda e, a1=a1: e.activation(out=sn[:], in_=a1[:], func=AF.Sqrt), reads=[a1], writes=[sn])
                            op("vector", lambda e: e.tensor_scalar(out=sn[:], in0=sn[:], scalar1=1e-12, scalar2=None, op0=ALU.max), reads=[sn], writes=[sn])
                            op("vector", lambda e: e.reciprocal(out=sn[:], in_=sn[:]), reads=[sn], writes=[sn])
                            op("vector", lambda e: e.scalar_tensor_tensor(out=apt[:], in0=kkr[:], scalar=-1.0, in1=sn[:], op0=ALU.mult, op1=ALU.mult), reads=[kkr, sn], writes=[apt])
                            op("vector", lambda e, c2=c2: e.scalar_tensor_tensor(out=bpt[:], in0=apt[:], scalar=-1.0, in1=aa[c2][:], op0=ALU.mult, op1=ALU.mult), reads=[apt, aa[c2]], writes=[bpt])
                            op("vector", lambda e, c2=c2: e.tensor_scalar(out=kpt[:], in0=aa[c2][:], scalar1=-1.0, scalar2=pcol(31 + c2), op0=ALU.add, op1=ALU.mult), reads=[aa[c2], prm_t], writes=[kpt])
                            op("vector", lambda e, k_s=k_s: e.scalar_tensor_tensor(out=kpt[:], in0=kpt[:], scalar=1.0, in1=RS(k_s), op0=ALU.add, op1=ALU.mult), reads=[kpt, k_s], writes=[kpt])
                            op("vector", lambda e, c2=c2, r_s=r_s: e.scalar_tensor_tensor(out=rkr[:], in0=RS(r_s), scalar=pcol(33 + c2), in1=kpt[:], op0=ALU.mult, op1=ALU.mult), reads=[r_s, kpt, prm_t], writes=[rkr])
                            a2 = nxaux()
                            op("tensor", lambda e, a2=a2: e.matmul(a2[:], lhsT=bdm, rhs=rkr[:], start=True, stop=True), reads=[cb, rkr], writes=[a2])
                            op("vector", lambda e, a2=a2, v_s=v_s: e.tensor_tensor(out=stg["bonus"][:], in0=a2[:], in1=RS(v_s), op=ALU.mult), reads=[a2, v_s], writes=[stg["bonus"]])
                            store("bonus", c2, tt)
                            op("vector", lambda e, c2=c2: e.tensor_tensor_scan(out=cum[:], data0=rmask, data1=lw[c2][:], initial=0.0, op0=ALU.mult, op1=ALU.add), reads=[cf, lw[c2]], writes=[cum])
                            op("gpsimd", lambda e, c2=c2: e.tensor_tensor(out=cumx[:], in0=cum[:], in1=lw[c2][:], op=ALU.subtract), reads=[cum, lw[c2]], writes=[cumx])
                            op("scalar", lambda e: e.activation(out=eP[:], in_=cum[:], func=AF.Exp), reads=[cum], writes=[eP])
                            op("scalar", lambda e: e.activation(out=eN[:], in_=cum[:], func=AF.Exp, scale=-1.0), reads=[cum], writes=[eN])
                            op("scalar", lambda e: e.activation(out=ePx[:], in_=cumx[:], func=AF.Exp), reads=[cumx], writes=[ePx])
                            op("gpsimd", lambda e, c2=c2, tt=tt: e.tensor_copy(out=pcs[:, c2, tt * 4:(tt + 1) * 4], in_=eP[:, CH - 1:TT:CH]), reads=[eP], writes=[pcs])
                            op("vector", lambda e, r_s=r_s: e.tensor_tensor(out=stg["R"][:], in0=RS(r_s), in1=eP[:], op=ALU.mult), reads=[r_s, eP], writes=[stg["R"]])
                            store("R", c2, tt)
                            op("gpsimd", lambda e: e.tensor_tensor(out=stg["A"][:], in0=apt[:], in1=ePx[:], op=ALU.mult), reads=[apt, ePx], writes=[stg["A"]])
                            store("A", c2, tt)
                            op("vector", lambda e: e.tensor_tensor(out=stg["B"][:], in0=bpt[:], in1=eN[:], op=ALU.mult), reads=[bpt, eN], writes=[stg["B"]])
                            store("B", c2, tt)
                            op("gpsimd", lambda e: e.tensor_tensor(out=stg["K"][:], in0=kpt[:], in1=eN[:], op=ALU.mult), reads=[kpt, eN], writes=[stg["K"]])
                            store("K", c2, tt)
                            op("scalar", lambda e, v_s=v_s: e.activation(out=stg["V2"][:], in_=RS(v_s), func=AF.Copy), reads=[v_s], writes=[stg["V2"]])
                            store("V2", c2, tt)
                        if ct in (0, 1, 2, 3):
                            nm = "q" if ct < 2 else "k"
                            c2 = ct % 2
                            scl = 0.125 if ct < 2 else 1.0
                            op("vector", lambda e, pp=pp, scl=scl: e.scalar_tensor_tensor(out=qf[:], in0=pp[:], scalar=scl, in1=rstd[:], op0=ALU.mult, op1=ALU.mult), reads=[pp, rstd], writes=[qf])
                            op("scalar", lambda e: e.activation(out=qb[:], in_=qf[:], func=AF.Copy), reads=[qf], writes=[qb])
                            a1 = nxaux()
                            op("tensor", lambda e, a1=a1: e.matmul(a1[:], lhsT=perm, rhs=qb[:], start=True, stop=True), reads=[cb, qb], writes=[a1])
                            op("gpsimd", lambda e: e.tensor_tensor(out=t1[:], in0=qf[:], in1=cst[:, 0, :], op=ALU.mult), reads=[qf, cst], writes=[t1])
                            op("vector", lambda e, a1=a1: e.tensor_tensor(out=t2[:], in0=a1[:], in1=cst[:, 1, :], op=ALU.mult), reads=[a1, cst], writes=[t2])
                            op("gpsimd", lambda e, nm=nm: e.tensor_tensor(out=stg[nm][:], in0=t1[:], in1=t2[:], op=ALU.add), reads=[t1, t2], writes=[stg[nm]])
                            store(nm, c2, tt)
                        if ct in (4, 5):
                            op("vector", lambda e, pp=pp: e.tensor_tensor(out=stg["v"][:], in0=pp[:], in1=rstd[:], op=ALU.mult), reads=[pp, rstd], writes=[stg["v"]])
                            store("v", ct - 4, tt)
                        if ct in (6, 7):
                            op("vector", lambda e, pp=pp: e.tensor_tensor(out=qf[:], in0=pp[:], in1=rstd[:], op=ALU.mult), reads=[pp, rstd], writes=[qf])
                            op("scalar", lambda e: e.activation(out=stg["za"][:], in_=qf[:], func=AF.Silu), reads=[qf], writes=[stg["za"]])
                            store("za", ct - 6, tt)
                if pas == 0:
                    for c2 in range(2):
                        op("sync", lambda e, c2=c2: e.dma_start(out=SD["pc"][c2 * 128:(c2 + 1) * 128, :], in_=pcs[:, c2, :]), reads=[pcs], writes=[P.dtok(("pc", c2))], dma=pcs)
                        op("sync", None, reads=[P.dtok(("pc", c2))])
            P.barrier()
            P.flush()
        if "C" in phases:
            build_attention(P, nc, SD, mix, cb, cf)
        if "D" in phases:
            build_rwkv(P, nc, SD, mix, cb, cf, prm_t)
        op("sync", None, writes=P.toks)
        P.flush()
    return nc


def _mk(P, sa, name, shape, dt, psum=False):
    tl = Tl.__new__(Tl)
    tl.t = (P.ps if psum else P.sb)(name, shape, dt, sa)
    tl.k = P.tok(name)
    return tl


def build_attention(P, nc, SD, mix, cb, cf):
    op = P.op
    NB = 4
    with contextlib.ExitStack() as sa:
        mk = lambda n, sh, dt, ps=False: _mk(P, sa, n, sh, dt, ps)
        QKVZ = [[mk(n, [64, S], BF16) for n in ("Q", "K", "V", "ZA")] for _ in range(1)]
        accs = [mk("acc", [65, S], F32) for _ in range(2)]
        vtok = mk("vtok", [128, 192, 65], BF16)
        UB = [mk("UB", [128, 512], F32, True) for _ in range(NB)]
        TB = [mk("TB", [128, 1024], BF16, True) for _ in range(2)]
        pB = mk("pB", [128, TT], F32, True)
        PT = [mk("PT", [128, 256], BF16) for _ in range(NB)]
        rec = [mk("rec", [64, TT], F32) for _ in range(2)]
        ost = [mk("ost", [64, TT], F32) for _ in range(2)]
        op("gpsimd", lambda e: e.memset(vtok[:], 1.0), writes=[vtok])
        mbias = cb[:, CB_AM:CB_AM + 256]
        ident = cb[:, CB_ID:CB_ID + 128]
        id64 = cb[0:64, CB_ID:CB_ID + 64]
        onesrow = cf[64:65, 64 + TT:64 + TT + 64]

        def load(h):
            rows = slice(h * 64, (h + 1) * 64)
            c2 = h // 2
            deps = [P.dtok((n, c2, tt)) for n in ["q", "k", "v", "za"] for tt in range(NTT)]
            for t, n in zip(QKVZ[0], ["q", "k", "v", "za"]):
                op("sync", lambda e, t=t, n=n, rows=rows: e.dma_start(out=t[:], in_=SD[n][rows, :]), reads=deps, writes=[t], dma=t)

        u = 0
        for h in range(4):
            rows = slice(h * 64, (h + 1) * 64)
            Q, K, V, ZA = QKVZ[0]
            load(h)
            for acc in accs:
                op("gpsimd", lambda e, acc=acc: e.memset(acc[:], 0.0), writes=[acc])
            blocks = []
            for d in (1, 4, 16):
                nb = (S // d) // 128
                for r in range(d):
                    for j in range(nb):
                        nq = 256 if j < nb - 1 else 128
                        k0 = r + 128 * j * d
                        blocks.append((slice(k0, k0 + 127 * d + 1, d), slice(k0, k0 + (nq - 1) * d + 1, d), nq))
            for g8 in range(len(blocks) // 8):
                tb = TB[g8 % 2]

                def tr8(e, g8=g8, tb=tb, V=V):
                    for q in range(8):
                        ins = e.transpose(tb[:, q * 64:(q + 1) * 64], V[:, blocks[g8 * 8 + q][0]], id64)
                    return ins
                op("tensor", tr8, reads=[V, cb], writes=[tb])
                src = tb[:, 0:512].rearrange("p (b c) -> p b c", c=64)
                if g8 % 2 == 0:
                    op("vector", lambda e, g8=g8, src=src: e.tensor_copy(out=vtok[:, g8 * 8:(g8 + 1) * 8, 0:64], in_=src), reads=[tb], writes=[vtok])
                else:
                    op("scalar", lambda e, g8=g8, src=src: e.activation(out=vtok[:, g8 * 8:(g8 + 1) * 8, 0:64], in_=src, func=AF.Copy), reads=[tb], writes=[vtok])

            def s_part(bi):
                ksl, qsl, nq = blocks[bi]
                i = (u + bi) % NB
                ub = UB[i]

                def smm(e, ub=ub, ksl=ksl, qsl=qsl, nq=nq, K=K, Q=Q):
                    e.matmul(ub[:, 0:nq], lhsT=K[:, ksl], rhs=Q[:, qsl], start=True, stop=False)
                    return e.matmul(ub[:, 0:nq], lhsT=ident, rhs=mbias[:, 0:nq], start=False, stop=True)
                op("tensor", smm, reads=[K, Q, cb], writes=[ub])
                op("scalar", lambda e, ub=ub, i=i, nq=nq: e.activation(out=PT[i][:, 0:nq], in_=ub[:, 0:nq], func=AF.Exp), reads=[ub], writes=[PT[i]])

            def o_part(bi):
                ksl, qsl, nq = blocks[bi]
                i = (u + bi) % NB
                ub = UB[i]
                op("tensor", lambda e, ub=ub, i=i, nq=nq, bi=bi: e.matmul(ub[0:65, 256:256 + nq], lhsT=vtok[:, bi, :], rhs=PT[i][:, 0:nq], start=True, stop=True), reads=[vtok, PT[i]], writes=[ub])
                acc = accs[bi % 2]
                op("vector", lambda e, ub=ub, nq=nq, qsl=qsl, acc=acc: e.tensor_tensor(out=acc[:, qsl], in0=ub[0:65, 256:256 + nq], in1=acc[:, qsl], op=ALU.add), reads=[ub, acc], writes=[acc])

            SK = 2
            nbk = len(blocks)
            for bi in range(nbk + SK):
                if bi < nbk:
                    s_part(bi)
                if bi >= SK:
                    o_part(bi - SK)
            u += nbk
            for tt in range(NTT):
                tsl = slice(tt * TT, (tt + 1) * TT)
                rc, os_ = rec[tt % 2], ost[tt % 2]
                def denmm(e, tsl=tsl):
                    e.matmul(pB[0:64, :], lhsT=onesrow, rhs=accs[0][64:65, tsl], start=True, stop=False)
                    return e.matmul(pB[0:64, :], lhsT=onesrow, rhs=accs[1][64:65, tsl], start=False, stop=True)
                op("tensor", denmm, reads=[cf, accs[0], accs[1]], writes=[pB])
                op("vector", lambda e, rc=rc: e.reciprocal(out=rc[:], in_=pB[0:64, :]), reads=[pB], writes=[rc])
                op("gpsimd", lambda e, os_=os_, tsl=tsl: e.tensor_tensor(out=os_[:], in0=accs[0][0:64, tsl], in1=accs[1][0:64, tsl], op=ALU.add), reads=[accs[0], accs[1]], writes=[os_])
                op("vector", lambda e, rc=rc, os_=os_: e.tensor_tensor(out=rc[:], in0=os_[:], in1=rc[:], op=ALU.mult), reads=[os_, rc], writes=[rc])
                op("gpsimd", lambda e, rc=rc, os_=os_, tsl=tsl, ZA=ZA: e.tensor_tensor(out=os_[:], in0=rc[:], in1=ZA[:, tsl], op=ALU.mult), reads=[rc, ZA], writes=[os_])
                op("sync", lambda e, os_=os_, tsl=tsl, rows=rows: e.dma_start(out=mix[rows, tsl], in_=os_[:]), reads=[os_], writes=[P.dtok(("mixa", h, tt))], dma=os_)
        P.barrier()
        P.flush()


DBG_D = {"nseg": 4, "stage": 9, "heads": 2}


def build_rwkv(P, nc, SD, mix, cb, cf, prm_t):
    op = P.op
    stage = DBG_D["stage"]
    SEG = 2048
    NSEG = S // SEG
    CPS = SEG // CH
    with contextlib.ExitStack() as sa:
        mk = lambda n, sh, dt, ps=False: _mk(P, sa, n, sh, dt, ps)
        ident = cb[:, CB_ID:CB_ID + 128]
        id64 = cb[0:64, CB_ID:CB_ID + 64]
        mask4 = cb[:, CB_M4:CB_M4 + 512]
        maskts = cb[:, CB_MTS:CB_MTS + 128]
        ones64 = cf[0:64, 0:64]
        HD = []
        for hh in range(2):
            d = {}
            for n in ["R", "A", "B", "K", "V2"]:
                d[n] = mk("r" + n, [64, SEG], BF16)
            for n in ["gate", "bonus", "y"]:
                d[n] = mk("r" + n, [64, SEG], F32)
            d["pc"] = mk("rpc", [64, NCH], F32)
            d["H"] = mk("H", [64, 64], F32)
            d["Hb"] = mk("Hb", [64, 64], BF16)
            d["Ht"] = mk("Ht", [64, 64], F32)
            b0 = P.ps("b0", [128, 512], F32, sa)
            b1 = P.ps("b1", [128, 512], F32, sa)
            b2 = P.ps("b2", [128, 512], F32, sa)
            if hh == 0:
                b3full = P.ps("b3", [128, 1024], BF16, sa)
            b3 = b3full[:, hh * 512:hh * 512 + 256]
            d["b0"], d["b1"], d["b2"], d["b3"] = b0, b1, b2, b3
            bank_of = {"g4p": "B0", "np": "B1", "ia": "B1", "ib": "B1", "ic": "B1", "zp": "B2", "up": "B2", "yp": "B2", "hp": "B2", "trp": "B3"}
            btok = {}
            for n, bk in bank_of.items():
                if DBG_D.get("banktok", 1):
                    if bk == "B3":
                        if hh == 0:
                            b3tok = P.tok("B3")
                        d["k_" + n] = b3tok
                        continue
                    if bk not in btok:
                        btok[bk] = P.tok(bk)
                    d["k_" + n] = btok[bk]
                else:
                    d["k_" + n] = P.tok(n)
            d["G4"] = mk("G4", [128, 512], BF16)
            d["N1"] = mk("N1", [128, 128], BF16)
            d["tok3"] = mk("tok3", [128, 192], BF16)
            d["Sm"] = [mk("Sm", [128, 128], BF16) for _ in range(2)]
            d["Nn"] = [mk("Nn", [128, 128], BF16) for _ in range(2)]
            d["Mn"] = [mk("Mn", [128, 128], BF16) for _ in range(2)]
            d["Zb"] = mk("Zb", [128, 64], BF16)
            d["Ub"] = mk("Ub", [128, 64], BF16)
            d["ysq"] = mk("ysq", [64, TT], F32)
            d["mt"] = mk("mt", [64, TT], F32)
            d["m2"] = mk("m2", [64, TT], F32)
            d["var"] = mk("var", [64, TT], F32)
            d["yc"] = mk("yc", [64, TT], F32)
            d["ost"] = mk("rost", [64, TT], F32)
            HD.append(d)

        def unit(d, h, seg, c):
            cs = slice(c * CH, (c + 1) * CH)
            gc = seg * CPS + c
            RT, AT, BT, KT, VT = d["R"], d["A"], d["B"], d["K"], d["V2"]
            b0, b1, b2, b3 = d["b0"], d["b1"], d["b2"], d["b3"]
            G4, N1, tok3 = d["G4"], d["N1"], d["tok3"]

            def g4(e):
                for bi, (l, r) in enumerate([(BT, AT), (KT, AT), (BT, RT), (KT, RT)]):
                    ins = e.matmul(b0[:, bi * 128:(bi + 1) * 128], lhsT=l[:, cs], rhs=r[:, cs], start=True, stop=True)
                return ins
            op("tensor", g4, reads=[RT, AT, BT, KT], writes=[d["k_g4p"]])
            op("vector", lambda e: e.tensor_tensor(out=G4[:], in0=b0[:], in1=mask4, op=ALU.mult), reads=[d["k_g4p"], cb], writes=[G4])
            if DBG_D.get("sub", 9) < 2:
                return
            op("tensor", lambda e: e.matmul(b1[:, 0:128], lhsT=AT[:, cs], rhs=BT[:, cs], start=True, stop=True), reads=[AT, BT], writes=[d["k_np"]])
            op("vector", lambda e: e.tensor_tensor(out=N1[:], in0=b1[:, 0:128], in1=maskts, op=ALU.mult), reads=[d["k_np"], cb], writes=[N1])

            if DBG_D.get("sub", 9) < 3:
                return

            def tr(e):
                e.transpose(b3[:, 0:64], VT[:, cs], id64)
                e.transpose(b3[:, 64:128], BT[:, cs], id64)
                return e.transpose(b3[:, 128:192], KT[:, cs], id64)
            op("tensor", tr, reads=[VT, BT, KT, cb], writes=[d["k_trp"]])
            op("scalar", lambda e: e.activation(out=tok3[:], in_=b3[:, 0:192], func=AF.Copy), reads=[d["k_trp"]], writes=[tok3])
            yield
            if stage < 2:
                return
            Sm = d["Sm"][0]
            op("gpsimd", lambda e: e.tensor_tensor(out=Sm[:], in0=G4[:, 0:128], in1=ident, op=ALU.add), reads=[G4, cb], writes=[Sm])
            Np, Mp = N1, None
            si = 0
            for lvl in range(DBG_D.get("nlvl", 6)):
                Nn = d["Nn"][lvl % 2]
                Mn = d["Mn"][lvl % 2]
                mp_ap = (lambda: G4[:, 0:128]) if Mp is None else (lambda Mp=Mp: Mp[:])
                mp_tok = G4 if Mp is None else Mp
                op("tensor", lambda e, mp_ap=mp_ap, Np=Np: e.matmul(b1[:, 128:256], lhsT=mp_ap(), rhs=Np[:], start=True, stop=True), reads=[mp_tok, Np], writes=[d["k_ia"]])
                op("scalar", lambda e, Nn=Nn: e.activation(out=Nn[:], in_=b1[:, 128:256], func=AF.Copy), reads=[d["k_ia"]], writes=[Nn])
                if lvl < 5:
                    op("tensor", lambda e, mp_ap=mp_ap, Np=Np: e.matmul(b1[:, 256:384], lhsT=Np[:], rhs=mp_ap(), start=True, stop=True), reads=[mp_tok, Np], writes=[d["k_ib"]])
                    op("vector", lambda e, Mn=Mn: e.tensor_copy(out=Mn[:], in_=b1[:, 256:384]), reads=[d["k_ib"]], writes=[Mn])
                yield
                So = d["Sm"][si]
                Sn = d["Sm"][1 - si]
                op("tensor", lambda e, Nn=Nn, So=So: e.matmul(b1[:, 384:512], lhsT=Nn[:], rhs=So[:], start=True, stop=True), reads=[Nn, So], writes=[d["k_ic"]])
                op("vector", lambda e, So=So, Sn=Sn: e.tensor_tensor(out=Sn[:], in0=b1[:, 384:512], in1=So[:], op=ALU.add), reads=[d["k_ic"], So], writes=[Sn])
                si = 1 - si
                Np, Mp = Nn, Mn
                yield
            if stage < 3:
                return
            Sf = d["Sm"][si]
            Hb, H, Ht = d["Hb"], d["H"], d["Ht"]
            Zb, Ub = d["Zb"], d["Ub"]

            def zmm(e):
                e.matmul(b2[:, 0:64], lhsT=AT[:, cs], rhs=Hb[:], start=True, stop=False)
                return e.matmul(b2[:, 0:64], lhsT=G4[:, 128:256], rhs=tok3[:, 0:64], start=False, stop=True)
            op("tensor", zmm, reads=[AT, Hb, G4, tok3], writes=[d["k_zp"]])
            op("scalar", lambda e: e.activation(out=Zb[:], in_=b2[:, 0:64], func=AF.Copy), reads=[d["k_zp"]], writes=[Zb])
            yield
            if DBG_D.get("cst", 9) < 2:
                return
            op("tensor", lambda e: e.matmul(b2[:, 64:128], lhsT=Sf[:], rhs=Zb[:], start=True, stop=True), reads=[Sf, Zb], writes=[d["k_up"]])
            op("vector", lambda e: e.tensor_copy(out=Ub[:], in_=b2[:, 64:128]), reads=[d["k_up"]], writes=[Ub])
            yield
            if DBG_D.get("cst", 9) < 3:
                return

            def ymm(e):
                e.matmul(b2[0:64, 128:256], lhsT=Hb[:], rhs=RT[:, cs], start=True, stop=False)
                e.matmul(b2[0:64, 128:256], lhsT=Ub[:], rhs=G4[:, 256:384], start=False, stop=False)
                return e.matmul(b2[0:64, 128:256], lhsT=tok3[:, 0:64], rhs=G4[:, 384:512], start=False, stop=True)
            op("tensor", ymm, reads=[Hb, RT, Ub, G4, tok3], writes=[d["k_yp"]])
            op("scalar", lambda e: e.activation(out=d["y"][:, cs], in_=b2[0:64, 128:256], func=AF.Copy), reads=[d["k_yp"]], writes=[d["y"]])

            if DBG_D.get("cst", 9) < 4:
                return

            def hmm(e):
                e.matmul(b2[0:64, 256:320], lhsT=tok3[:, 64:128], rhs=Ub[:], start=True, stop=False)
                return e.matmul(b2[0:64, 256:320], lhsT=tok3[:, 128:192], rhs=tok3[:, 0:64], start=False, stop=True)
            op("tensor", hmm, reads=[tok3, Ub], writes=[d["k_hp"]])
            op("vector", lambda e: e.tensor_tensor(out=Ht[:], in0=b2[0:64, 256:320], in1=H[:], op=ALU.add), reads=[d["k_hp"], H], writes=[Ht])
            if DBG_D.get("cst", 9) < 5:
                return
            op("vector", lambda e: e.tensor_scalar(out=H[:], in0=Ht[:], scalar1=d["pc"][:, gc:gc + 1], scalar2=None, op0=ALU.mult), reads=[Ht, d["pc"]], writes=[H])
            if DBG_D.get("cst", 9) < 6:
                return
            if DBG_D.get("hbeng", "vector") == "scalar":
                op("scalar", lambda e: e.activation(out=Hb[:], in_=H[:], func=AF.Copy), reads=[H], writes=[Hb])
            else:
                op(DBG_D.get("hbeng", "vector"), lambda e: e.tensor_copy(out=Hb[:], in_=H[:]), reads=[H], writes=[Hb])
            yield

        def post(d, h, seg):
            for q in range(SEG // TT):
                ts = slice(q * TT, (q + 1) * TT)
                gts = slice(seg * SEG + q * TT, seg * SEG + (q + 1) * TT)
                y = d["y"]
                b2 = d["b2"]
                op("scalar", lambda e, ts=ts: e.activation(out=d["ysq"][:], in_=y[:, ts], func=AF.Square), reads=[y], writes=[d["ysq"]])
                op("tensor", lambda e, ts=ts: e.matmul(b2[0:64, :], lhsT=ones64, rhs=y[:, ts], start=True, stop=True), reads=[cf, y], writes=[d["k_zp"], d["k_up"], d["k_yp"], d["k_hp"]])
                op("scalar", lambda e: e.activation(out=d["mt"][:], in_=b2[0:64, :], func=AF.Copy), reads=[d["k_zp"]], writes=[d["mt"]])
                op("tensor", lambda e: e.matmul(d["b0"][0:64, :], lhsT=ones64, rhs=d["ysq"][:], start=True, stop=True), reads=[cf, d["ysq"]], writes=[d["k_g4p"]])
                op("gpsimd", lambda e: e.tensor_tensor(out=d["m2"][:], in0=d["mt"][:], in1=d["mt"][:], op=ALU.mult), reads=[d["mt"]], writes=[d["m2"]])
                op("vector", lambda e: e.tensor_tensor(out=d["var"][:], in0=d["b0"][0:64, :], in1=d["m2"][:], op=ALU.subtract), reads=[d["k_g4p"], d["m2"]], writes=[d["var"]])
                op("vector", lambda e: e.tensor_scalar(out=d["var"][:], in0=d["var"][:], scalar1=64e-5, scalar2=None, op0=ALU.add), reads=[d["var"]], writes=[d["var"]])
                op("scalar", lambda e: e.activation(out=d["var"][:], in_=d["var"][:], func=AF.Sqrt), reads=[d["var"]], writes=[d["var"]])
                op("vector", lambda e: e.reciprocal(out=d["var"][:], in_=d["var"][:]), reads=[d["var"]], writes=[d["var"]])
                op("gpsimd", lambda e, ts=ts: e.tensor_tensor(out=d["yc"][:], in0=y[:, ts], in1=d["mt"][:], op=ALU.subtract), reads=[y, d["mt"]], writes=[d["yc"]])
                op("vector", lambda e: e.tensor_tensor(out=d["yc"][:], in0=d["yc"][:], in1=d["var"][:], op=ALU.mult), reads=[d["yc"], d["var"]], writes=[d["yc"]])
                op("vector", lambda e: e.tensor_scalar(out=d["yc"][:], in0=d["yc"][:], scalar1=prm_t[0:64, 40 + h:41 + h], scalar2=prm_t[0:64, 44 + h:45 + h], op0=ALU.mult, op1=ALU.add), reads=[d["yc"], prm_t], writes=[d["yc"]])
                op("gpsimd", lambda e, ts=ts: e.tensor_tensor(out=d["yc"][:], in0=d["yc"][:], in1=d["bonus"][:, ts], op=ALU.add), reads=[d["yc"], d["bonus"]], writes=[d["yc"]])
                op("vector", lambda e, ts=ts: e.tensor_tensor(out=d["ost"][:], in0=d["yc"][:], in1=d["gate"][:, ts], op=ALU.mult), reads=[d["yc"], d["gate"]], writes=[d["ost"]])
                op("sync", lambda e, gts=gts: e.dma_start(out=mix[256 + h * 64:256 + (h + 1) * 64, gts], in_=d["ost"][:]), reads=[d["ost"]], writes=[P.dtok(("mixr", h, seg, q))], dma=d["ost"])

        for hp in range(DBG_D["heads"]):
            heads = [2 * hp, 2 * hp + 1]
            for hh, h in enumerate(heads):
                d = HD[hh]
                op("gpsimd", lambda e, d=d: e.memset(d["H"][:], 0.0), writes=[d["H"]])
                op("gpsimd", lambda e, d=d: e.memset(d["Hb"][:], 0.0), writes=[d["Hb"]])
                c2 = h // 2
                op("sync", lambda e, d=d, h=h: e.dma_start(out=d["pc"][:], in_=SD["pc"][h * 64:(h + 1) * 64, :]), reads=[P.dtok(("pc", c2))], writes=[d["pc"]], dma=d["pc"])
            for seg in range(DBG_D["nseg"]):
                for hh, h in enumerate(heads):
                    d = HD[hh]
                    c2 = h // 2
                    for n in ["R", "A", "B", "K", "V2", "gate", "bonus"]:
                        deps = [P.dtok((n, c2, tt)) for tt in range(seg * (SEG // TT), (seg + 1) * (SEG // TT))]
                        op("sync", lambda e, d=d, n=n, h=h, seg=seg: e.dma_start(out=d[n][:], in_=SD[n][h * 64:(h + 1) * 64, seg * SEG:(seg + 1) * SEG]), reads=deps, writes=[d[n]], dma=d[n])
                for c in range(CPS if stage >= 1 else 0):
                    gens = [unit(HD[hh], h, seg, c) for hh, h in enumerate(heads)]
                    alive = True
                    while alive:
                        alive = False
                        for g in gens:
                            try:
                                next(g)
                                alive = True
                            except StopIteration:
                                pass
                if stage >= 4:
                    for hh, h in enumerate(heads):
                        post(HD[hh], h, seg)
        P.barrier()
        P.flush()


def build_l2():
    nc = bass.Bass("TRN2", target_bir_lowering=False)
    NT = 2048
    mixT = nc.dram_tensor("mixT", [D, NT], F32, kind="ExternalInput").ap()
    wout = nc.dram_tensor("wout", [D, D], F32, kind="ExternalInput").ap()
    xin = nc.dram_tensor("xin", [NT, D], F32, kind="ExternalInput").ap()
    gfin = nc.dram_tensor("gfin", [128, D], F32, kind="ExternalInput").ap()
    out = nc.dram_tensor("out", [NT, D], F32, kind="ExternalOutput").ap()
    with contextlib.ExitStack() as st:
        P = Prog(nc, st)
        op = P.op
        mk = lambda n, sh, dt, ps=False: _mk(P, st, n, sh, dt, ps)
        Wo = mk("Wo", [128, 16, D], BF16)
        wst = [mk("wst", [128, D], F32) for _ in range(2)]
        gf = mk("gf", [128, D], F32)
        op("sync", lambda e: e.dma_start(out=gf[:], in_=gfin), writes=[gf], dma=gf)
        for kc in range(16):
            w = wst[kc % 2]
            op("sync", lambda e, w=w, kc=kc: e.dma_start(out=w[:], in_=wout[kc * 128:(kc + 1) * 128, :]), writes=[w], dma=w)
            op("vector" if kc % 2 == 0 else "gpsimd", lambda e, w=w, kc=kc: e.tensor_copy(out=Wo[:, kc, :], in_=w[:]), reads=[w], writes=[Wo])
        mf = [mk("mf", [128, 16, 128], F32) for _ in range(2)]
        mb = [mk("mb", [128, 16, 128], BF16) for _ in range(2)]
        xt = [mk("xt", [128, D], F32) for _ in range(2)]
        ys = [mk("ys", [128, D], F32) for _ in range(2)]
        junk = mk("junk", [128, D], F32)
        ss = mk("ss", [128, 1], F32)
        pp = [mk("pp", [128, 512], F32, True) for _ in range(4)]
        mv = mixT.rearrange("(kc p) t -> p kc t", p=128)
        for t in range(NT // 128):
            i = t % 2
            tsl = slice(t * 128, (t + 1) * 128)
            op("sync", lambda e, i=i, tsl=tsl: e.dma_start(out=mf[i][:], in_=mv[:, :, tsl]), writes=[mf[i]], dma=mf[i])
            op("sync", lambda e, i=i, tsl=tsl: e.dma_start(out=xt[i][:], in_=xin[tsl, :]), writes=[xt[i]], dma=xt[i])
            op("gpsimd", lambda e, i=i: e.tensor_copy(out=mb[i][:], in_=mf[i][:]), reads=[mf[i]], writes=[mb[i]])
            for cg in range(4):
                def mm(e, i=i, cg=cg):
                    for kc in range(16):
                        ins = e.matmul(pp[cg][:], lhsT=mb[i][:, kc, :], rhs=Wo[:, kc, cg * 512:(cg + 1) * 512], start=(kc == 0), stop=(kc == 15))
                    return ins
                op("tensor", mm, reads=[mb[i], Wo], writes=[pp[cg]])
                op("vector", lambda e, i=i, cg=cg: e.tensor_tensor(out=ys[i][:, cg * 512:(cg + 1) * 512], in0=pp[cg][:], in1=xt[i][:, cg * 512:(cg + 1) * 512], op=ALU.add), reads=[pp[cg], xt[i]], writes=[ys[i]])
            op("scalar", lambda e, i=i: e.activation(out=junk[:], in_=ys[i][:], func=AF.Square, accum_out=ss[:]), reads=[ys[i]], writes=[junk, ss])
            op("vector", lambda e: e.tensor_scalar(out=ss[:], in0=ss[:], scalar1=1.0 / D, scalar2=1e-5, op0=ALU.mult, op1=ALU.add), reads=[ss], writes=[ss])
            op("scalar", lambda e: e.activation(out=ss[:], in_=ss[:], func=AF.Sqrt), reads=[ss], writes=[ss])
            op("vector", lambda e: e.reciprocal(out=ss[:], in_=ss[:]), reads=[ss], writes=[ss])
            op("vector", lambda e, i=i: e.scalar_tensor_tensor(out=ys[i][:], in0=ys[i][:], scalar=ss[:, 0:1], in1=gf[:], op0=ALU.mult, op1=ALU.mult), reads=[ys[i], ss, gf], writes=[ys[i]])
            op("sync", lambda e, i=i, tsl=tsl: e.dma_start(out=out[tsl, :], in_=ys[i][:]), reads=[ys[i]], writes=[P.dtok(("out", t))], dma=ys[i])
        op("sync", None, writes=P.toks)
        P.flush()
    return nc


def _group_cols(g):
    hs = np.arange(g * 256, (g + 1) * 256)
    base = 4 * A
    cols = [hs, A + hs, 2 * A + hs, 3 * A + hs,
            base + hs, base + R + hs, base + 2 * R + hs, base + SHIFT + hs,
            base + 3 * R + np.arange(128), base + 3 * R + 128 + np.arange(160)]
    return np.concatenate(cols)


def _prep_l1(inp, b, g, consts):
    cb, cf, tab = consts
    f = lambda n: np.asarray(inp[n], dtype=np.float32)
    hs = slice(g * 256, (g + 1) * 256)
    xT = np.ascontiguousarray(f("x")[b].T)
    wsl = np.ascontiguousarray(f("w_in")[0][:, _group_cols(g)])
    prm = np.zeros((128, 64), np.float32)
    prm[:, 0:16] = f("norm_g")[0].reshape(16, 128).T
    mu = f("shift_mu")[0]
    for i, off in enumerate([0, R, 2 * R]):
        prm[:, 16 + 2 * i:18 + 2 * i] = mu[off + g * 256: off + (g + 1) * 256].reshape(2, 128).T
    prm[:, 22] = mu[3 * R:3 * R + 128]
    prm[:, 23] = mu[3 * R + 128:3 * R + 256]
    prm[0:32, 24] = mu[3 * R + 256:3 * R + 288]
    for i, n in enumerate(["w0", "a0", "k_k", "k_a"]):
        prm[:, 25 + 2 * i:27 + 2 * i] = f(n)[0][hs].reshape(2, 128).T
    prm[:, 33:35] = f("r_k")[0].reshape(-1)[hs].reshape(2, 128).T
    lg = f("lnx_g")[0][hs].reshape(4, 64)
    lb = f("lnx_b")[0][hs].reshape(4, 64)
    prm[0:64, 40:44] = lg.T
    prm[0:64, 44:48] = lb.T
    cm = np.zeros((128, 768), np.float32)
    cm[0:64, 0:256] = f("w2")[0][:, hs]
    cm[64:128, 0:256] = f("a2")[0][:, hs]
    cm[:, 256:512] = f("g2")[0][0:128, hs]
    cm[0:32, 512:768] = f("g2")[0][128:160, hs]
    return {"xT": xT, "wsl": wsl, "prm": prm, "cm": cm, "cb": cb, "cf": cf, "tab": tab}


_CACHE = {}


def kernel(**inputs):
    consts = _consts()
    if "l1" not in _CACHE:
        _CACHE["l1"] = build_l1()
        _CACHE["l2"] = build_l2()
    in1 = [_prep_l1(inputs, c // 4, c % 4, consts) for c in range(8)]
    r1 = run_bass_kernel_spmd(_CACHE["l1"], in1, core_ids=list(range(8))).results
    x = np.asarray(inputs["x"], dtype=np.float32)
    wout = np.asarray(inputs["w_out"], dtype=np.float32)[0]
    gfin = np.ascontiguousarray(np.broadcast_to(np.asarray(inputs["final_g"], dtype=np.float32)[None, :], (128, D)))
    in2 = []
    for c in range(8):
        b, qd = c // 4, c % 4
        ts = slice(qd * 2048, (qd + 1) * 2048)
        mt = np.empty((D, 2048), np.float32)
        for g in range(4):
            m = np.asarray(r1[b * 4 + g]["mix"])
            mt[g * 256:(g + 1) * 256] = m[0:256, ts]
            mt[A + g * 256:A + (g + 1) * 256] = m[256:512, ts]
        in2.append({"mixT": mt, "wout": wout, "xin": np.ascontiguousarray(x[b, ts]), "gfin": gfin})
    r2 = run_bass_kernel_spmd(_CACHE["l2"], in2, core_ids=list(range(8))).results
    out = np.empty((2, S, D), np.float32)
    for c in range(8):
        b, qd = c // 4, c % 4
        out[b, qd * 2048:(qd + 1) * 2048] = np.asarray(r2[c]["out"])
    return out
```

```python
import contextlib
import types
import numpy as np
import ml_dtypes
import concourse.bass as bass
import concourse.mybir as mybir
from concourse.bass_utils import run_bass_kernel_spmd

F32 = mybir.dt.float32
BF16 = mybir.dt.bfloat16
AF = mybir.ActivationFunctionType
ALU = mybir.AluOpType
NPBF = ml_dtypes.bfloat16

ENGS = ["tensor", "vector", "scalar", "gpsimd", "sync"]

D = 2048
S = 8192
A = 1024
R = 1024
SHIFT = 3 * R + 64 + 64 + 160
TT = 512
NTT = S // TT
CH = 128
NCH = S // CH
DEC = 0.6065306597126334


def _snap(fn):
    if fn is None or fn.__closure__ is None:
        return fn
    cells = []
    for c in fn.__closure__:
        try:
            cells.append(types.CellType(c.cell_contents))
        except ValueError:
            cells.append(c)
    g = types.FunctionType(fn.__code__, fn.__globals__, fn.__name__, fn.__defaults__, tuple(cells))
    g.__kwdefaults__ = fn.__kwdefaults__
    return g


class Tok:
    __slots__ = ("name", "writes", "reads", "sem", "cnt")

    def __init__(self, name=""):
        self.name = name
        self.writes = []
        self.reads = []
        self.sem = None
        self.cnt = 0


class Tl:
    def __init__(self, P, name, shape, dt, psum=False):
        self.t = (P.ps if psum else P.sb)(name, shape, dt)
        self.k = P.tok(name)

    def __getitem__(self, i):
        return self.t[i]


def _k(t):
    return t.k if hasattr(t, "k") else t


class Prog:
    def __init__(self, nc, stack):
        self.nc = nc
        self.stack = stack
        self.ops = {e: [] for e in ENGS}
        self.ecount = {e: 0 for e in ENGS}
        self.waited = {e: {} for e in ENGS}
        self.esem = {e: stack.enter_context(nc.semaphore("s_" + e)) for e in ENGS}
        self.semobj = {("E", e): self.esem[e] for e in ENGS}
        self.nsem = len(ENGS)
        self.toks = []
        self.dtoks = {}
        self.nm = 0

    def sb(self, name, shape, dt, stack=None):
        self.nm += 1
        return (stack or self.stack).enter_context(self.nc.sbuf_tensor("%s_%d" % (name, self.nm), list(shape), dt))

    def ps(self, name, shape, dt=F32, stack=None):
        self.nm += 1
        return (stack or self.stack).enter_context(self.nc.psum_tensor("%s_%d" % (name, self.nm), list(shape), dt))

    def tok(self, name=""):
        t = Tok(name)
        self.toks.append(t)
        return t

    def dtok(self, key):
        if key not in self.dtoks:
            self.dtoks[key] = self.tok(str(key))
        return self.dtoks[key]

    def dsem(self, tok):
        if tok.sem is None:
            tok.sem = self.stack.enter_context(self.nc.semaphore("d%d" % self.nsem))
            self.nsem += 1
            self.semobj[("D", id(tok))] = tok.sem
        return tok.sem

    def op(self, eng, fn, reads=(), writes=(), dma=None):
        fn = _snap(fn)
        reads = [_k(t) for t in reads]
        writes = [_k(t) for t in writes]
        need = []
        for t in reads:
            need += t.writes
        for t in writes:
            need += t.writes
            need += t.reads
        waits = {}
        for (key, val, src) in need:
            if self.waited[eng].get(key, 0) >= val:
                continue
            if waits.get(key, 0) < val:
                waits[key] = val
        for k, v in waits.items():
            self.waited[eng][k] = v
        ev = None
        if fn is not None:
            if dma is None:
                self.ecount[eng] += 1
                ev = (("E", eng), self.ecount[eng], eng)
            else:
                dma = _k(dma)
                self.dsem(dma)
                dma.cnt += 16
                ev = (("D", id(dma)), dma.cnt, None)
            for t in reads:
                t.reads.append(ev)
            for t in writes:
                t.writes = [ev]
                t.reads = []
        self.ops[eng].append((list(waits.items()), fn, ev))

    def barrier(self):
        for e in ENGS:
            self.op(e, None, writes=self.toks)
        for t in self.toks:
            t.writes = []
            t.reads = []

    def emit(self, eng_name, e):
        for waits, fn, ev in self.ops[eng_name]:
            for key, val in waits:
                e.wait_ge(self.semobj[key], val)
            if fn is None:
                continue
            ins = fn(e)
            key, val, src = ev
            ins.then_inc(self.semobj[key], 16 if key[0] == "D" else 1)
        self.ops[eng_name] = []

    def flush(self):
        with self.nc.Block() as block:
            @block.tensor
            def _(e):
                self.emit("tensor", e)

            @block.vector
            def _(e):
                self.emit("vector", e)

            @block.scalar
            def _(e):
                self.emit("scalar", e)

            @block.gpsimd
            def _(e):
                self.emit("gpsimd", e)

            @block.sync
            def _(e):
                self.emit("sync", e)


def _consts():
    ident = np.eye(128, dtype=np.float32)
    perm = np.zeros((128, 128), np.float32)
    for m in range(128):
        p = m + 32 if (m % 64) < 32 else m - 32
        perm[p, m] = 1.0
    bd = np.zeros((128, 128), np.float32)
    bd[:64, :64] = 1.0
    bd[64:, 64:] = 1.0
    ki = np.arange(128)[:, None]
    qi = np.arange(128)[None, :]
    amask = (np.concatenate([(ki <= qi), (qi <= ki)], axis=1).astype(np.float32) - 1.0) * 30000.0
    strict = (ki < qi).astype(np.float32)
    incl = (ki <= qi).astype(np.float32)
    mask4 = np.concatenate([strict, strict, incl, incl], axis=1)
    maskts = (qi < ki).astype(np.float32)
    cb = np.concatenate([ident, perm, bd, amask, mask4, maskts, np.ones((128, 128), np.float32)], axis=1).astype(NPBF)
    ones64 = np.full((128, 64), 1.0 / 64, np.float32)
    rmask = np.ones((128, TT), np.float32)
    rmask[:, ::CH] = 0.0
    onesf = np.ones((128, 64), np.float32)
    cf = np.concatenate([ones64, rmask, onesf], axis=1)
    inv_freq = (10000.0 ** (-np.arange(0, 64, 2, dtype=np.float32) / 64)).astype(np.float32)
    pos = np.arange(S, dtype=np.float32)
    ang = (pos[:, None] * inv_freq[None, :]).astype(np.float32)
    cos = np.cos(ang).astype(np.float32).T
    sin = np.sin(ang).astype(np.float32).T
    cosf = np.tile(cos, (4, 1))
    sinf = np.concatenate([-sin, sin, -sin, sin], axis=0)
    tab = np.stack([cosf, sinf], axis=1).astype(np.float32)
    return cb, cf, tab


CB_ID, CB_PERM, CB_BD, CB_AM, CB_M4, CB_MTS = 0, 128, 256, 384, 640, 1152
CB_ONES = 1280
CB_W = 1408
CF_W = 64 + TT + 64


def build_l1(dbg=False, phases="ACD", ntt=NTT, passes=(0, 1)):
    nc = bass.Bass("TRN2", target_bir_lowering=False)
    xT = nc.dram_tensor("xT", [D, S], F32, kind="ExternalInput").ap()
    wsl = nc.dram_tensor("wsl", [D, 2336], F32, kind="ExternalInput").ap()
    prm = nc.dram_tensor("prm", [128, 64], F32, kind="ExternalInput").ap()
    cm = nc.dram_tensor("cm", [128, 768], F32, kind="ExternalInput").ap()
    cbd = nc.dram_tensor("cb", [128, CB_W], BF16, kind="ExternalInput").ap()
    cfd = nc.dram_tensor("cf", [128, CF_W], F32, kind="ExternalInput").ap()
    tab = nc.dram_tensor("tab", [128, 2, S], F32, kind="ExternalInput").ap()
    mix = nc.dram_tensor("mix", [512, S], F32, kind="ExternalOutput").ap()
    sk = "ExternalOutput" if dbg else "Internal"
    SD = {}
    for n in ["q", "k", "v", "za", "R", "A", "B", "K", "V2"]:
        SD[n] = nc.dram_tensor("S_" + n, [256, S], BF16, kind=sk).ap()
    for n in ["gate", "bonus"]:
        SD[n] = nc.dram_tensor("S_" + n, [256, S], F32, kind=sk).ap()
    SD["pc"] = nc.dram_tensor("S_pc", [256, NCH], F32, kind=sk).ap()

    with contextlib.ExitStack() as st:
        P = Prog(nc, st)
        op = P.op
        prm_t = Tl(P, "prm", [128, 64], F32)
        cb = Tl(P, "cb", [128, CB_W], BF16)
        cf = Tl(P, "cf", [128, CF_W], F32)
        cmf = Tl(P, "cmf", [128, 768], F32)
        cmb = Tl(P, "cmb", [128, 768], BF16)
        op("sync", lambda e: e.dma_start(out=prm_t[:], in_=prm), writes=[prm_t], dma=prm_t)
        op("sync", lambda e: e.dma_start(out=cb[:], in_=cbd), writes=[cb], dma=cb)
        op("sync", lambda e: e.dma_start(out=cf[:], in_=cfd), writes=[cf], dma=cf)
        op("sync", lambda e: e.dma_start(out=cmf[:], in_=cm), writes=[cmf], dma=cmf)
        op("vector", lambda e: e.tensor_copy(out=cmb[:], in_=cmf[:]), reads=[cmf], writes=[cmb])
        ident = cb[:, CB_ID:CB_ID + 128]
        perm = cb[:, CB_PERM:CB_PERM + 128]
        bdm = cb[:, CB_BD:CB_BD + 128]
        rmask = cf[:, 64:64 + TT]

        def pcol(i):
            return prm_t[:, i:i + 1]

        with contextlib.ExitStack() as sa:
            WN = 1312

            def mk(name, shape, dt, psum=False):
                tl = Tl.__new__(Tl)
                tl.t = (P.ps if psum else P.sb)(name, shape, dt, sa)
                tl.k = P.tok(name)
                return tl

            Wb = mk("Wb", [128, 16, WN], BF16)
            ws = [mk("ws", [128, WN], F32) for _ in range(2)]
            xs = [mk("xs", [128, 2, TT], F32) for _ in range(2)]
            sq = [mk("sq", [128, 2, TT], BF16) for _ in range(2)]
            xb = [mk("xb", [128, 16, TT], BF16) for _ in range(2)]
            pj = [mk("pj", [128, TT], F32, True) for _ in range(3)]
            ssp = mk("ssp", [128, TT], F32, True)
            aux = [mk("aux", [128, TT], F32, True) for _ in range(3)]
            auxi = [0]

            def nxaux():
                auxi[0] += 1
                return aux[auxi[0] % 3]

            rstd = mk("rstd", [128, TT], F32)
            tss = mk("tss", [128, TT], F32)
            U = {ct: mk("U%d" % ct, [128, TT + 1], F32) for ct in [8, 9, 10, 11, 12, 13, 16, 17, 18]}
            lastc = {ct: mk("lc%d" % ct, [128, 1], F32) for ct in U}
            for ct in U:
                op("gpsimd", lambda e, ct=ct: e.memset(lastc[ct][:], 0.0), writes=[lastc[ct]])
            dtl = mk("dtl", [128, TT], F32)
            LA = mk("LA", [128, TT], BF16)
            GL0 = mk("GL0", [128, TT], BF16)
            GL1 = mk("GL1", [128, TT], BF16)
            lw = [mk("lw", [128, TT], F32) for _ in range(2)]
            aa = [mk("aa", [128, TT], F32) for _ in range(2)]
            gg = [mk("gg", [128, TT], F32) for _ in range(2)]
            zr = mk("zr", [128, TT], F32)
            kkr = mk("kkr", [128, TT], F32)
            sqb = mk("sqb", [128, TT], BF16)
            sn = mk("sn", [128, TT], F32)
            apt = mk("apt", [128, TT], F32)
            bpt = mk("bpt", [128, TT], F32)
            kpt = mk("kpt", [128, TT], F32)
            rkr = mk("rkr", [128, TT], BF16)
            cum = mk("cum", [128, TT], F32)
            cumx = mk("cumx", [128, TT], F32)
            eP = mk("eP", [128, TT], F32)
            eN = mk("eN", [128, TT], F32)
            ePx = mk("ePx", [128, TT], F32)
            pcs = mk("pcs", [128, 2, NCH], F32)
            stg = {n: mk("st_" + n, [128, TT], BF16) for n in ["R", "A", "B", "K", "V2", "q", "k", "v", "za"]}
            stg["gate"] = mk("st_gate", [128, TT], F32)
            stg["bonus"] = mk("st_bonus", [128, TT], F32)
            cst = mk("cst", [128, 2, TT], F32)
            qf = mk("qf", [128, TT], F32)
            qb = mk("qb", [128, TT], BF16)
            t1 = mk("t1", [128, TT], F32)
            t2 = mk("t2", [128, TT], F32)

            def store(name, c2, tt):
                dst = SD[name][c2 * 128:(c2 + 1) * 128, tt * TT:(tt + 1) * TT]
                op("sync", lambda e: e.dma_start(out=dst, in_=stg[name][:]), reads=[stg[name]],
                   writes=[P.dtok((name, c2, tt))], dma=stg[name])

            xTv = xT.rearrange("(kc p) t -> p kc t", p=128)
            xcnt = [0]

            for pas in passes:
                if pas == 0:
                    c0, ncol = 1024, 1312
                    cts = [16, 17, 18, 8, 10, 12, 14, 9, 11, 13, 15]
                else:
                    c0, ncol = 0, 1024
                    cts = [0, 1, 2, 3, 4, 5, 6, 7]
                for kc in range(16):
                    w = ws[kc % 2]
                    op("sync", lambda e, w=w, kc=kc: e.dma_start(out=w[:, 0:ncol], in_=wsl[kc * 128:(kc + 1) * 128, c0:c0 + ncol]),
                       writes=[w], dma=w)
                    op("vector", lambda e, w=w, kc=kc: e.tensor_scalar(out=Wb[:, kc, 0:ncol], in0=w[:, 0:ncol], scalar1=pcol(kc), scalar2=None, op0=ALU.mult),
                       reads=[w, prm_t], writes=[Wb])

                for tt in range(ntt):
                    tsl = slice(tt * TT, (tt + 1) * TT)
                    xbb = xb[tt % 2]
                    for j in range(8):
                        xi = xcnt[0] % 2
                        xcnt[0] += 1
                        op("sync", lambda e, xi=xi, j=j: e.dma_start(out=xs[xi][:], in_=xTv[:, 2 * j:2 * j + 2, tsl]), writes=[xs[xi]], dma=xs[xi])
                        op("vector", lambda e, xi=xi, j=j: e.tensor_copy(out=xbb[:, 2 * j:2 * j + 2, :], in_=xs[xi][:]), reads=[xs[xi]], writes=[xbb])
                        op("scalar", lambda e, xi=xi: e.activation(out=sq[xi][:], in_=xs[xi][:], func=AF.Square), reads=[xs[xi]], writes=[sq[xi]])

                        def ssmm(e, xi=xi, j=j):
                            for q in range(2):
                                ins = e.matmul(ssp[:], lhsT=cb[:, CB_ONES:CB_ONES + 128], rhs=sq[xi][:, q, :], start=(j == 0 and q == 0), stop=(j == 7 and q == 1))
                            return ins
                        op("tensor", ssmm, reads=[sq[xi], cb], writes=[ssp])
                    op("vector", lambda e: e.tensor_scalar(out=tss[:], in0=ssp[:], scalar1=1.0 / D, scalar2=1e-5, op0=ALU.mult, op1=ALU.add), reads=[ssp], writes=[tss])
                    op("scalar", lambda e: e.activation(out=tss[:], in_=tss[:], func=AF.Sqrt), reads=[tss], writes=[tss])
                    op("vector", lambda e: e.reciprocal(out=rstd[:], in_=tss[:]), reads=[tss], writes=[rstd])
                    if pas == 1:
                        op("sync", lambda e: e.dma_start(out=cst[:], in_=tab[:, :, tsl]), writes=[cst], dma=cst)

                    for ci, ct in enumerate(cts):
                        pp = pj[ci % 3]
                        wc0 = ct * 128 - c0
                        wn = 32 if ct == 18 else 128

                        def proj(e, pp=pp, wc0=wc0, wn=wn):
                            for kc in range(16):
                                ins = e.matmul(pp[0:wn, :], lhsT=Wb[:, kc, wc0:wc0 + wn], rhs=xbb[:, kc, :], start=(kc == 0), stop=(kc == 15))
                            return ins
                        op("tensor", proj, reads=[Wb, xbb], writes=[pp])

                        if ct in U:
                            u = U[ct]
                            lc = lastc[ct]
                            mucol = {8: 16, 9: 17, 10: 18, 11: 19, 12: 20, 13: 21, 16: 22, 17: 23, 18: 24}[ct]
                            op("vector", lambda e, u=u, pp=pp, wn=wn: e.tensor_tensor(out=u[0:wn, 1:TT + 1], in0=pp[0:wn, :], in1=rstd[0:wn, :], op=ALU.mult), reads=[pp, rstd], writes=[u])
                            op("gpsimd", lambda e, u=u, lc=lc, wn=wn: e.tensor_copy(out=u[0:wn, 0:1], in_=lc[0:wn, :]), reads=[lc], writes=[u])
                            op("gpsimd", lambda e, u=u, lc=lc, wn=wn: e.tensor_copy(out=lc[0:wn, :], in_=u[0:wn, TT:TT + 1]), reads=[u], writes=[lc])
                            op("gpsimd", lambda e, u=u, wn=wn: e.tensor_tensor(out=dtl[0:wn, :], in0=u[0:wn, 0:TT], in1=u[0:wn, 1:TT + 1], op=ALU.subtract), reads=[u], writes=[dtl])
                            op("vector", lambda e, u=u, wn=wn, mucol=mucol: e.scalar_tensor_tensor(out=u[0:wn, 1:TT + 1], in0=dtl[0:wn, :], scalar=prm_t[0:wn, mucol:mucol + 1], in1=u[0:wn, 1:TT + 1], op0=ALU.mult, op1=ALU.add), reads=[dtl, u, prm_t], writes=[u])

                        if ct == 16:
                            u = U[16]
                            op("scalar", lambda e, u=u: e.activation(out=LA[0:64, :], in_=u[0:64, 1:TT + 1], func=AF.Tanh), reads=[u], writes=[LA])
                            op("scalar", lambda e, u=u: e.activation(out=LA[64:128, :], in_=u[64:128, 1:TT + 1], func=AF.Copy), reads=[u], writes=[LA])
                            for c2 in range(2):
                                a1 = nxaux()
                                op("tensor", lambda e, a1=a1, c2=c2: e.matmul(a1[:], lhsT=cmb[0:64, c2 * 128:(c2 + 1) * 128], rhs=LA[0:64, :], start=True, stop=True), reads=[cmb, LA], writes=[a1])
                                op("scalar", lambda e, a1=a1, c2=c2: e.activation(out=lw[c2][:], in_=a1[:], func=AF.Sigmoid, bias=pcol(25 + c2), scale=1.0), reads=[a1, prm_t], writes=[lw[c2]])
                                op("gpsimd", lambda e, c2=c2: e.tensor_scalar(out=lw[c2][:], in0=lw[c2][:], scalar1=-DEC, scalar2=None, op0=ALU.mult), reads=[lw[c2]], writes=[lw[c2]])
                                a2 = nxaux()
                                op("tensor", lambda e, a2=a2, c2=c2: e.matmul(a2[:], lhsT=cmb[64:128, c2 * 128:(c2 + 1) * 128], rhs=LA[64:128, :], start=True, stop=True), reads=[cmb, LA], writes=[a2])
                                op("scalar", lambda e, a2=a2, c2=c2: e.activation(out=aa[c2][:], in_=a2[:], func=AF.Sigmoid, bias=pcol(27 + c2), scale=1.0), reads=[a2, prm_t], writes=[aa[c2]])
                        if ct == 17:
                            op("scalar", lambda e: e.activation(out=GL0[:], in_=U[17][:, 1:TT + 1], func=AF.Sigmoid), reads=[U[17]], writes=[GL0])
                        if ct == 18:
                            op("scalar", lambda e: e.activation(out=GL1[0:32, :], in_=U[18][0:32, 1:TT + 1], func=AF.Sigmoid), reads=[U[18]], writes=[GL1])
                            for c2 in range(2):
                                a1 = nxaux()

                                def gmm(e, a1=a1, c2=c2):
                                    e.matmul(a1[:], lhsT=cmb[:, 256 + c2 * 128:256 + (c2 + 1) * 128], rhs=GL0[:], start=True, stop=False)
                                    return e.matmul(a1[:], lhsT=cmb[0:32, 512 + c2 * 128:512 + (c2 + 1) * 128], rhs=GL1[0:32, :], start=False, stop=True)
                                op("tensor", gmm, reads=[cmb, GL0, GL1], writes=[a1])
                                op("scalar", lambda e, a1=a1, c2=c2: e.activation(out=gg[c2][:], in_=a1[:], func=AF.Copy), reads=[a1], writes=[gg[c2]])
                        if ct in (14, 15):
                            c2 = ct - 14
                            op("vector", lambda e, pp=pp: e.tensor_tensor(out=zr[:], in0=pp[:], in1=rstd[:], op=ALU.mult), reads=[pp, rstd], writes=[zr])
                            op("scalar", lambda e: e.activation(out=zr[:], in_=zr[:], func=AF.Silu), reads=[zr], writes=[zr])
                            r_s = U[8 + c2]
                            k_s = U[10 + c2]
                            v_s = U[12 + c2]
                            RS = lambda t: t[:, 1:TT + 1]
                            op("gpsimd", lambda e, c2=c2: e.tensor_tensor(out=stg["gate"][:], in0=gg[c2][:], in1=zr[:], op=ALU.mult), reads=[gg[c2], zr], writes=[stg["gate"]])
                            store("gate", c2, tt)
                            op("vector", lambda e, c2=c2, k_s=k_s: e.tensor_scalar(out=kkr[:], in0=RS(k_s), scalar1=pcol(29 + c2), scalar2=None, op0=ALU.mult), reads=[k_s, prm_t], writes=[kkr])
                            op("scalar", lambda e: e.activation(out=sqb[:], in_=kkr[:], func=AF.Square), reads=[kkr], writes=[sqb])
                            a1 = nxaux()
                            op("tensor", lambda e, a1=a1: e.matmul(a1[:], lhsT=bdm, rhs=sqb[:], start=True, stop=True), reads=[cb, sqb], writes=[a1])
                            op("scalar", lambda e, a1=a1: e.activation(out=sn[:], in_=a1[:], func=AF.Sqrt), reads=[a1], writes=[sn])
                            op("vector", lambda e: e.tensor_scalar(out=sn[:], in0=sn[:], scalar1=1e-12, scalar2=None, op0=ALU.max), reads=[sn], writes=[sn])
                            op("vector", lambda e: e.reciprocal(out=sn[:], in_=sn[:]), reads=[sn], writes=[sn])
                            op("vector", lambda e: e.scalar_tensor_tensor(out=apt[:], in0=kkr[:], scalar=-1.0, in1=sn[:], op0=ALU.mult, op1=ALU.mult), reads=[kkr, sn], writes=[apt])
                            op("vector", lambda e, c2=c2: e.scalar_tensor_tensor(out=bpt[:], in0=apt[:], scalar=-1.0, in1=aa[c2][:], op0=ALU.mult, op1=ALU.mult), reads=[apt, aa[c2]], writes=[bpt])
                            op("vector", lambda e, c2=c2: e.tensor_scalar(out=kpt[:], in0=aa[c2][:], scalar1=-1.0, scalar2=pcol(31 + c2), op0=ALU.add, op1=ALU.mult), reads=[aa[c2], prm_t], writes=[kpt])
                            op("vector", lambda e, k_s=k_s: e.scalar_tensor_tensor(out=kpt[:], in0=kpt[:], scalar=1.0, in1=RS(k_s), op0=ALU.add, op1=ALU.mult), reads=[kpt, k_s], writes=[kpt])
                            op("vector", lambda e, c2=c2, r_s=r_s: e.scalar_tensor_tensor(out=rkr[:], in0=RS(r_s), scalar=pcol(33 + c2), in1=kpt[:], op0=ALU.mult, op1=ALU.mult), reads=[r_s, kpt, prm_t], writes=[rkr])
                            a2 = nxaux()
                            op("tensor", lambda e, a2=a2: e.matmul(a2[:], lhsT=bdm, rhs=rkr[:], start=True, stop=True), reads=[cb, rkr], writes=[a2])
                            op("vector", lambda e, a2=a2, v_s=v_s: e.tensor_tensor(out=stg["bonus"][:], in0=a2[:], in1=RS(v_s), op=ALU.mult), reads=[a2, v_s], writes=[stg["bonus"]])
                            store("bonus", c2, tt)
                            op("vector", lambda e, c2=c2: e.tensor_tensor_scan(out=cum[:], data0=rmask, data1=lw[c2][:], initial=0.0, op0=ALU.mult, op1=ALU.add), reads=[cf, lw[c2]], writes=[cum])
                            op("gpsimd", lambda e, c2=c2: e.tensor_tensor(out=cumx[:], in0=cum[:], in1=lw[c2][:], op=ALU.subtract), reads=[cum, lw[c2]], writes=[cumx])
                            op("scalar", lambda e: e.activation(out=eP[:], in_=cum[:], func=AF.Exp), reads=[cum], writes=[eP])
                            op("scalar", lambda e: e.activation(out=eN[:], in_=cum[:], func=AF.Exp, scale=-1.0), reads=[cum], writes=[eN])
                            op("scalar", lambda e: e.activation(out=ePx[:], in_=cumx[:], func=AF.Exp), reads=[cumx], writes=[ePx])
                            op("gpsimd", lambda e, c2=c2, tt=tt: e.tensor_copy(out=pcs[:, c2, tt * 4:(tt + 1) * 4], in_=eP[:, CH - 1:TT:CH]), reads=[eP], writes=[pcs])
                            op("vector", lambda e, r_s=r_s: e.tensor_tensor(out=stg["R"][:], in0=RS(r_s), in1=eP[:], op=ALU.mult), reads=[r_s, eP], writes=[stg["R"]])
                            store("R", c2, tt)
                            op("gpsimd", lambda e: e.tensor_tensor(out=stg["A"][:], in0=apt[:], in1=ePx[:], op=ALU.mult), reads=[apt, ePx], writes=[stg["A"]])
                            store("A", c2, tt)
                            op("vector", lambda e: e.tensor_tensor(out=stg["B"][:], in0=bpt[:], in1=eN[:], op=ALU.mult), reads=[bpt, eN], writes=[stg["B"]])
                            store("B", c2, tt)
                            op("gpsimd", lambda e: e.tensor_tensor(out=stg["K"][:], in0=kpt[:], in1=eN[:], op=ALU.mult), reads=[kpt, eN], writes=[stg["K"]])
                            store("K", c2, tt)
                            op("scalar", lambda e, v_s=v_s: e.activation(out=stg["V2"][:], in_=RS(v_s), func=AF.Copy), reads=[v_s], writes=[stg["V2"]])
                            store("V2", c2, tt)
                        if ct in (0, 1, 2, 3):
                            nm = "q" if ct < 2 else "k"
                            c2 = ct % 2
                            scl = 0.125 if ct < 2 else 1.0
                            op("vector", lambda e, pp=pp, scl=scl: e.scalar_tensor_tensor(out=qf[:], in0=pp[:], scalar=scl, in1=rstd[:], op0=ALU.mult, op1=ALU.mult), reads=[pp, rstd], writes=[qf])
                            op("scalar", lambda e: e.activation(out=qb[:], in_=qf[:], func=AF.Copy), reads=[qf], writes=[qb])
                            a1 = nxaux()
                            op("tensor", lambda e, a1=a1: e.matmul(a1[:], lhsT=perm, rhs=qb[:], start=True, stop=True), reads=[cb, qb], writes=[a1])
                            op("gpsimd", lambda e: e.tensor_tensor(out=t1[:], in0=qf[:], in1=cst[:, 0, :], op=ALU.mult), reads=[qf, cst], writes=[t1])
                            op("vector", lambda e, a1=a1: e.tensor_tensor(out=t2[:], in0=a1[:], in1=cst[:, 1, :], op=ALU.mult), reads=[a1, cst], writes=[t2])
                            op("gpsimd", lambda e, nm=nm: e.tensor_tensor(out=stg[nm][:], in0=t1[:], in1=t2[:], op=ALU.add), reads=[t1, t2], writes=[stg[nm]])
                            store(nm, c2, tt)
                        if ct in (4, 5):
                            op("vector", lambda e, pp=pp: e.tensor_tensor(out=stg["v"][:], in0=pp[:], in1=rstd[:], op=ALU.mult), reads=[pp, rstd], writes=[stg["v"]])
                            store("v", ct - 4, tt)
                        if ct in (6, 7):
                            op("vector", lambda e, pp=pp: e.tensor_tensor(out=qf[:], in0=pp[:], in1=rstd[:], op=ALU.mult), reads=[pp, rstd], writes=[qf])
                            op("scalar", lambda e: e.activation(out=stg["za"][:], in_=qf[:], func=AF.Silu), reads=[qf], writes=[stg["za"]])
                            store("za", ct - 6, tt)
                if pas == 0:
                    for c2 in range(2):
                        op("sync", lambda e, c2=c2: e.dma_start(out=SD["pc"][c2 * 128:(c2 + 1) * 128, :], in_=pcs[:, c2, :]), reads=[pcs], writes=[P.dtok(("pc", c2))], dma=pcs)
                        op("sync", None, reads=[P.dtok(("pc", c2))])
            P.barrier()
            P.flush()
        if "C" in phases:
            build_attention(P, nc, SD, mix, cb, cf)
        if "D" in phases:
            build_rwkv(P, nc, SD, mix, cb, cf, prm_t)
        op("sync", None, writes=P.toks)
        P.flush()
    return nc


def _mk(P, sa, name, shape, dt, psum=False):
    tl = Tl.__new__(Tl)
    tl.t = (P.ps if psum else P.sb)(name, shape, dt, sa)
    tl.k = P.tok(name)
    return tl


def build_attention(P, nc, SD, mix, cb, cf):
    op = P.op
    NB = 4
    with contextlib.ExitStack() as sa:
        mk = lambda n, sh, dt, ps=False: _mk(P, sa, n, sh, dt, ps)
        QKVZ = [[mk(n, [64, S], BF16) for n in ("Q", "K", "V", "ZA")] for _ in range(1)]
        accs = [mk("acc", [65, S], F32) for _ in range(2)]
        vtok = mk("vtok", [128, 192, 65], BF16)
        UB = [mk("UB", [128, 512], F32, True) for _ in range(NB)]
        TB = [mk("TB", [128, 1024], BF16, True) for _ in range(2)]
        pB = mk("pB", [128, TT], F32, True)
        PT = [mk("PT", [128, 256], BF16) for _ in range(NB)]
        rec = [mk("rec", [64, TT], F32) for _ in range(2)]
        ost = [mk("ost", [64, TT], F32) for _ in range(2)]
        op("gpsimd", lambda e: e.memset(vtok[:], 1.0), writes=[vtok])
        mbias = cb[:, CB_AM:CB_AM + 256]
        ident = cb[:, CB_ID:CB_ID + 128]
        id64 = cb[0:64, CB_ID:CB_ID + 64]
        onesrow = cf[64:65, 64 + TT:64 + TT + 64]

        def load(h):
            rows = slice(h * 64, (h + 1) * 64)
            c2 = h // 2
            deps = [P.dtok((n, c2, tt)) for n in ["q", "k", "v", "za"] for tt in range(NTT)]
            for t, n in zip(QKVZ[0], ["q", "k", "v", "za"]):
                op("sync", lambda e, t=t, n=n, rows=rows: e.dma_start(out=t[:], in_=SD[n][rows, :]), reads=deps, writes=[t], dma=t)

        u = 0
        for h in range(4):
            rows = slice(h * 64, (h + 1) * 64)
            Q, K, V, ZA = QKVZ[0]
            load(h)
            for acc in accs:
                op("gpsimd", lambda e, acc=acc: e.memset(acc[:], 0.0), writes=[acc])
            blocks = []
            for d in (1, 4, 16):
                nb = (S // d) // 128
                for r in range(d):
                    for j in range(nb):
                        nq = 256 if j < nb - 1 else 128
                        k0 = r + 128 * j * d
                        blocks.append((slice(k0, k0 + 127 * d + 1, d), slice(k0, k0 + (nq - 1) * d + 1, d), nq))
            for g8 in range(len(blocks) // 8):
                tb = TB[g8 % 2]

                def tr8(e, g8=g8, tb=tb, V=V):
                    for q in range(8):
                        ins = e.transpose(tb[:, q * 64:(q + 1) * 64], V[:, blocks[g8 * 8 + q][0]], id64)
                    return ins
                op("tensor", tr8, reads=[V, cb], writes=[tb])
                src = tb[:, 0:512].rearrange("p (b c) -> p b c", c=64)
                if g8 % 2 == 0:
                    op("vector", lambda e, g8=g8, src=src: e.tensor_copy(out=vtok[:, g8 * 8:(g8 + 1) * 8, 0:64], in_=src), reads=[tb], writes=[vtok])
                else:
                    op("scalar", lambda e, g8=g8, src=src: e.activation(out=vtok[:, g8 * 8:(g8 + 1) * 8, 0:64], in_=src, func=AF.Copy), reads=[tb], writes=[vtok])

            def s_part(bi):
                ksl, qsl, nq = blocks[bi]
                i = (u + bi) % NB
                ub = UB[i]

                def smm(e, ub=ub, ksl=ksl, qsl=qsl, nq=nq, K=K, Q=Q):
                    e.matmul(ub[:, 0:nq], lhsT=K[:, ksl], rhs=Q[:, qsl], start=True, stop=False)
                    return e.matmul(ub[:, 0:nq], lhsT=ident, rhs=mbias[:, 0:nq], start=False, stop=True)
                op("tensor", smm, reads=[K, Q, cb], writes=[ub])
                op("scalar", lambda e, ub=ub, i=i, nq=nq: e.activation(out=PT[i][:, 0:nq], in_=ub[:, 0:nq], func=AF.Exp), reads=[ub], writes=[PT[i]])

            def o_part(bi):
                ksl, qsl, nq = blocks[bi]
                i = (u + bi) % NB
                ub = UB[i]
                op("tensor", lambda e, ub=ub, i=i, nq=nq, bi=bi: e.matmul(ub[0:65, 256:256 + nq], lhsT=vtok[:, bi, :], rhs=PT[i][:, 0:nq], start=True, stop=True), reads=[vtok, PT[i]], writes=[ub])
                acc = accs[bi % 2]
                op("vector", lambda e, ub=ub, nq=nq, qsl=qsl, acc=acc: e.tensor_tensor(out=acc[:, qsl], in0=ub[0:65, 256:256 + nq], in1=acc[:, qsl], op=ALU.add), reads=[ub, acc], writes=[acc])

            SK = 2
            nbk = len(blocks)
            for bi in range(nbk + SK):
                if bi < nbk:
                    s_part(bi)
                if bi >= SK:
                    o_part(bi - SK)
            u += nbk
            for tt in range(NTT):
                tsl = slice(tt * TT, (tt + 1) * TT)
                rc, os_ = rec[tt % 2], ost[tt % 2]
                def denmm(e, tsl=tsl):
                    e.matmul(pB[0:64, :], lhsT=onesrow, rhs=accs[0][64:65, tsl], start=True, stop=False)
                    return e.matmul(pB[0:64, :], lhsT=onesrow, rhs=accs[1][64:65, tsl], start=False, stop=True)
                op("tensor", denmm, reads=[cf, accs[0], accs[1]], writes=[pB])
                op("vector", lambda e, rc=rc: e.reciprocal(out=rc[:], in_=pB[0:64, :]), reads=[pB], writes=[rc])
                op("gpsimd", lambda e, os_=os_, tsl=tsl: e.tensor_tensor(out=os_[:], in0=accs[0][0:64, tsl], in1=accs[1][0:64, tsl], op=ALU.add), reads=[accs[0], accs[1]], writes=[os_])
                op("vector", lambda e, rc=rc, os_=os_: e.tensor_tensor(out=rc[:], in0=os_[:], in1=rc[:], op=ALU.mult), reads=[os_, rc], writes=[rc])
                op("gpsimd", lambda e, rc=rc, os_=os_, tsl=tsl, ZA=ZA: e.tensor_tensor(out=os_[:], in0=rc[:], in1=ZA[:, tsl], op=ALU.mult), reads=[rc, ZA], writes=[os_])
                op("sync", lambda e, os_=os_, tsl=tsl, rows=rows: e.dma_start(out=mix[rows, tsl], in_=os_[:]), reads=[os_], writes=[P.dtok(("mixa", h, tt))], dma=os_)
        P.barrier()
        P.flush()


def build_rwkv(P, nc, SD, mix, cb, cf, prm_t):
    op = P.op
    SEG = 512
    NSEG = S // SEG
    CPS = SEG // CH
    PD = 3
    NP = PD + 1
    with contextlib.ExitStack() as sa:
        mk = lambda n, sh, dt, ps=False: _mk(P, sa, n, sh, dt, ps)
        ident = cb[:, CB_ID:CB_ID + 128]
        id64 = cb[0:64, CB_ID:CB_ID + 64]
        mask4 = cb[:, CB_M4:CB_M4 + 512]
        maskts = cb[:, CB_MTS:CB_MTS + 128]
        ones64 = cf[0:64, 0:64]
        pool = [(P.ps("pb", [128, 512], F32, sa), P.tok("PB")) for _ in range(6)]
        b3 = P.ps("b3", [128, 1024], BF16, sa)
        t3 = P.tok("B3")
        pi = [0]

        def nxb():
            pi[0] += 1
            return pool[pi[0] % 6]

        HD = []
        for hh in range(4):
            d = {}
            for n in ["R", "A", "B", "K", "V2"]:
                d[n] = [mk("r" + n, [64, SEG], BF16) for _ in range(2)]
            d["y"] = [mk("ry", [64, SEG], F32) for _ in range(2)]
            for n in ["gate", "bonus"]:
                d[n] = mk("r" + n, [64, SEG], F32)
            d["pc"] = mk("rpc", [64, NCH], F32)
            d["H"] = mk("H", [64, 64], F32)
            d["Hb"] = mk("Hb", [64, 64], BF16)
            d["Ht"] = mk("Ht", [64, 64], F32)
            d["G4"] = [mk("G4", [128, 512], BF16) for _ in range(NP)]
            d["N1"] = [mk("N1", [128, 128], BF16) for _ in range(NP)]
            d["tok3"] = [mk("tok3", [128, 192], BF16) for _ in range(NP)]
            d["Sm"] = [[mk("Sm", [128, 128], BF16) for _ in range(2)] for _ in range(NP)]
            d["NM"] = [[mk("NM", [128, 256], BF16) for _ in range(2)] for _ in range(NP)]
            d["Sf"] = [None] * NP
            d["Zb"] = mk("Zb", [128, 64], BF16)
            d["Ub"] = mk("Ub", [128, 64], BF16)
            for n in ["ysq", "mt", "m2", "var", "yc", "ost"]:
                d[n] = mk("r" + n, [64, TT], F32)
            HD.append(d)

        def pre(d, h, gc):
            seg, c = gc // CPS, gc % CPS
            p = gc % NP
            cs = slice(c * CH, (c + 1) * CH)
            RT, AT, BT, KT, VT = [d[n][seg % 2] for n in ["R", "A", "B", "K", "V2"]]
            G4, N1, tok3 = d["G4"][p], d["N1"][p], d["tok3"][p]
            b0, t0 = nxb()

            def g4(e):
                for bi, (l, r) in enumerate([(BT, AT), (KT, AT), (BT, RT), (KT, RT)]):
                    ins = e.matmul(b0[:, bi * 128:(bi + 1) * 128], lhsT=l[:, cs], rhs=r[:, cs], start=True, stop=True)
                return ins
            op("tensor", g4, reads=[RT, AT, BT, KT], writes=[t0])
            op("vector", lambda e: e.tensor_tensor(out=G4[:], in0=b0[:], in1=mask4, op=ALU.mult), reads=[cb], writes=[G4, t0])
            b1, t1 = nxb()
            op("tensor", lambda e: e.matmul(b1[:, 0:128], lhsT=AT[:, cs], rhs=BT[:, cs], start=True, stop=True), reads=[AT, BT], writes=[t1])
            op("vector", lambda e: e.tensor_tensor(out=N1[:], in0=b1[:, 0:128], in1=maskts, op=ALU.mult), reads=[cb], writes=[N1, t1])
            bq = b3[:, h * 256:(h + 1) * 256]

            def tr(e):
                e.transpose(bq[:, 0:64], VT[:, cs], id64)
                e.transpose(bq[:, 64:128], BT[:, cs], id64)
                return e.transpose(bq[:, 128:192], KT[:, cs], id64)
            op("tensor", tr, reads=[VT, BT, KT, cb], writes=[t3])
            op("scalar", lambda e: e.activation(out=tok3[:], in_=bq[:, 0:192], func=AF.Copy), reads=[], writes=[tok3, t3])
            yield
            Sm = d["Sm"][p][0]
            op("gpsimd", lambda e: e.tensor_tensor(out=Sm[:], in0=G4[:, 0:128], in1=ident, op=ALU.add), reads=[G4, cb], writes=[Sm])
            np_ap, mp_ap = (lambda: N1[:]), (lambda: G4[:, 0:128])
            np_tok, mp_tok = N1, G4
            si = 0
            prevNM = None
            for lvl in range(7):
                bs, ts_ = nxb()
                NM = d["NM"][p][lvl % 2] if lvl < 6 else None
                So = d["Sm"][p][si]
                Sn = d["Sm"][p][1 - si]

                def lvl_mm(e, np_ap=np_ap, mp_ap=mp_ap, bs=bs, lvl=lvl, prevNM=prevNM, So=So):
                    ins = None
                    if lvl < 6:
                        e.matmul(bs[:, 0:128], lhsT=mp_ap(), rhs=np_ap(), start=True, stop=True)
                        ins = e.matmul(bs[:, 128:256], lhsT=np_ap(), rhs=mp_ap(), start=True, stop=True)
                    if lvl > 0:
                        ins = e.matmul(bs[:, 256:384], lhsT=prevNM[:, 0:128], rhs=So[:], start=True, stop=True)
                    return ins
                rd = [mp_tok, np_tok] + ([prevNM, So] if lvl > 0 else [])
                op("tensor", lvl_mm, reads=rd, writes=[ts_])
                if lvl < 6:
                    op("scalar", lambda e, NM=NM, bs=bs: e.activation(out=NM[:], in_=bs[:, 0:256], func=AF.Copy), reads=[], writes=[NM, ts_])
                if lvl > 0:
                    op("vector", lambda e, So=So, Sn=Sn, bs=bs: e.tensor_tensor(out=Sn[:], in0=bs[:, 256:384], in1=So[:], op=ALU.add), reads=[So], writes=[Sn, ts_])
                    si = 1 - si
                if lvl < 6:
                    np_ap, mp_ap = (lambda NM=NM: NM[:, 0:128]), (lambda NM=NM: NM[:, 128:256])
                    np_tok = mp_tok = NM
                    prevNM = NM
                yield
            d["Sf"][p] = d["Sm"][p][si]

        def chain(d, h, gc):
            seg, c = gc // CPS, gc % CPS
            p = gc % NP
            cs = slice(c * CH, (c + 1) * CH)
            RT, AT = d["R"][seg % 2], d["A"][seg % 2]
            G4, tok3, Sf = d["G4"][p], d["tok3"][p], d["Sf"][p]
            y = d["y"][seg % 2]
            Hb, H, Ht = d["Hb"], d["H"], d["Ht"]
            Zb, Ub = d["Zb"], d["Ub"]
            bz, tz = nxb()

            def zmm(e):
                e.matmul(bz[:, 0:64], lhsT=AT[:, cs], rhs=Hb[:], start=True, stop=False)
                return e.matmul(bz[:, 0:64], lhsT=G4[:, 128:256], rhs=tok3[:, 0:64], start=False, stop=True)
            op("tensor", zmm, reads=[AT, Hb, G4, tok3], writes=[tz])
            op("scalar", lambda e: e.activation(out=Zb[:], in_=bz[:, 0:64], func=AF.Copy), reads=[], writes=[Zb, tz])
            yield
            bu, tu = nxb()
            op("tensor", lambda e: e.matmul(bu[:, 0:64], lhsT=Sf[:], rhs=Zb[:], start=True, stop=True), reads=[Sf, Zb], writes=[tu])
            op("vector", lambda e: e.tensor_copy(out=Ub[:], in_=bu[:, 0:64]), reads=[], writes=[Ub, tu])
            yield
            by, ty = nxb()
            bh, th = nxb()

            def yhmm(e):
                e.matmul(bh[0:64, 0:64], lhsT=tok3[:, 64:128], rhs=Ub[:], start=True, stop=False)
                e.matmul(bh[0:64, 0:64], lhsT=tok3[:, 128:192], rhs=tok3[:, 0:64], start=False, stop=True)
                e.matmul(by[0:64, 0:128], lhsT=Hb[:], rhs=RT[:, cs], start=True, stop=False)
                e.matmul(by[0:64, 0:128], lhsT=Ub[:], rhs=G4[:, 256:384], start=False, stop=False)
                return e.matmul(by[0:64, 0:128], lhsT=tok3[:, 0:64], rhs=G4[:, 384:512], start=False, stop=True)
            op("tensor", yhmm, reads=[Hb, RT, Ub, G4, tok3], writes=[ty, th])
            op("vector", lambda e: e.tensor_tensor(out=Ht[:], in0=bh[0:64, 0:64], in1=H[:], op=ALU.add), reads=[H], writes=[Ht, th])
            op("scalar", lambda e: e.activation(out=y[:, cs], in_=by[0:64, 0:128], func=AF.Copy), reads=[], writes=[y, ty])
            op("vector", lambda e: e.tensor_scalar(out=H[:], in0=Ht[:], scalar1=d["pc"][:, gc:gc + 1], scalar2=None, op0=ALU.mult), reads=[Ht, d["pc"]], writes=[H])
            op("vector", lambda e: e.tensor_copy(out=Hb[:], in_=H[:]), reads=[H], writes=[Hb])
            yield

        def post(d, h, seg):
            y = d["y"][seg % 2]
            for q in range(SEG // TT):
                ts = slice(q * TT, (q + 1) * TT)
                gts = slice(seg * SEG + q * TT, seg * SEG + (q + 1) * TT)
                bm, tm = nxb()
                bq, tq = nxb()
                op("scalar", lambda e, ts=ts: e.activation(out=d["ysq"][:], in_=y[:, ts], func=AF.Square), reads=[y], writes=[d["ysq"]])
                op("tensor", lambda e, ts=ts, bm=bm: e.matmul(bm[0:64, :], lhsT=ones64, rhs=y[:, ts], start=True, stop=True), reads=[cf, y], writes=[tm])
                op("scalar", lambda e, bm=bm: e.activation(out=d["mt"][:], in_=bm[0:64, :], func=AF.Copy), reads=[], writes=[d["mt"], tm])
                op("tensor", lambda e, bq=bq: e.matmul(bq[0:64, :], lhsT=ones64, rhs=d["ysq"][:], start=True, stop=True), reads=[cf, d["ysq"]], writes=[tq])
                op("gpsimd", lambda e: e.tensor_tensor(out=d["m2"][:], in0=d["mt"][:], in1=d["mt"][:], op=ALU.mult), reads=[d["mt"]], writes=[d["m2"]])
                op("vector", lambda e, bq=bq: e.tensor_tensor(out=d["var"][:], in0=bq[0:64, :], in1=d["m2"][:], op=ALU.subtract), reads=[d["m2"]], writes=[d["var"], tq])
                yield
                op("gpsimd", lambda e: e.tensor_scalar(out=d["var"][:], in0=d["var"][:], scalar1=64e-5, scalar2=None, op0=ALU.add), reads=[d["var"]], writes=[d["var"]])
                op("scalar", lambda e: e.activation(out=d["var"][:], in_=d["var"][:], func=AF.Sqrt), reads=[d["var"]], writes=[d["var"]])
                op("vector", lambda e: e.reciprocal(out=d["var"][:], in_=d["var"][:]), reads=[d["var"]], writes=[d["var"]])
                op("gpsimd", lambda e, ts=ts: e.tensor_tensor(out=d["yc"][:], in0=y[:, ts], in1=d["mt"][:], op=ALU.subtract), reads=[y, d["mt"]], writes=[d["yc"]])
                yield
                op("gpsimd", lambda e: e.tensor_tensor(out=d["yc"][:], in0=d["yc"][:], in1=d["var"][:], op=ALU.mult), reads=[d["yc"], d["var"]], writes=[d["yc"]])
                op("vector", lambda e: e.tensor_scalar(out=d["yc"][:], in0=d["yc"][:], scalar1=prm_t[0:64, 40 + h:41 + h], scalar2=prm_t[0:64, 44 + h:45 + h], op0=ALU.mult, op1=ALU.add), reads=[d["yc"], prm_t], writes=[d["yc"]])
                op("gpsimd", lambda e, ts=ts: e.tensor_tensor(out=d["yc"][:], in0=d["yc"][:], in1=d["bonus"][:, ts], op=ALU.add), reads=[d["yc"], d["bonus"]], writes=[d["yc"]])
                op("gpsimd", lambda e, ts=ts: e.tensor_tensor(out=d["ost"][:], in0=d["yc"][:], in1=d["gate"][:, ts], op=ALU.mult), reads=[d["yc"], d["gate"]], writes=[d["ost"]])
                op("sync", lambda e, gts=gts: e.dma_start(out=mix[256 + h * 64:256 + (h + 1) * 64, gts], in_=d["ost"][:]), reads=[d["ost"]], writes=[P.dtok(("mixr", h, seg, q))], dma=d["ost"])
                yield

        def loads(seg, names, idx):
            for h in range(4):
                d = HD[h]
                c2 = h // 2
                for n in names:
                    dst = d[n][idx] if idx is not None else d[n]
                    deps = [P.dtok((n, c2, tt)) for tt in range(seg * (SEG // TT), (seg + 1) * (SEG // TT))]
                    op("sync", lambda e, dst=dst, n=n, h=h, seg=seg: e.dma_start(out=dst[:], in_=SD[n][h * 64:(h + 1) * 64, seg * SEG:(seg + 1) * SEG]), reads=deps, writes=[dst], dma=dst)

        for h in range(4):
            d = HD[h]
            op("gpsimd", lambda e, d=d: e.memset(d["H"][:], 0.0), writes=[d["H"]])
            op("gpsimd", lambda e, d=d: e.memset(d["Hb"][:], 0.0), writes=[d["Hb"]])
            op("sync", lambda e, d=d, h=h: e.dma_start(out=d["pc"][:], in_=SD["pc"][h * 64:(h + 1) * 64, :]), reads=[P.dtok(("pc", h // 2))], writes=[d["pc"]], dma=d["pc"])
        loads(0, ["R", "A", "B", "K", "V2"], 0)

        def run(gens):
            alive = True
            while alive:
                alive = False
                for g in gens:
                    try:
                        next(g)
                        alive = True
                    except StopIteration:
                        pass

        run([pre(HD[h], h, 0) for h in range(4)])
        active = []
        for gc in range(1, PD):
            active.append([gc, [pre(HD[h], h, gc) for h in range(4)]])
        posts = []
        for gc in range(NCH):
            seg, c = gc // CPS, gc % CPS
            if c == 0 and seg + 1 < NSEG:
                loads(seg + 1, ["R", "A", "B", "K", "V2"], (seg + 1) % 2)
            if gc + PD < NCH:
                active.append([gc + PD, [pre(HD[h], h, gc + PD) for h in range(4)]])
            must = [chain(HD[h], h, gc) for h in range(4)]
            while True:
                pend = False
                for g in list(must):
                    try:
                        next(g)
                        pend = True
                    except StopIteration:
                        must.remove(g)
                for ent in active:
                    for g in list(ent[1]):
                        try:
                            next(g)
                            if ent[0] == gc + 1:
                                pend = True
                        except StopIteration:
                            ent[1].remove(g)
                for g in list(posts):
                    try:
                        next(g)
                    except StopIteration:
                        posts.remove(g)
                active = [ent for ent in active if ent[1]]
                if not pend:
                    break
            if c == CPS - 1:
                loads(seg, ["gate", "bonus"], None)
                posts += [post(HD[h], h, seg) for h in range(4)]
        run(posts)
        P.barrier()
        P.flush()


def build_l2():
    nc = bass.Bass("TRN2", target_bir_lowering=False)
    NT = 2048
    mixT = nc.dram_tensor("mixT", [D, NT], F32, kind="ExternalInput").ap()
    wout = nc.dram_tensor("wout", [D, D], F32, kind="ExternalInput").ap()
    xin = nc.dram_tensor("xin", [NT, D], F32, kind="ExternalInput").ap()
    gfin = nc.dram_tensor("gfin", [128, D], F32, kind="ExternalInput").ap()
    out = nc.dram_tensor("out", [NT, D], F32, kind="ExternalOutput").ap()
    with contextlib.ExitStack() as st:
        P = Prog(nc, st)
        op = P.op
        mk = lambda n, sh, dt, ps=False: _mk(P, st, n, sh, dt, ps)
        Wo = mk("Wo", [128, 16, D], BF16)
        wst = [mk("wst", [128, D], F32) for _ in range(2)]
        gf = mk("gf", [128, D], F32)
        op("sync", lambda e: e.dma_start(out=gf[:], in_=gfin), writes=[gf], dma=gf)
        for kc in range(16):
            w = wst[kc % 2]
            op("sync", lambda e, w=w, kc=kc: e.dma_start(out=w[:], in_=wout[kc * 128:(kc + 1) * 128, :]), writes=[w], dma=w)
            op("vector" if kc % 2 == 0 else "gpsimd", lambda e, w=w, kc=kc: e.tensor_copy(out=Wo[:, kc, :], in_=w[:]), reads=[w], writes=[Wo])
        mf = [mk("mf", [128, 16, 128], F32) for _ in range(2)]
        mb = [mk("mb", [128, 16, 128], BF16) for _ in range(2)]
        xt = [mk("xt", [128, D], F32) for _ in range(2)]
        ys = [mk("ys", [128, D], F32) for _ in range(2)]
        junk = mk("junk", [128, D], F32)
        ss = mk("ss", [128, 1], F32)
        pp = [mk("pp", [128, 512], F32, True) for _ in range(4)]
        mv = mixT.rearrange("(kc p) t -> p kc t", p=128)
        for t in range(NT // 128):
            i = t % 2
            tsl = slice(t * 128, (t + 1) * 128)
            op("sync", lambda e, i=i, tsl=tsl: e.dma_start(out=mf[i][:], in_=mv[:, :, tsl]), writes=[mf[i]], dma=mf[i])
            op("sync", lambda e, i=i, tsl=tsl: e.dma_start(out=xt[i][:], in_=xin[tsl, :]), writes=[xt[i]], dma=xt[i])
            op("gpsimd", lambda e, i=i: e.tensor_copy(out=mb[i][:], in_=mf[i][:]), reads=[mf[i]], writes=[mb[i]])
            for cg in range(4):
                def mm(e, i=i, cg=cg):
                    for kc in range(16):
                        ins = e.matmul(pp[cg][:], lhsT=mb[i][:, kc, :], rhs=Wo[:, kc, cg * 512:(cg + 1) * 512], start=(kc == 0), stop=(kc == 15))
                    return ins
                op("tensor", mm, reads=[mb[i], Wo], writes=[pp[cg]])
                op("vector", lambda e, i=i, cg=cg: e.tensor_tensor(out=ys[i][:, cg * 512:(cg + 1) * 512], in0=pp[cg][:], in1=xt[i][:, cg * 512:(cg + 1) * 512], op=ALU.add), reads=[pp[cg], xt[i]], writes=[ys[i]])
            op("scalar", lambda e, i=i: e.activation(out=junk[:], in_=ys[i][:], func=AF.Square, accum_out=ss[:]), reads=[ys[i]], writes=[junk, ss])
            op("vector", lambda e: e.tensor_scalar(out=ss[:], in0=ss[:], scalar1=1.0 / D, scalar2=1e-5, op0=ALU.mult, op1=ALU.add), reads=[ss], writes=[ss])
            op("scalar", lambda e: e.activation(out=ss[:], in_=ss[:], func=AF.Sqrt), reads=[ss], writes=[ss])
            op("vector", lambda e: e.reciprocal(out=ss[:], in_=ss[:]), reads=[ss], writes=[ss])
            op("vector", lambda e, i=i: e.scalar_tensor_tensor(out=ys[i][:], in0=ys[i][:], scalar=ss[:, 0:1], in1=gf[:], op0=ALU.mult, op1=ALU.mult), reads=[ys[i], ss, gf], writes=[ys[i]])
            op("sync", lambda e, i=i, tsl=tsl: e.dma_start(out=out[tsl, :], in_=ys[i][:]), reads=[ys[i]], writes=[P.dtok(("out", t))], dma=ys[i])
        op("sync", None, writes=P.toks)
        P.flush()
    return nc


def _group_cols(g):
    hs = np.arange(g * 256, (g + 1) * 256)
    base = 4 * A
    cols = [hs, A + hs, 2 * A + hs, 3 * A + hs,
            base + hs, base + R + hs, base + 2 * R + hs, base + SHIFT + hs,
            base + 3 * R + np.arange(128), base + 3 * R + 128 + np.arange(160)]
    return np.concatenate(cols)


def _prep_l1(inp, b, g, consts):
    cb, cf, tab = consts
    f = lambda n: np.asarray(inp[n], dtype=np.float32)
    hs = slice(g * 256, (g + 1) * 256)
    xT = np.ascontiguousarray(f("x")[b].T)
    wsl = np.ascontiguousarray(f("w_in")[0][:, _group_cols(g)])
    prm = np.zeros((128, 64), np.float32)
    prm[:, 0:16] = f("norm_g")[0].reshape(16, 128).T
    mu = f("shift_mu")[0]
    for i, off in enumerate([0, R, 2 * R]):
        prm[:, 16 + 2 * i:18 + 2 * i] = mu[off + g * 256: off + (g + 1) * 256].reshape(2, 128).T
    prm[:, 22] = mu[3 * R:3 * R + 128]
    prm[:, 23] = mu[3 * R + 128:3 * R + 256]
    prm[0:32, 24] = mu[3 * R + 256:3 * R + 288]
    for i, n in enumerate(["w0", "a0", "k_k", "k_a"]):
        prm[:, 25 + 2 * i:27 + 2 * i] = f(n)[0][hs].reshape(2, 128).T
    prm[:, 33:35] = f("r_k")[0].reshape(-1)[hs].reshape(2, 128).T
    lg = f("lnx_g")[0][hs].reshape(4, 64)
    lb = f("lnx_b")[0][hs].reshape(4, 64)
    prm[0:64, 40:44] = lg.T
    prm[0:64, 44:48] = lb.T
    cm = np.zeros((128, 768), np.float32)
    cm[0:64, 0:256] = f("w2")[0][:, hs]
    cm[64:128, 0:256] = f("a2")[0][:, hs]
    cm[:, 256:512] = f("g2")[0][0:128, hs]
    cm[0:32, 512:768] = f("g2")[0][128:160, hs]
    return {"xT": xT, "wsl": wsl, "prm": prm, "cm": cm, "cb": cb, "cf": cf, "tab": tab}


_CACHE = {}


def kernel(**inputs):
    consts = _consts()
    if "l1" not in _CACHE:
        _CACHE["l1"] = build_l1()
        _CACHE["l2"] = build_l2()
    in1 = [_prep_l1(inputs, c // 4, c % 4, consts) for c in range(8)]
    r1 = run_bass_kernel_spmd(_CACHE["l1"], in1, core_ids=list(range(8))).results
    x = np.asarray(inputs["x"], dtype=np.float32)
    wout = np.asarray(inputs["w_out"], dtype=np.float32)[0]
    gfin = np.ascontiguousarray(np.broadcast_to(np.asarray(inputs["final_g"], dtype=np.float32)[None, :], (128, D)))
    in2 = []
    for c in range(8):
        b, qd = c // 4, c % 4
        ts = slice(qd * 2048, (qd + 1) * 2048)
        mt = np.empty((D, 2048), np.float32)
        for g in range(4):
            m = np.asarray(r1[b * 4 + g]["mix"])
            mt[g * 256:(g + 1) * 256] = m[0:256, ts]
            mt[A + g * 256:A + (g + 1) * 256] = m[256:512, ts]
        in2.append({"mixT": mt, "wout": wout, "xin": np.ascontiguousarray(x[b, ts]), "gfin": gfin})
    r2 = run_bass_kernel_spmd(_CACHE["l2"], in2, core_ids=list(range(8))).results
    out = np.empty((2, S, D), np.float32)
    for c in range(8):
        b, qd = c // 4, c % 4
        out[b, qd * 2048:(qd + 1) * 2048] = np.asarray(r2[c]["out"])
    return out
```

```python
import contextlib
import types
import numpy as np
import ml_dtypes
import concourse.bass as bass
import concourse.mybir as mybir
from concourse.bass_utils import run_bass_kernel_spmd

F32 = mybir.dt.float32
BF16 = mybir.dt.bfloat16
AF = mybir.ActivationFunctionType
ALU = mybir.AluOpType
NPBF = ml_dtypes.bfloat16

ENGS = ["tensor", "vector", "scalar", "gpsimd", "sync"]

D = 2048
S = 8192
A = 1024
R = 1024
SHIFT = 3 * R + 64 + 64 + 160
TT = 512
NTT = S // TT
CH = 128
NCH = S // CH
DEC = 0.6065306597126334


def _snap(fn):
    if fn is None or fn.__closure__ is None:
        return fn
    cells = []
    for c in fn.__closure__:
        try:
            cells.append(types.CellType(c.cell_contents))
        except ValueError:
            cells.append(c)
    g = types.FunctionType(fn.__code__, fn.__globals__, fn.__name__, fn.__defaults__, tuple(cells))
    g.__kwdefaults__ = fn.__kwdefaults__
    return g


class Tok:
    __slots__ = ("name", "writes", "reads", "sem", "cnt")

    def __init__(self, name=""):
        self.name = name
        self.writes = []
        self.reads = []
        self.sem = None
        self.cnt = 0


class Tl:
    def __init__(self, P, name, shape, dt, psum=False):
        self.t = (P.ps if psum else P.sb)(name, shape, dt)
        self.k = P.tok(name)

    def __getitem__(self, i):
        return self.t[i]


def _k(t):
    return t.k if hasattr(t, "k") else t


class Prog:
    def __init__(self, nc, stack):
        self.nc = nc
        self.stack = stack
        self.ops = {e: [] for e in ENGS}
        self.ecount = {e: 0 for e in ENGS}
        self.waited = {e: {} for e in ENGS}
        self.esem = {e: stack.enter_context(nc.semaphore("s_" + e)) for e in ENGS}
        self.semobj = {("E", e): self.esem[e] for e in ENGS}
        self.nsem = len(ENGS)
        self.toks = []
        self.dtoks = {}
        self.nm = 0

    def sb(self, name, shape, dt, stack=None):
        self.nm += 1
        return (stack or self.stack).enter_context(self.nc.sbuf_tensor("%s_%d" % (name, self.nm), list(shape), dt))

    def ps(self, name, shape, dt=F32, stack=None):
        self.nm += 1
        return (stack or self.stack).enter_context(self.nc.psum_tensor("%s_%d" % (name, self.nm), list(shape), dt))

    def tok(self, name=""):
        t = Tok(name)
        self.toks.append(t)
        return t

    def dtok(self, key):
        if key not in self.dtoks:
            self.dtoks[key] = self.tok(str(key))
        return self.dtoks[key]

    def dsem(self, tok):
        if tok.sem is None:
            tok.sem = self.stack.enter_context(self.nc.semaphore("d%d" % self.nsem))
            self.nsem += 1
            self.semobj[("D", id(tok))] = tok.sem
        return tok.sem

    def op(self, eng, fn, reads=(), writes=(), dma=None):
        fn = _snap(fn)
        reads = [_k(t) for t in reads]
        writes = [_k(t) for t in writes]
        need = []
        for t in reads:
            need += t.writes
        for t in writes:
            need += t.writes
            need += t.reads
        waits = {}
        for (key, val, src) in need:
            if self.waited[eng].get(key, 0) >= val:
                continue
            if waits.get(key, 0) < val:
                waits[key] = val
        for k, v in waits.items():
            self.waited[eng][k] = v
        ev = None
        if fn is not None:
            if dma is None:
                self.ecount[eng] += 1
                ev = (("E", eng), self.ecount[eng], eng)
            else:
                dma = _k(dma)
                self.dsem(dma)
                dma.cnt += 16
                ev = (("D", id(dma)), dma.cnt, None)
            for t in reads:
                t.reads.append(ev)
            for t in writes:
                t.writes = [ev]
                t.reads = []
        self.ops[eng].append((list(waits.items()), fn, ev))

    def barrier(self):
        for e in ENGS:
            self.op(e, None, writes=self.toks)
        for t in self.toks:
            t.writes = []
            t.reads = []

    def emit(self, eng_name, e):
        for waits, fn, ev in self.ops[eng_name]:
            for key, val in waits:
                e.wait_ge(self.semobj[key], val)
            if fn is None:
                continue
            ins = fn(e)
            key, val, src = ev
            ins.then_inc(self.semobj[key], 16 if key[0] == "D" else 1)
        self.ops[eng_name] = []

    def flush(self):
        with self.nc.Block() as block:
            @block.tensor
            def _(e):
                self.emit("tensor", e)

            @block.vector
            def _(e):
                self.emit("vector", e)

            @block.scalar
            def _(e):
                self.emit("scalar", e)

            @block.gpsimd
            def _(e):
                self.emit("gpsimd", e)

            @block.sync
            def _(e):
                self.emit("sync", e)


def _consts():
    ident = np.eye(128, dtype=np.float32)
    perm = np.zeros((128, 128), np.float32)
    for m in range(128):
        p = m + 32 if (m % 64) < 32 else m - 32
        perm[p, m] = 1.0
    bd = np.zeros((128, 128), np.float32)
    bd[:64, :64] = 1.0
    bd[64:, 64:] = 1.0
    ki = np.arange(128)[:, None]
    qi = np.arange(128)[None, :]
    amask = (np.concatenate([(ki <= qi), (qi <= ki)], axis=1).astype(np.float32) - 1.0) * 30000.0
    strict = (ki < qi).astype(np.float32)
    incl = (ki <= qi).astype(np.float32)
    mask4 = np.concatenate([strict, strict, incl, incl], axis=1)
    maskts = (qi < ki).astype(np.float32)
    cb = np.concatenate([ident, perm, bd, amask, mask4, maskts, np.ones((128, 128), np.float32)], axis=1).astype(NPBF)
    ones64 = np.full((128, 64), 1.0 / 64, np.float32)
    rmask = np.ones((128, TT), np.float32)
    rmask[:, ::CH] = 0.0
    onesf = np.ones((128, 64), np.float32)
    cf = np.concatenate([ones64, rmask, onesf], axis=1)
    inv_freq = (10000.0 ** (-np.arange(0, 64, 2, dtype=np.float32) / 64)).astype(np.float32)
    pos = np.arange(S, dtype=np.float32)
    ang = (pos[:, None] * inv_freq[None, :]).astype(np.float32)
    cos = np.cos(ang).astype(np.float32).T
    sin = np.sin(ang).astype(np.float32).T
    cosf = np.tile(cos, (4, 1))
    sinf = np.concatenate([-sin, sin, -sin, sin], axis=0)
    tab = np.stack([cosf, sinf], axis=1).astype(np.float32)
    return cb, cf, tab


CB_ID, CB_PERM, CB_BD, CB_AM, CB_M4, CB_MTS = 0, 128, 256, 384, 640, 1152
CB_ONES = 1280
CB_W = 1408
CF_W = 64 + TT + 64


def build_l1(dbg=False, phases="ACD", ntt=NTT, passes=(0, 1)):
    nc = bass.Bass("TRN2", target_bir_lowering=False)
    xT = nc.dram_tensor("xT", [D, S], F32, kind="ExternalInput").ap()
    wsl = nc.dram_tensor("wsl", [D, 2336], F32, kind="ExternalInput").ap()
    prm = nc.dram_tensor("prm", [128, 64], F32, kind="ExternalInput").ap()
    cm = nc.dram_tensor("cm", [128, 768], F32, kind="ExternalInput").ap()
    cbd = nc.dram_tensor("cb", [128, CB_W], BF16, kind="ExternalInput").ap()
    cfd = nc.dram_tensor("cf", [128, CF_W], F32, kind="ExternalInput").ap()
    tab = nc.dram_tensor("tab", [128, 2, S], F32, kind="ExternalInput").ap()
    mix = nc.dram_tensor("mix", [512, S], F32, kind="ExternalOutput").ap()
    sk = "ExternalOutput" if dbg else "Internal"
    SD = {}
    for n in ["q", "k", "v", "za", "R", "A", "B", "K", "V2"]:
        SD[n] = nc.dram_tensor("S_" + n, [256, S], BF16, kind=sk).ap()
    for n in ["gate", "bonus"]:
        SD[n] = nc.dram_tensor("S_" + n, [256, S], F32, kind=sk).ap()
    SD["pc"] = nc.dram_tensor("S_pc", [256, NCH], F32, kind=sk).ap()

    with contextlib.ExitStack() as st:
        P = Prog(nc, st)
        op = P.op
        prm_t = Tl(P, "prm", [128, 64], F32)
        cb = Tl(P, "cb", [128, CB_W], BF16)
        cf = Tl(P, "cf", [128, CF_W], F32)
        cmf = Tl(P, "cmf", [128, 768], F32)
        cmb = Tl(P, "cmb", [128, 768], BF16)
        op("sync", lambda e: e.dma_start(out=prm_t[:], in_=prm), writes=[prm_t], dma=prm_t)
        op("sync", lambda e: e.dma_start(out=cb[:], in_=cbd), writes=[cb], dma=cb)
        op("sync", lambda e: e.dma_start(out=cf[:], in_=cfd), writes=[cf], dma=cf)
        op("sync", lambda e: e.dma_start(out=cmf[:], in_=cm), writes=[cmf], dma=cmf)
        op("vector", lambda e: e.tensor_copy(out=cmb[:], in_=cmf[:]), reads=[cmf], writes=[cmb])
        ident = cb[:, CB_ID:CB_ID + 128]
        perm = cb[:, CB_PERM:CB_PERM + 128]
        bdm = cb[:, CB_BD:CB_BD + 128]
        rmask = cf[:, 64:64 + TT]

        def pcol(i):
            return prm_t[:, i:i + 1]

        with contextlib.ExitStack() as sa:
            WN = 1312

            def mk(name, shape, dt, psum=False):
                tl = Tl.__new__(Tl)
                tl.t = (P.ps if psum else P.sb)(name, shape, dt, sa)
                tl.k = P.tok(name)
                return tl

            Wb = mk("Wb", [128, 16, WN], BF16)
            ws = [mk("ws", [128, WN], F32) for _ in range(2)]
            xs = [mk("xs", [128, 2, TT], F32) for _ in range(2)]
            sq = [mk("sq", [128, 2, TT], BF16) for _ in range(2)]
            xb = [mk("xb", [128, 16, TT], BF16) for _ in range(2)]
            pj = [mk("pj", [128, TT], F32, True) for _ in range(3)]
            ssp = mk("ssp", [128, TT], F32, True)
            aux = [mk("aux", [128, TT], F32, True) for _ in range(3)]
            auxi = [0]

            def nxaux():
                auxi[0] += 1
                return aux[auxi[0] % 3]

            rstd = mk("rstd", [128, TT], F32)
            tss = mk("tss", [128, TT], F32)
            U = {ct: mk("U%d" % ct, [128, TT + 1], F32) for ct in [8, 9, 10, 11, 12, 13, 16, 17, 18]}
            lastc = {ct: mk("lc%d" % ct, [128, 1], F32) for ct in U}
            for ct in U:
                op("gpsimd", lambda e, ct=ct: e.memset(lastc[ct][:], 0.0), writes=[lastc[ct]])
            dtl = mk("dtl", [128, TT], F32)
            LA = mk("LA", [128, TT], BF16)
            GL0 = mk("GL0", [128, TT], BF16)
            GL1 = mk("GL1", [128, TT], BF16)
            lw = [mk("lw", [128, TT], F32) for _ in range(2)]
            aa = [mk("aa", [128, TT], F32) for _ in range(2)]
            gg = [mk("gg", [128, TT], F32) for _ in range(2)]
            zr = mk("zr", [128, TT], F32)
            kkr = mk("kkr", [128, TT], F32)
            sqb = mk("sqb", [128, TT], BF16)
            sn = mk("sn", [128, TT], F32)
            apt = mk("apt", [128, TT], F32)
            bpt = mk("bpt", [128, TT], F32)
            kpt = mk("kpt", [128, TT], F32)
            rkr = mk("rkr", [128, TT], BF16)
            cum = mk("cum", [128, TT], F32)
            cumx = mk("cumx", [128, TT], F32)
            eP = mk("eP", [128, TT], F32)
            eN = mk("eN", [128, TT], F32)
            ePx = mk("ePx", [128, TT], F32)
            pcs = mk("pcs", [128, 2, NCH], F32)
            stg = {n: mk("st_" + n, [128, TT], BF16) for n in ["R", "A", "B", "K", "V2", "q", "k", "v", "za"]}
            stg["gate"] = mk("st_gate", [128, TT], F32)
            stg["bonus"] = mk("st_bonus", [128, TT], F32)
            cst = mk("cst", [128, 2, TT], F32)
            qf_s = [mk("qf", [128, TT], F32) for _ in range(2)]
            qb_s = [mk("qb", [128, TT], BF16) for _ in range(2)]
            t1_s = [mk("t1", [128, TT], F32) for _ in range(2)]
            t2_s = [mk("t2", [128, TT], F32) for _ in range(2)]

            def store(name, c2, tt):
                dst = SD[name][c2 * 128:(c2 + 1) * 128, tt * TT:(tt + 1) * TT]
                op("sync", lambda e: e.dma_start(out=dst, in_=stg[name][:]), reads=[stg[name]],
                   writes=[P.dtok((name, c2, tt))], dma=stg[name])

            xTv = xT.rearrange("(kc p) t -> p kc t", p=128)
            xcnt = [0]
            gtile = [0]
            curprep = [None]
            rstd_s = [rstd, mk("rstd2", [128, TT], F32)]

            def prep_x(tt, gt):
                tsl = slice(tt * TT, (tt + 1) * TT)
                xbb = xb[gt % 2]
                for j in range(8):
                    xi = xcnt[0] % 2
                    xcnt[0] += 1
                    op("sync", lambda e, xi=xi, j=j: e.dma_start(out=xs[xi][:], in_=xTv[:, 2 * j:2 * j + 2, tsl]), writes=[xs[xi]], dma=xs[xi])
                    op("vector", lambda e, xi=xi, j=j: e.tensor_copy(out=xbb[:, 2 * j:2 * j + 2, :], in_=xs[xi][:]), reads=[xs[xi]], writes=[xbb])
                    op("scalar", lambda e, xi=xi: e.activation(out=sq[xi][:], in_=xs[xi][:], func=AF.Square), reads=[xs[xi]], writes=[sq[xi]])

                    def ssmm(e, xi=xi, j=j):
                        for q in range(2):
                            ins = e.matmul(ssp[:], lhsT=cb[:, CB_ONES:CB_ONES + 128], rhs=sq[xi][:, q, :], start=(j == 0 and q == 0), stop=(j == 7 and q == 1))
                        return ins
                    op("tensor", ssmm, reads=[sq[xi], cb], writes=[ssp])
                    if j < 7:
                        yield
                rs = rstd_s[gt % 2]
                op("vector", lambda e: e.tensor_scalar(out=tss[:], in0=ssp[:], scalar1=1.0 / D, scalar2=1e-5, op0=ALU.mult, op1=ALU.add), reads=[ssp], writes=[tss])
                op("scalar", lambda e: e.activation(out=tss[:], in_=tss[:], func=AF.Sqrt), reads=[tss], writes=[tss])
                op("vector", lambda e: e.reciprocal(out=rs[:], in_=tss[:]), reads=[tss], writes=[rs])
                yield

            for pas in passes:
                if pas == 0:
                    c0, ncol = 1024, 1312
                    cts = [16, 17, 18, 8, 10, 12, 14, 9, 11, 13, 15]
                else:
                    c0, ncol = 0, 1024
                    cts = [0, 1, 2, 3, 4, 5, 6, 7]
                for kc in range(16):
                    w = ws[kc % 2]
                    op("sync", lambda e, w=w, kc=kc: e.dma_start(out=w[:, 0:ncol], in_=wsl[kc * 128:(kc + 1) * 128, c0:c0 + ncol]),
                       writes=[w], dma=w)
                    op("vector", lambda e, w=w, kc=kc: e.tensor_scalar(out=Wb[:, kc, 0:ncol], in0=w[:, 0:ncol], scalar1=pcol(kc), scalar2=None, op0=ALU.mult),
                       reads=[w, prm_t], writes=[Wb])

                for tt in range(ntt):
                    tsl = slice(tt * TT, (tt + 1) * TT)
                    gt = gtile[0]
                    gtile[0] += 1
                    xbb = xb[gt % 2]
                    rstd = rstd_s[gt % 2]
                    if curprep[0] is None:
                        curprep[0] = prep_x(tt, gt)
                    for _ in curprep[0]:
                        pass
                    if tt + 1 < ntt:
                        curprep[0] = prep_x(tt + 1, gt + 1)
                    elif pas != passes[-1]:
                        curprep[0] = prep_x(0, gt + 1)
                    else:
                        curprep[0] = iter(())
                    if pas == 1:
                        op("sync", lambda e: e.dma_start(out=cst[:], in_=tab[:, :, tsl]), writes=[cst], dma=cst)

                    pend = []
                    for ci, ct in enumerate(cts):
                        pp = pj[ci % 3]
                        wc0 = ct * 128 - c0
                        wn = 32 if ct == 18 else 128

                        def proj(e, pp=pp, wc0=wc0, wn=wn):
                            for kc in range(16):
                                ins = e.matmul(pp[0:wn, :], lhsT=Wb[:, kc, wc0:wc0 + wn], rhs=xbb[:, kc, :], start=(kc == 0), stop=(kc == 15))
                            return ins
                        op("tensor", proj, reads=[Wb, xbb], writes=[pp])

                        def post_ct(ct, pp, tt, wn):
                            qf, qb, t1, t2 = qf_s[ct % 2], qb_s[ct % 2], t1_s[ct % 2], t2_s[ct % 2]
                            if ct in U:
                                u = U[ct]
                                lc = lastc[ct]
                                mucol = {8: 16, 9: 17, 10: 18, 11: 19, 12: 20, 13: 21, 16: 22, 17: 23, 18: 24}[ct]
                                op("vector", lambda e, u=u, pp=pp, wn=wn: e.tensor_tensor(out=u[0:wn, 1:TT + 1], in0=pp[0:wn, :], in1=rstd[0:wn, :], op=ALU.mult), reads=[pp, rstd], writes=[u])
                                op("gpsimd", lambda e, u=u, lc=lc, wn=wn: e.tensor_copy(out=u[0:wn, 0:1], in_=lc[0:wn, :]), reads=[lc], writes=[u])
                                op("gpsimd", lambda e, u=u, lc=lc, wn=wn: e.tensor_copy(out=lc[0:wn, :], in_=u[0:wn, TT:TT + 1]), reads=[u], writes=[lc])
                                op("gpsimd", lambda e, u=u, wn=wn: e.tensor_tensor(out=dtl[0:wn, :], in0=u[0:wn, 0:TT], in1=u[0:wn, 1:TT + 1], op=ALU.subtract), reads=[u], writes=[dtl])
                                op("vector", lambda e, u=u, wn=wn, mucol=mucol: e.scalar_tensor_tensor(out=u[0:wn, 1:TT + 1], in0=dtl[0:wn, :], scalar=prm_t[0:wn, mucol:mucol + 1], in1=u[0:wn, 1:TT + 1], op0=ALU.mult, op1=ALU.add), reads=[dtl, u, prm_t], writes=[u])

                            if ct == 16:
                                u = U[16]
                                op("scalar", lambda e, u=u: e.activation(out=LA[0:64, :], in_=u[0:64, 1:TT + 1], func=AF.Tanh), reads=[u], writes=[LA])
                                op("scalar", lambda e, u=u: e.activation(out=LA[64:128, :], in_=u[64:128, 1:TT + 1], func=AF.Copy), reads=[u], writes=[LA])
                                for c2 in range(2):
                                    yield
                                    yield
                                    a1 = nxaux()
                                    op("tensor", lambda e, a1=a1, c2=c2: e.matmul(a1[:], lhsT=cmb[0:64, c2 * 128:(c2 + 1) * 128], rhs=LA[0:64, :], start=True, stop=True), reads=[cmb, LA], writes=[a1])
                                    op("scalar", lambda e, a1=a1, c2=c2: e.activation(out=lw[c2][:], in_=a1[:], func=AF.Sigmoid, bias=pcol(25 + c2), scale=1.0), reads=[a1, prm_t], writes=[lw[c2]])
                                    op("gpsimd", lambda e, c2=c2: e.tensor_scalar(out=lw[c2][:], in0=lw[c2][:], scalar1=-DEC, scalar2=None, op0=ALU.mult), reads=[lw[c2]], writes=[lw[c2]])
                                    yield
                                    yield
                                    a2 = nxaux()
                                    op("tensor", lambda e, a2=a2, c2=c2: e.matmul(a2[:], lhsT=cmb[64:128, c2 * 128:(c2 + 1) * 128], rhs=LA[64:128, :], start=True, stop=True), reads=[cmb, LA], writes=[a2])
                                    op("scalar", lambda e, a2=a2, c2=c2: e.activation(out=aa[c2][:], in_=a2[:], func=AF.Sigmoid, bias=pcol(27 + c2), scale=1.0), reads=[a2, prm_t], writes=[aa[c2]])
                            if ct == 17:
                                op("scalar", lambda e: e.activation(out=GL0[:], in_=U[17][:, 1:TT + 1], func=AF.Sigmoid), reads=[U[17]], writes=[GL0])
                            if ct == 18:
                                op("scalar", lambda e: e.activation(out=GL1[0:32, :], in_=U[18][0:32, 1:TT + 1], func=AF.Sigmoid), reads=[U[18]], writes=[GL1])
                                for c2 in range(2):
                                    yield
                                    yield
                                    a1 = nxaux()

                                    def gmm(e, a1=a1, c2=c2):
                                        e.matmul(a1[:], lhsT=cmb[:, 256 + c2 * 128:256 + (c2 + 1) * 128], rhs=GL0[:], start=True, stop=False)
                                        return e.matmul(a1[:], lhsT=cmb[0:32, 512 + c2 * 128:512 + (c2 + 1) * 128], rhs=GL1[0:32, :], start=False, stop=True)
                                    op("tensor", gmm, reads=[cmb, GL0, GL1], writes=[a1])
                                    op("scalar", lambda e, a1=a1, c2=c2: e.activation(out=gg[c2][:], in_=a1[:], func=AF.Copy), reads=[a1], writes=[gg[c2]])
                            if ct in (14, 15):
                                c2 = ct - 14
                                op("vector", lambda e, pp=pp: e.tensor_tensor(out=zr[:], in0=pp[:], in1=rstd[:], op=ALU.mult), reads=[pp, rstd], writes=[zr])
                                op("scalar", lambda e: e.activation(out=zr[:], in_=zr[:], func=AF.Silu), reads=[zr], writes=[zr])
                                r_s = U[8 + c2]
                                k_s = U[10 + c2]
                                v_s = U[12 + c2]
                                RS = lambda t: t[:, 1:TT + 1]
                                op("gpsimd", lambda e, c2=c2: e.tensor_tensor(out=stg["gate"][:], in0=gg[c2][:], in1=zr[:], op=ALU.mult), reads=[gg[c2], zr], writes=[stg["gate"]])
                                store("gate", c2, tt)
                                op("vector", lambda e, c2=c2, k_s=k_s: e.tensor_scalar(out=kkr[:], in0=RS(k_s), scalar1=pcol(29 + c2), scalar2=None, op0=ALU.mult), reads=[k_s, prm_t], writes=[kkr])
                                op("scalar", lambda e: e.activation(out=sqb[:], in_=kkr[:], func=AF.Square), reads=[kkr], writes=[sqb])
                                yield
                                yield
                                a1 = nxaux()
                                op("tensor", lambda e, a1=a1: e.matmul(a1[:], lhsT=bdm, rhs=sqb[:], start=True, stop=True), reads=[cb, sqb], writes=[a1])
                                op("scalar", lambda e, a1=a1: e.activation(out=sn[:], in_=a1[:], func=AF.Sqrt), reads=[a1], writes=[sn])
                                op("vector", lambda e: e.tensor_scalar(out=sn[:], in0=sn[:], scalar1=1e-12, scalar2=None, op0=ALU.max), reads=[sn], writes=[sn])
                                op("vector", lambda e: e.reciprocal(out=sn[:], in_=sn[:]), reads=[sn], writes=[sn])
                                op("vector", lambda e: e.scalar_tensor_tensor(out=apt[:], in0=kkr[:], scalar=-1.0, in1=sn[:], op0=ALU.mult, op1=ALU.mult), reads=[kkr, sn], writes=[apt])
                                op("vector", lambda e, c2=c2: e.scalar_tensor_tensor(out=bpt[:], in0=apt[:], scalar=-1.0, in1=aa[c2][:], op0=ALU.mult, op1=ALU.mult), reads=[apt, aa[c2]], writes=[bpt])
                                op("vector", lambda e, c2=c2: e.tensor_scalar(out=kpt[:], in0=aa[c2][:], scalar1=-1.0, scalar2=pcol(31 + c2), op0=ALU.add, op1=ALU.mult), reads=[aa[c2], prm_t], writes=[kpt])
                                op("vector", lambda e, k_s=k_s: e.scalar_tensor_tensor(out=kpt[:], in0=kpt[:], scalar=1.0, in1=RS(k_s), op0=ALU.add, op1=ALU.mult), reads=[kpt, k_s], writes=[kpt])
                                op("vector", lambda e, c2=c2, r_s=r_s: e.scalar_tensor_tensor(out=rkr[:], in0=RS(r_s), scalar=pcol(33 + c2), in1=kpt[:], op0=ALU.mult, op1=ALU.mult), reads=[r_s, kpt, prm_t], writes=[rkr])
                                yield
                                yield
                                a2 = nxaux()
                                op("tensor", lambda e, a2=a2: e.matmul(a2[:], lhsT=bdm, rhs=rkr[:], start=True, stop=True), reads=[cb, rkr], writes=[a2])
                                op("vector", lambda e, a2=a2, v_s=v_s: e.tensor_tensor(out=stg["bonus"][:], in0=a2[:], in1=RS(v_s), op=ALU.mult), reads=[a2, v_s], writes=[stg["bonus"]])
                                store("bonus", c2, tt)
                                op("vector", lambda e, c2=c2: e.tensor_tensor_scan(out=cum[:], data0=rmask, data1=lw[c2][:], initial=0.0, op0=ALU.mult, op1=ALU.add), reads=[cf, lw[c2]], writes=[cum])
                                op("gpsimd", lambda e, c2=c2: e.tensor_tensor(out=cumx[:], in0=cum[:], in1=lw[c2][:], op=ALU.subtract), reads=[cum, lw[c2]], writes=[cumx])
                                op("scalar", lambda e: e.activation(out=eP[:], in_=cum[:], func=AF.Exp), reads=[cum], writes=[eP])
                                op("scalar", lambda e: e.activation(out=eN[:], in_=cum[:], func=AF.Exp, scale=-1.0), reads=[cum], writes=[eN])
                                op("scalar", lambda e: e.activation(out=ePx[:], in_=cumx[:], func=AF.Exp), reads=[cumx], writes=[ePx])
                                op("gpsimd", lambda e, c2=c2, tt=tt: e.tensor_copy(out=pcs[:, c2, tt * 4:(tt + 1) * 4], in_=eP[:, CH - 1:TT:CH]), reads=[eP], writes=[pcs])
                                op("vector", lambda e, r_s=r_s: e.tensor_tensor(out=stg["R"][:], in0=RS(r_s), in1=eP[:], op=ALU.mult), reads=[r_s, eP], writes=[stg["R"]])
                                store("R", c2, tt)
                                op("gpsimd", lambda e: e.tensor_tensor(out=stg["A"][:], in0=apt[:], in1=ePx[:], op=ALU.mult), reads=[apt, ePx], writes=[stg["A"]])
                                store("A", c2, tt)
                                op("vector", lambda e: e.tensor_tensor(out=stg["B"][:], in0=bpt[:], in1=eN[:], op=ALU.mult), reads=[bpt, eN], writes=[stg["B"]])
                                store("B", c2, tt)
                                op("gpsimd", lambda e: e.tensor_tensor(out=stg["K"][:], in0=kpt[:], in1=eN[:], op=ALU.mult), reads=[kpt, eN], writes=[stg["K"]])
                                store("K", c2, tt)
                                op("scalar", lambda e, v_s=v_s: e.activation(out=stg["V2"][:], in_=RS(v_s), func=AF.Copy), reads=[v_s], writes=[stg["V2"]])
                                store("V2", c2, tt)
                            if ct in (0, 1, 2, 3):
                                nm = "q" if ct < 2 else "k"
                                c2 = ct % 2
                                scl = 0.125 if ct < 2 else 1.0
                                op("vector", lambda e, pp=pp, scl=scl: e.scalar_tensor_tensor(out=qf[:], in0=pp[:], scalar=scl, in1=rstd[:], op0=ALU.mult, op1=ALU.mult), reads=[pp, rstd], writes=[qf])
                                op("scalar", lambda e: e.activation(out=qb[:], in_=qf[:], func=AF.Copy), reads=[qf], writes=[qb])
                                yield
                                yield
                                a1 = nxaux()
                                op("tensor", lambda e, a1=a1: e.matmul(a1[:], lhsT=perm, rhs=qb[:], start=True, stop=True), reads=[cb, qb], writes=[a1])
                                op("gpsimd", lambda e: e.tensor_tensor(out=t1[:], in0=qf[:], in1=cst[:, 0, :], op=ALU.mult), reads=[qf, cst], writes=[t1])
                                op("vector", lambda e, a1=a1: e.tensor_tensor(out=t2[:], in0=a1[:], in1=cst[:, 1, :], op=ALU.mult), reads=[a1, cst], writes=[t2])
                                op("gpsimd", lambda e, nm=nm: e.tensor_tensor(out=stg[nm][:], in0=t1[:], in1=t2[:], op=ALU.add), reads=[t1, t2], writes=[stg[nm]])
                                store(nm, c2, tt)
                            if ct in (4, 5):
                                op("vector", lambda e, pp=pp: e.tensor_tensor(out=stg["v"][:], in0=pp[:], in1=rstd[:], op=ALU.mult), reads=[pp, rstd], writes=[stg["v"]])
                                store("v", ct - 4, tt)
                            if ct in (6, 7):
                                op("vector", lambda e, pp=pp: e.tensor_tensor(out=qf[:], in0=pp[:], in1=rstd[:], op=ALU.mult), reads=[pp, rstd], writes=[qf])
                                op("scalar", lambda e: e.activation(out=stg["za"][:], in_=qf[:], func=AF.Silu), reads=[qf], writes=[stg["za"]])
                                store("za", ct - 6, tt)
                            yield
                        pend.append(post_ct(ct, pp, tt, wn))
                        next(curprep[0], None)
                        for g in list(pend):
                            try:
                                next(g)
                            except StopIteration:
                                pend.remove(g)
                    while pend:
                        for g in list(pend):
                            try:
                                next(g)
                            except StopIteration:
                                pend.remove(g)
                if pas == 0:
                    for c2 in range(2):
                        op("sync", lambda e, c2=c2: e.dma_start(out=SD["pc"][c2 * 128:(c2 + 1) * 128, :], in_=pcs[:, c2, :]), reads=[pcs], writes=[P.dtok(("pc", c2))], dma=pcs)
                        op("sync", None, reads=[P.dtok(("pc", c2))])
            P.barrier()
            P.flush()
        if "C" in phases:
            build_attention(P, nc, SD, mix, cb, cf)
        if "D" in phases:
            build_rwkv(P, nc, SD, mix, cb, cf, prm_t)
        op("sync", None, writes=P.toks)
        P.flush()
    return nc


def _mk(P, sa, name, shape, dt, psum=False):
    tl = Tl.__new__(Tl)
    tl.t = (P.ps if psum else P.sb)(name, shape, dt, sa)
    tl.k = P.tok(name)
    return tl


def build_attention(P, nc, SD, mix, cb, cf):
    op = P.op
    NB = 4
    with contextlib.ExitStack() as sa:
        mk = lambda n, sh, dt, ps=False: _mk(P, sa, n, sh, dt, ps)
        QKVZ = [[mk(n, [64, S], BF16) for n in ("Q", "K", "V", "ZA")] for _ in range(1)]
        accs = [mk("acc", [65, S], F32) for _ in range(2)]
        vtok = mk("vtok", [128, 192, 65], BF16)
        UB = [mk("UB", [128, 512], F32, True) for _ in range(NB)]
        TB = [mk("TB", [128, 1024], BF16, True) for _ in range(2)]
        pB = mk("pB", [128, TT], F32, True)
        PT = [mk("PT", [128, 256], BF16) for _ in range(NB)]
        rec = [mk("rec", [64, TT], F32) for _ in range(2)]
        ost = [mk("ost", [64, TT], F32) for _ in range(2)]
        op("gpsimd", lambda e: e.memset(vtok[:], 1.0), writes=[vtok])
        mbias = cb[:, CB_AM:CB_AM + 256]
        ident = cb[:, CB_ID:CB_ID + 128]
        id64 = cb[0:64, CB_ID:CB_ID + 64]
        onesrow = cf[64:65, 64 + TT:64 + TT + 64]

        def load(h):
            rows = slice(h * 64, (h + 1) * 64)
            c2 = h // 2
            deps = [P.dtok((n, c2, tt)) for n in ["q", "k", "v", "za"] for tt in range(NTT)]
            for t, n in zip(QKVZ[0], ["q", "k", "v", "za"]):
                op("sync", lambda e, t=t, n=n, rows=rows: e.dma_start(out=t[:], in_=SD[n][rows, :]), reads=deps, writes=[t], dma=t)

        u = 0
        for h in range(4):
            rows = slice(h * 64, (h + 1) * 64)
            Q, K, V, ZA = QKVZ[0]
            load(h)
            for acc in accs:
                op("gpsimd", lambda e, acc=acc: e.memset(acc[:], 0.0), writes=[acc])
            blocks = []
            for d in (1, 4, 16):
                nb = (S // d) // 128
                for r in range(d):
                    for j in range(nb):
                        nq = 256 if j < nb - 1 else 128
                        k0 = r + 128 * j * d
                        blocks.append((slice(k0, k0 + 127 * d + 1, d), slice(k0, k0 + (nq - 1) * d + 1, d), nq))
            for g8 in range(len(blocks) // 8):
                tb = TB[g8 % 2]

                def tr8(e, g8=g8, tb=tb, V=V):
                    for q in range(8):
                        ins = e.transpose(tb[:, q * 64:(q + 1) * 64], V[:, blocks[g8 * 8 + q][0]], id64)
                    return ins
                op("tensor", tr8, reads=[V, cb], writes=[tb])
                src = tb[:, 0:512].rearrange("p (b c) -> p b c", c=64)
                if g8 % 2 == 0:
                    op("vector", lambda e, g8=g8, src=src: e.tensor_copy(out=vtok[:, g8 * 8:(g8 + 1) * 8, 0:64], in_=src), reads=[tb], writes=[vtok])
                else:
                    op("scalar", lambda e, g8=g8, src=src: e.activation(out=vtok[:, g8 * 8:(g8 + 1) * 8, 0:64], in_=src, func=AF.Copy), reads=[tb], writes=[vtok])

            def s_part(bi):
                ksl, qsl, nq = blocks[bi]
                i = (u + bi) % NB
                ub = UB[i]

                def smm(e, ub=ub, ksl=ksl, qsl=qsl, nq=nq, K=K, Q=Q):
                    e.matmul(ub[:, 0:nq], lhsT=K[:, ksl], rhs=Q[:, qsl], start=True, stop=False)
                    return e.matmul(ub[:, 0:nq], lhsT=ident, rhs=mbias[:, 0:nq], start=False, stop=True)
                op("tensor", smm, reads=[K, Q, cb], writes=[ub])
                op("scalar", lambda e, ub=ub, i=i, nq=nq: e.activation(out=PT[i][:, 0:nq], in_=ub[:, 0:nq], func=AF.Exp), reads=[ub], writes=[PT[i]])

            def o_part(bi):
                ksl, qsl, nq = blocks[bi]
                i = (u + bi) % NB
                ub = UB[i]
                op("tensor", lambda e, ub=ub, i=i, nq=nq, bi=bi: e.matmul(ub[0:65, 256:256 + nq], lhsT=vtok[:, bi, :], rhs=PT[i][:, 0:nq], start=True, stop=True), reads=[vtok, PT[i]], writes=[ub])
                acc = accs[bi % 2]
                op("vector", lambda e, ub=ub, nq=nq, qsl=qsl, acc=acc: e.tensor_tensor(out=acc[:, qsl], in0=ub[0:65, 256:256 + nq], in1=acc[:, qsl], op=ALU.add), reads=[ub, acc], writes=[acc])

            SK = 2
            nbk = len(blocks)
            for bi in range(nbk + SK):
                if bi < nbk:
                    s_part(bi)
                if bi >= SK:
                    o_part(bi - SK)
            u += nbk
            for tt in range(NTT):
                tsl = slice(tt * TT, (tt + 1) * TT)
                rc, os_ = rec[tt % 2], ost[tt % 2]
                def denmm(e, tsl=tsl):
                    e.matmul(pB[0:64, :], lhsT=onesrow, rhs=accs[0][64:65, tsl], start=True, stop=False)
                    return e.matmul(pB[0:64, :], lhsT=onesrow, rhs=accs[1][64:65, tsl], start=False, stop=True)
                op("tensor", denmm, reads=[cf, accs[0], accs[1]], writes=[pB])
                op("vector", lambda e, rc=rc: e.reciprocal(out=rc[:], in_=pB[0:64, :]), reads=[pB], writes=[rc])
                op("gpsimd", lambda e, os_=os_, tsl=tsl: e.tensor_tensor(out=os_[:], in0=accs[0][0:64, tsl], in1=accs[1][0:64, tsl], op=ALU.add), reads=[accs[0], accs[1]], writes=[os_])
                op("vector", lambda e, rc=rc, os_=os_: e.tensor_tensor(out=rc[:], in0=os_[:], in1=rc[:], op=ALU.mult), reads=[os_, rc], writes=[rc])
                op("gpsimd", lambda e, rc=rc, os_=os_, tsl=tsl, ZA=ZA: e.tensor_tensor(out=os_[:], in0=rc[:], in1=ZA[:, tsl], op=ALU.mult), reads=[rc, ZA], writes=[os_])
                op("sync", lambda e, os_=os_, tsl=tsl, rows=rows: e.dma_start(out=mix[rows, tsl], in_=os_[:]), reads=[os_], writes=[P.dtok(("mixa", h, tt))], dma=os_)
        P.barrier()
        P.flush()


def build_rwkv(P, nc, SD, mix, cb, cf, prm_t):
    op = P.op
    SEG = 512
    NSEG = S // SEG
    CPS = SEG // CH
    PD = 3
    NP = PD + 1
    with contextlib.ExitStack() as sa:
        mk = lambda n, sh, dt, ps=False: _mk(P, sa, n, sh, dt, ps)
        ident = cb[:, CB_ID:CB_ID + 128]
        id64 = cb[0:64, CB_ID:CB_ID + 64]
        mask4 = cb[:, CB_M4:CB_M4 + 512]
        maskts = cb[:, CB_MTS:CB_MTS + 128]
        ones64 = cf[0:64, 0:64]
        pool = [(P.ps("pb", [128, 512], F32, sa), P.tok("PB")) for _ in range(6)]
        b3 = P.ps("b3", [128, 1024], BF16, sa)
        t3 = P.tok("B3")
        pi = [0]

        def nxb():
            pi[0] += 1
            return pool[pi[0] % 6]

        HD = []
        for hh in range(4):
            d = {}
            for n in ["R", "A", "B", "K", "V2"]:
                d[n] = [mk("r" + n, [64, SEG], BF16) for _ in range(2)]
            d["y"] = [mk("ry", [64, SEG], F32) for _ in range(2)]
            for n in ["gate", "bonus"]:
                d[n] = mk("r" + n, [64, SEG], F32)
            d["pc"] = mk("rpc", [64, NCH], F32)
            d["H"] = mk("H", [64, 64], F32)
            d["Hb"] = mk("Hb", [64, 64], BF16)
            d["Ht"] = mk("Ht", [64, 64], F32)
            d["G4"] = [mk("G4", [128, 512], BF16) for _ in range(NP)]
            d["N1"] = [mk("N1", [128, 128], BF16) for _ in range(NP)]
            d["tok3"] = [mk("tok3", [128, 192], BF16) for _ in range(NP)]
            d["Sm"] = [[mk("Sm", [128, 128], BF16) for _ in range(2)] for _ in range(NP)]
            d["NM"] = [[mk("NM", [128, 256], BF16) for _ in range(2)] for _ in range(NP)]
            d["Sf"] = [None] * NP
            d["Zb"] = mk("Zb", [128, 64], BF16)
            d["Ub"] = mk("Ub", [128, 64], BF16)
            for n in ["ysq", "mt", "m2", "var", "yc", "ost"]:
                d[n] = mk("r" + n, [64, TT], F32)
            HD.append(d)

        def pre(d, h, gc):
            seg, c = gc // CPS, gc % CPS
            p = gc % NP
            cs = slice(c * CH, (c + 1) * CH)
            RT, AT, BT, KT, VT = [d[n][seg % 2] for n in ["R", "A", "B", "K", "V2"]]
            G4, N1, tok3 = d["G4"][p], d["N1"][p], d["tok3"][p]
            b0, t0 = nxb()

            def g4(e):
                for bi, (l, r) in enumerate([(BT, AT), (KT, AT), (BT, RT), (KT, RT)]):
                    ins = e.matmul(b0[:, bi * 128:(bi + 1) * 128], lhsT=l[:, cs], rhs=r[:, cs], start=True, stop=True)
                return ins
            op("tensor", g4, reads=[RT, AT, BT, KT], writes=[t0])
            op("vector", lambda e: e.tensor_tensor(out=G4[:], in0=b0[:], in1=mask4, op=ALU.mult), reads=[cb], writes=[G4, t0])
            b1, t1 = nxb()
            op("tensor", lambda e: e.matmul(b1[:, 0:128], lhsT=AT[:, cs], rhs=BT[:, cs], start=True, stop=True), reads=[AT, BT], writes=[t1])
            op("vector", lambda e: e.tensor_tensor(out=N1[:], in0=b1[:, 0:128], in1=maskts, op=ALU.mult), reads=[cb], writes=[N1, t1])
            bq = b3[:, h * 256:(h + 1) * 256]

            def tr(e):
                e.transpose(bq[:, 0:64], VT[:, cs], id64)
                e.transpose(bq[:, 64:128], BT[:, cs], id64)
                return e.transpose(bq[:, 128:192], KT[:, cs], id64)
            op("tensor", tr, reads=[VT, BT, KT, cb], writes=[t3])
            op("scalar", lambda e: e.activation(out=tok3[:], in_=bq[:, 0:192], func=AF.Copy), reads=[], writes=[tok3, t3])
            yield
            Sm = d["Sm"][p][0]
            op("gpsimd", lambda e: e.tensor_tensor(out=Sm[:], in0=G4[:, 0:128], in1=ident, op=ALU.add), reads=[G4, cb], writes=[Sm])
            np_ap, mp_ap = (lambda: N1[:]), (lambda: G4[:, 0:128])
            np_tok, mp_tok = N1, G4
            si = 0
            prevNM = None
            for lvl in range(7):
                bs, ts_ = nxb()
                NM = d["NM"][p][lvl % 2] if lvl < 6 else None
                So = d["Sm"][p][si]
                Sn = d["Sm"][p][1 - si]

                def lvl_mm(e, np_ap=np_ap, mp_ap=mp_ap, bs=bs, lvl=lvl, prevNM=prevNM, So=So):
                    ins = None
                    if lvl < 6:
                        e.matmul(bs[:, 0:128], lhsT=mp_ap(), rhs=np_ap(), start=True, stop=True)
                        ins = e.matmul(bs[:, 128:256], lhsT=np_ap(), rhs=mp_ap(), start=True, stop=True)
                    if lvl > 0:
                        ins = e.matmul(bs[:, 256:384], lhsT=prevNM[:, 0:128], rhs=So[:], start=True, stop=True)
                    return ins
                rd = [mp_tok, np_tok] + ([prevNM, So] if lvl > 0 else [])
                op("tensor", lvl_mm, reads=rd, writes=[ts_])
                if lvl < 6:
                    op("scalar", lambda e, NM=NM, bs=bs: e.activation(out=NM[:], in_=bs[:, 0:256], func=AF.Copy), reads=[], writes=[NM, ts_])
                if lvl > 0:
                    op("vector", lambda e, So=So, Sn=Sn, bs=bs: e.tensor_tensor(out=Sn[:], in0=bs[:, 256:384], in1=So[:], op=ALU.add), reads=[So], writes=[Sn, ts_])
                    si = 1 - si
                if lvl < 6:
                    np_ap, mp_ap = (lambda NM=NM: NM[:, 0:128]), (lambda NM=NM: NM[:, 128:256])
                    np_tok = mp_tok = NM
                    prevNM = NM
                yield
            d["Sf"][p] = d["Sm"][p][si]

        def chain(d, h, gc):
            seg, c = gc // CPS, gc % CPS
            p = gc % NP
            cs = slice(c * CH, (c + 1) * CH)
            RT, AT = d["R"][seg % 2], d["A"][seg % 2]
            G4, tok3, Sf = d["G4"][p], d["tok3"][p], d["Sf"][p]
            y = d["y"][seg % 2]
            Hb, H, Ht = d["Hb"], d["H"], d["Ht"]
            Zb, Ub = d["Zb"], d["Ub"]
            bz, tz = nxb()

            def zmm(e):
                e.matmul(bz[:, 0:64], lhsT=AT[:, cs], rhs=Hb[:], start=True, stop=False)
                return e.matmul(bz[:, 0:64], lhsT=G4[:, 128:256], rhs=tok3[:, 0:64], start=False, stop=True)
            op("tensor", zmm, reads=[AT, Hb, G4, tok3], writes=[tz])
            op("scalar", lambda e: e.activation(out=Zb[:], in_=bz[:, 0:64], func=AF.Copy), reads=[], writes=[Zb, tz])
            yield
            bu, tu = nxb()
            op("tensor", lambda e: e.matmul(bu[:, 0:64], lhsT=Sf[:], rhs=Zb[:], start=True, stop=True), reads=[Sf, Zb], writes=[tu])
            op("vector", lambda e: e.tensor_copy(out=Ub[:], in_=bu[:, 0:64]), reads=[], writes=[Ub, tu])
            yield
            by, ty = nxb()
            bh, th = nxb()

            def yhmm(e):
                e.matmul(bh[0:64, 0:64], lhsT=tok3[:, 64:128], rhs=Ub[:], start=True, stop=False)
                e.matmul(bh[0:64, 0:64], lhsT=tok3[:, 128:192], rhs=tok3[:, 0:64], start=False, stop=True)
                e.matmul(by[0:64, 0:128], lhsT=Hb[:], rhs=RT[:, cs], start=True, stop=False)
                e.matmul(by[0:64, 0:128], lhsT=Ub[:], rhs=G4[:, 256:384], start=False, stop=False)
                return e.matmul(by[0:64, 0:128], lhsT=tok3[:, 0:64], rhs=G4[:, 384:512], start=False, stop=True)
            op("tensor", yhmm, reads=[Hb, RT, Ub, G4, tok3], writes=[ty, th])
            op("vector", lambda e: e.tensor_tensor(out=Ht[:], in0=bh[0:64, 0:64], in1=H[:], op=ALU.add), reads=[H], writes=[Ht, th])
            op("scalar", lambda e: e.activation(out=y[:, cs], in_=by[0:64, 0:128], func=AF.Copy), reads=[], writes=[y, ty])
            op("vector", lambda e: e.tensor_scalar(out=H[:], in0=Ht[:], scalar1=d["pc"][:, gc:gc + 1], scalar2=None, op0=ALU.mult), reads=[Ht, d["pc"]], writes=[H])
            op("vector", lambda e: e.tensor_copy(out=Hb[:], in_=H[:]), reads=[H], writes=[Hb])
            yield

        def post(d, h, seg):
            y = d["y"][seg % 2]
            for q in range(SEG // TT):
                ts = slice(q * TT, (q + 1) * TT)
                gts = slice(seg * SEG + q * TT, seg * SEG + (q + 1) * TT)
                bm, tm = nxb()
                bq, tq = nxb()
                op("scalar", lambda e, ts=ts: e.activation(out=d["ysq"][:], in_=y[:, ts], func=AF.Square), reads=[y], writes=[d["ysq"]])
                op("tensor", lambda e, ts=ts, bm=bm: e.matmul(bm[0:64, :], lhsT=ones64, rhs=y[:, ts], start=True, stop=True), reads=[cf, y], writes=[tm])
                op("scalar", lambda e, bm=bm: e.activation(out=d["mt"][:], in_=bm[0:64, :], func=AF.Copy), reads=[], writes=[d["mt"], tm])
                op("tensor", lambda e, bq=bq: e.matmul(bq[0:64, :], lhsT=ones64, rhs=d["ysq"][:], start=True, stop=True), reads=[cf, d["ysq"]], writes=[tq])
                op("gpsimd", lambda e: e.tensor_tensor(out=d["m2"][:], in0=d["mt"][:], in1=d["mt"][:], op=ALU.mult), reads=[d["mt"]], writes=[d["m2"]])
                op("vector", lambda e, bq=bq: e.tensor_tensor(out=d["var"][:], in0=bq[0:64, :], in1=d["m2"][:], op=ALU.subtract), reads=[d["m2"]], writes=[d["var"], tq])
                yield
                op("gpsimd", lambda e: e.tensor_scalar(out=d["var"][:], in0=d["var"][:], scalar1=64e-5, scalar2=None, op0=ALU.add), reads=[d["var"]], writes=[d["var"]])
                op("scalar", lambda e: e.activation(out=d["var"][:], in_=d["var"][:], func=AF.Sqrt), reads=[d["var"]], writes=[d["var"]])
                op("vector", lambda e: e.reciprocal(out=d["var"][:], in_=d["var"][:]), reads=[d["var"]], writes=[d["var"]])
                op("gpsimd", lambda e, ts=ts: e.tensor_tensor(out=d["yc"][:], in0=y[:, ts], in1=d["mt"][:], op=ALU.subtract), reads=[y, d["mt"]], writes=[d["yc"]])
                yield
                op("gpsimd", lambda e: e.tensor_tensor(out=d["yc"][:], in0=d["yc"][:], in1=d["var"][:], op=ALU.mult), reads=[d["yc"], d["var"]], writes=[d["yc"]])
                op("vector", lambda e: e.tensor_scalar(out=d["yc"][:], in0=d["yc"][:], scalar1=prm_t[0:64, 40 + h:41 + h], scalar2=prm_t[0:64, 44 + h:45 + h], op0=ALU.mult, op1=ALU.add), reads=[d["yc"], prm_t], writes=[d["yc"]])
                op("gpsimd", lambda e, ts=ts: e.tensor_tensor(out=d["yc"][:], in0=d["yc"][:], in1=d["bonus"][:, ts], op=ALU.add), reads=[d["yc"], d["bonus"]], writes=[d["yc"]])
                op("gpsimd", lambda e, ts=ts: e.tensor_tensor(out=d["ost"][:], in0=d["yc"][:], in1=d["gate"][:, ts], op=ALU.mult), reads=[d["yc"], d["gate"]], writes=[d["ost"]])
                op("sync", lambda e, gts=gts: e.dma_start(out=mix[256 + h * 64:256 + (h + 1) * 64, gts], in_=d["ost"][:]), reads=[d["ost"]], writes=[P.dtok(("mixr", h, seg, q))], dma=d["ost"])
                yield

        def loads(seg, names, idx):
            for h in range(4):
                d = HD[h]
                c2 = h // 2
                for n in names:
                    dst = d[n][idx] if idx is not None else d[n]
                    deps = [P.dtok((n, c2, tt)) for tt in range(seg * (SEG // TT), (seg + 1) * (SEG // TT))]
                    op("sync", lambda e, dst=dst, n=n, h=h, seg=seg: e.dma_start(out=dst[:], in_=SD[n][h * 64:(h + 1) * 64, seg * SEG:(seg + 1) * SEG]), reads=deps, writes=[dst], dma=dst)

        for h in range(4):
            d = HD[h]
            op("gpsimd", lambda e, d=d: e.memset(d["H"][:], 0.0), writes=[d["H"]])
            op("gpsimd", lambda e, d=d: e.memset(d["Hb"][:], 0.0), writes=[d["Hb"]])
            op("sync", lambda e, d=d, h=h: e.dma_start(out=d["pc"][:], in_=SD["pc"][h * 64:(h + 1) * 64, :]), reads=[P.dtok(("pc", h // 2))], writes=[d["pc"]], dma=d["pc"])
        loads(0, ["R", "A", "B", "K", "V2"], 0)

        def run(gens):
            alive = True
            while alive:
                alive = False
                for g in gens:
                    try:
                        next(g)
                        alive = True
                    except StopIteration:
                        pass

        run([pre(HD[h], h, 0) for h in range(4)])
        active = []
        for gc in range(1, PD):
            active.append([gc, [pre(HD[h], h, gc) for h in range(4)]])
        posts = []
        for gc in range(NCH):
            seg, c = gc // CPS, gc % CPS
            if c == 0 and seg + 1 < NSEG:
                loads(seg + 1, ["R", "A", "B", "K", "V2"], (seg + 1) % 2)
            if gc + PD < NCH:
                active.append([gc + PD, [pre(HD[h], h, gc + PD) for h in range(4)]])
            must = [chain(HD[h], h, gc) for h in range(4)]
            while True:
                pend = False
                for g in list(must):
                    try:
                        next(g)
                        pend = True
                    except StopIteration:
                        must.remove(g)
                for ent in active:
                    for g in list(ent[1]):
                        try:
                            next(g)
                            if ent[0] == gc + 1:
                                pend = True
                        except StopIteration:
                            ent[1].remove(g)
                for g in list(posts):
                    try:
                        next(g)
                    except StopIteration:
                        posts.remove(g)
                active = [ent for ent in active if ent[1]]
                if not pend:
                    break
            if c == CPS - 1:
                loads(seg, ["gate", "bonus"], None)
                posts += [post(HD[h], h, seg) for h in range(4)]
        run(posts)
        P.barrier()
        P.flush()


def build_l2():
    nc = bass.Bass("TRN2", target_bir_lowering=False)
    NT = 2048
    mixT = nc.dram_tensor("mixT", [D, NT], F32, kind="ExternalInput").ap()
    wout = nc.dram_tensor("wout", [D, D], F32, kind="ExternalInput").ap()
    xin = nc.dram_tensor("xin", [NT, D], F32, kind="ExternalInput").ap()
    gfin = nc.dram_tensor("gfin", [128, D], F32, kind="ExternalInput").ap()
    out = nc.dram_tensor("out", [NT, D], F32, kind="ExternalOutput").ap()
    with contextlib.ExitStack() as st:
        P = Prog(nc, st)
        op = P.op
        mk = lambda n, sh, dt, ps=False: _mk(P, st, n, sh, dt, ps)
        Wo = mk("Wo", [128, 16, D], BF16)
        wst = [mk("wst", [128, D], F32) for _ in range(2)]
        gf = mk("gf", [128, D], F32)
        op("sync", lambda e: e.dma_start(out=gf[:], in_=gfin), writes=[gf], dma=gf)
        for kc in range(16):
            w = wst[kc % 2]
            op("sync", lambda e, w=w, kc=kc: e.dma_start(out=w[:], in_=wout[kc * 128:(kc + 1) * 128, :]), writes=[w], dma=w)
            op("vector" if kc % 2 == 0 else "gpsimd", lambda e, w=w, kc=kc: e.tensor_copy(out=Wo[:, kc, :], in_=w[:]), reads=[w], writes=[Wo])
        mf = [mk("mf", [128, 16, 128], F32) for _ in range(2)]
        mb = [mk("mb", [128, 16, 128], BF16) for _ in range(2)]
        xt = [mk("xt", [128, D], F32) for _ in range(2)]
        ys = [mk("ys", [128, D], F32) for _ in range(2)]
        junk = mk("junk", [128, D], F32)
        ss = mk("ss", [128, 1], F32)
        pp = [mk("pp", [128, 512], F32, True) for _ in range(4)]
        mv = mixT.rearrange("(kc p) t -> p kc t", p=128)
        for t in range(NT // 128):
            i = t % 2
            tsl = slice(t * 128, (t + 1) * 128)
            op("sync", lambda e, i=i, tsl=tsl: e.dma_start(out=mf[i][:], in_=mv[:, :, tsl]), writes=[mf[i]], dma=mf[i])
            op("sync", lambda e, i=i, tsl=tsl: e.dma_start(out=xt[i][:], in_=xin[tsl, :]), writes=[xt[i]], dma=xt[i])
            op("gpsimd", lambda e, i=i: e.tensor_copy(out=mb[i][:], in_=mf[i][:]), reads=[mf[i]], writes=[mb[i]])
            for cg in range(4):
                def mm(e, i=i, cg=cg):
                    for kc in range(16):
                        ins = e.matmul(pp[cg][:], lhsT=mb[i][:, kc, :], rhs=Wo[:, kc, cg * 512:(cg + 1) * 512], start=(kc == 0), stop=(kc == 15))
                    return ins
                op("tensor", mm, reads=[mb[i], Wo], writes=[pp[cg]])
                op("vector", lambda e, i=i, cg=cg: e.tensor_tensor(out=ys[i][:, cg * 512:(cg + 1) * 512], in0=pp[cg][:], in1=xt[i][:, cg * 512:(cg + 1) * 512], op=ALU.add), reads=[pp[cg], xt[i]], writes=[ys[i]])
            op("scalar", lambda e, i=i: e.activation(out=junk[:], in_=ys[i][:], func=AF.Square, accum_out=ss[:]), reads=[ys[i]], writes=[junk, ss])
            op("vector", lambda e: e.tensor_scalar(out=ss[:], in0=ss[:], scalar1=1.0 / D, scalar2=1e-5, op0=ALU.mult, op1=ALU.add), reads=[ss], writes=[ss])
            op("scalar", lambda e: e.activation(out=ss[:], in_=ss[:], func=AF.Sqrt), reads=[ss], writes=[ss])
            op("vector", lambda e: e.reciprocal(out=ss[:], in_=ss[:]), reads=[ss], writes=[ss])
            op("vector", lambda e, i=i: e.scalar_tensor_tensor(out=ys[i][:], in0=ys[i][:], scalar=ss[:, 0:1], in1=gf[:], op0=ALU.mult, op1=ALU.mult), reads=[ys[i], ss, gf], writes=[ys[i]])
            op("sync", lambda e, i=i, tsl=tsl: e.dma_start(out=out[tsl, :], in_=ys[i][:]), reads=[ys[i]], writes=[P.dtok(("out", t))], dma=ys[i])
        op("sync", None, writes=P.toks)
        P.flush()
    return nc


def _group_cols(g):
    hs = np.arange(g * 256, (g + 1) * 256)
    base = 4 * A
    cols = [hs, A + hs, 2 * A + hs, 3 * A + hs,
            base + hs, base + R + hs, base + 2 * R + hs, base + SHIFT + hs,
            base + 3 * R + np.arange(128), base + 3 * R + 128 + np.arange(160)]
    return np.concatenate(cols)


def _prep_l1(inp, b, g, consts):
    cb, cf, tab = consts
    f = lambda n: np.asarray(inp[n], dtype=np.float32)
    hs = slice(g * 256, (g + 1) * 256)
    xT = np.ascontiguousarray(f("x")[b].T)
    wsl = np.ascontiguousarray(f("w_in")[0][:, _group_cols(g)])
    prm = np.zeros((128, 64), np.float32)
    prm[:, 0:16] = f("norm_g")[0].reshape(16, 128).T
    mu = f("shift_mu")[0]
    for i, off in enumerate([0, R, 2 * R]):
        prm[:, 16 + 2 * i:18 + 2 * i] = mu[off + g * 256: off + (g + 1) * 256].reshape(2, 128).T
    prm[:, 22] = mu[3 * R:3 * R + 128]
    prm[:, 23] = mu[3 * R + 128:3 * R + 256]
    prm[0:32, 24] = mu[3 * R + 256:3 * R + 288]
    for i, n in enumerate(["w0", "a0", "k_k", "k_a"]):
        prm[:, 25 + 2 * i:27 + 2 * i] = f(n)[0][hs].reshape(2, 128).T
    prm[:, 33:35] = f("r_k")[0].reshape(-1)[hs].reshape(2, 128).T
    lg = f("lnx_g")[0][hs].reshape(4, 64)
    lb = f("lnx_b")[0][hs].reshape(4, 64)
    prm[0:64, 40:44] = lg.T
    prm[0:64, 44:48] = lb.T
    cm = np.zeros((128, 768), np.float32)
    cm[0:64, 0:256] = f("w2")[0][:, hs]
    cm[64:128, 0:256] = f("a2")[0][:, hs]
    cm[:, 256:512] = f("g2")[0][0:128, hs]
    cm[0:32, 512:768] = f("g2")[0][128:160, hs]
    return {"xT": xT, "wsl": wsl, "prm": prm, "cm": cm, "cb": cb, "cf": cf, "tab": tab}


_CACHE = {}


def kernel(**inputs):
    consts = _consts()
    if "l1" not in _CACHE:
        _CACHE["l1"] = build_l1()
        _CACHE["l2"] = build_l2()
    in1 = [_prep_l1(inputs, c // 4, c % 4, consts) for c in range(8)]
    r1 = run_bass_kernel_spmd(_CACHE["l1"], in1, core_ids=list(range(8))).results
    x = np.asarray(inputs["x"], dtype=np.float32)
    wout = np.asarray(inputs["w_out"], dtype=np.float32)[0]
    gfin = np.ascontiguousarray(np.broadcast_to(np.asarray(inputs["final_g"], dtype=np.float32)[None, :], (128, D)))
    in2 = []
    for c in range(8):
        b, qd = c // 4, c % 4
        ts = slice(qd * 2048, (qd + 1) * 2048)
        mt = np.empty((D, 2048), np.float32)
        for g in range(4):
            m = np.asarray(r1[b * 4 + g]["mix"])
            mt[g * 256:(g + 1) * 256] = m[0:256, ts]
            mt[A + g * 256:A + (g + 1) * 256] = m[256:512, ts]
        in2.append({"mixT": mt, "wout": wout, "xin": np.ascontiguousarray(x[b, ts]), "gfin": gfin})
    r2 = run_bass_kernel_spmd(_CACHE["l2"], in2, core_ids=list(range(8))).results
    out = np.empty((2, S, D), np.float32)
    for c in range(8):
        b, qd = c // 4, c % 4
        out[b, qd * 2048:(qd + 1) * 2048] = np.asarray(r2[c]["out"])
    return out
```

```python
import contextlib
import types
import numpy as np
import ml_dtypes
import concourse.bass as bass
import concourse.mybir as mybir
from concourse.bass_utils import run_bass_kernel_spmd

F32 = mybir.dt.float32
BF16 = mybir.dt.bfloat16
AF = mybir.ActivationFunctionType
ALU = mybir.AluOpType
NPBF = ml_dtypes.bfloat16

ENGS = ["tensor", "vector", "scalar", "gpsimd", "sync"]

D = 2048
S = 8192
A = 1024
R = 1024
SHIFT = 3 * R + 64 + 64 + 160
TT = 512
NTT = S // TT
CH = 128
NCH = S // CH
DEC = 0.6065306597126334


def _snap(fn):
    if fn is None or fn.__closure__ is None:
        return fn
    cells = []
    for c in fn.__closure__:
        try:
            cells.append(types.CellType(c.cell_contents))
        except ValueError:
            cells.append(c)
    g = types.FunctionType(fn.__code__, fn.__globals__, fn.__name__, fn.__defaults__, tuple(cells))
    g.__kwdefaults__ = fn.__kwdefaults__
    return g


class Tok:
    __slots__ = ("name", "writes", "reads", "sem", "cnt")

    def __init__(self, name=""):
        self.name = name
        self.writes = []
        self.reads = []
        self.sem = None
        self.cnt = 0


class Tl:
    def __init__(self, P, name, shape, dt, psum=False):
        self.t = (P.ps if psum else P.sb)(name, shape, dt)
        self.k = P.tok(name)

    def __getitem__(self, i):
        return self.t[i]


def _k(t):
    return t.k if hasattr(t, "k") else t


class Prog:
    def __init__(self, nc, stack):
        self.nc = nc
        self.stack = stack
        self.ops = {e: [] for e in ENGS}
        self.ecount = {e: 0 for e in ENGS}
        self.waited = {e: {} for e in ENGS}
        self.esem = {e: stack.enter_context(nc.semaphore("s_" + e)) for e in ENGS}
        self.semobj = {("E", e): self.esem[e] for e in ENGS}
        self.nsem = len(ENGS)
        self.toks = []
        self.dtoks = {}
        self.nm = 0

    def sb(self, name, shape, dt, stack=None):
        self.nm += 1
        return (stack or self.stack).enter_context(self.nc.sbuf_tensor("%s_%d" % (name, self.nm), list(shape), dt))

    def ps(self, name, shape, dt=F32, stack=None):
        self.nm += 1
        return (stack or self.stack).enter_context(self.nc.psum_tensor("%s_%d" % (name, self.nm), list(shape), dt))

    def tok(self, name=""):
        t = Tok(name)
        self.toks.append(t)
        return t

    def dtok(self, key):
        if key not in self.dtoks:
            self.dtoks[key] = self.tok(str(key))
        return self.dtoks[key]

    def dsem(self, tok):
        if tok.sem is None:
            tok.sem = self.stack.enter_context(self.nc.semaphore("d%d" % self.nsem))
            self.nsem += 1
            self.semobj[("D", id(tok))] = tok.sem
        return tok.sem

    def op(self, eng, fn, reads=(), writes=(), dma=None):
        fn = _snap(fn)
        reads = [_k(t) for t in reads]
        writes = [_k(t) for t in writes]
        need = []
        for t in reads:
            need += t.writes
        for t in writes:
            need += t.writes
            need += t.reads
        waits = {}
        for (key, val, src) in need:
            if self.waited[eng].get(key, 0) >= val:
                continue
            if waits.get(key, 0) < val:
                waits[key] = val
        for k, v in waits.items():
            self.waited[eng][k] = v
        ev = None
        if fn is not None:
            if dma is None:
                self.ecount[eng] += 1
                ev = (("E", eng), self.ecount[eng], eng)
            else:
                dma = _k(dma)
                self.dsem(dma)
                dma.cnt += 16
                ev = (("D", id(dma)), dma.cnt, None)
            for t in reads:
                t.reads.append(ev)
            for t in writes:
                t.writes = [ev]
                t.reads = []
        self.ops[eng].append((list(waits.items()), fn, ev))

    def barrier(self):
        for e in ENGS:
            self.op(e, None, writes=self.toks)
        for t in self.toks:
            t.writes = []
            t.reads = []

    def emit(self, eng_name, e):
        for waits, fn, ev in self.ops[eng_name]:
            for key, val in waits:
                e.wait_ge(self.semobj[key], val)
            if fn is None:
                continue
            ins = fn(e)
            key, val, src = ev
            ins.then_inc(self.semobj[key], 16 if key[0] == "D" else 1)
        self.ops[eng_name] = []

    def flush(self):
        with self.nc.Block() as block:
            @block.tensor
            def _(e):
                self.emit("tensor", e)

            @block.vector
            def _(e):
                self.emit("vector", e)

            @block.scalar
            def _(e):
                self.emit("scalar", e)

            @block.gpsimd
            def _(e):
                self.emit("gpsimd", e)

            @block.sync
            def _(e):
                self.emit("sync", e)


def _consts():
    ident = np.eye(128, dtype=np.float32)
    perm = np.zeros((128, 128), np.float32)
    for m in range(128):
        p = m + 32 if (m % 64) < 32 else m - 32
        perm[p, m] = 1.0
    bd = np.zeros((128, 128), np.float32)
    bd[:64, :64] = 1.0
    bd[64:, 64:] = 1.0
    ki = np.arange(128)[:, None]
    qi = np.arange(128)[None, :]
    amask = (np.concatenate([(ki <= qi), (qi <= ki)], axis=1).astype(np.float32) - 1.0) * 30000.0
    strict = (ki < qi).astype(np.float32)
    incl = (ki <= qi).astype(np.float32)
    mask4 = np.concatenate([strict, strict, incl, incl], axis=1)
    maskts = (qi < ki).astype(np.float32)
    cb = np.concatenate([ident, perm, bd, amask, mask4, maskts, np.ones((128, 128), np.float32)], axis=1).astype(NPBF)
    ones64 = np.full((128, 64), 1.0 / 64, np.float32)
    rmask = np.ones((128, TT), np.float32)
    rmask[:, ::CH] = 0.0
    onesf = np.ones((128, 64), np.float32)
    cf = np.concatenate([ones64, rmask, onesf], axis=1)
    inv_freq = (10000.0 ** (-np.arange(0, 64, 2, dtype=np.float32) / 64)).astype(np.float32)
    pos = np.arange(S, dtype=np.float32)
    ang = (pos[:, None] * inv_freq[None, :]).astype(np.float32)
    cos = np.cos(ang).astype(np.float32).T
    sin = np.sin(ang).astype(np.float32).T
    cosf = np.tile(cos, (4, 1))
    sinf = np.concatenate([-sin, sin, -sin, sin], axis=0)
    tab = np.stack([cosf, sinf], axis=1).astype(np.float32)
    return cb, cf, tab


CB_ID, CB_PERM, CB_BD, CB_AM, CB_M4, CB_MTS = 0, 128, 256, 384, 640, 1152
CB_ONES = 1280
CB_W = 1408
CF_W = 64 + TT + 64


def build_l1(dbg=False, phases="ACD", ntt=NTT, passes=(0, 1)):
    nc = bass.Bass("TRN2", target_bir_lowering=False)
    xT = nc.dram_tensor("xT", [D, S], F32, kind="ExternalInput").ap()
    wsl = nc.dram_tensor("wsl", [D, 2336], F32, kind="ExternalInput").ap()
    prm = nc.dram_tensor("prm", [128, 64], F32, kind="ExternalInput").ap()
    cm = nc.dram_tensor("cm", [128, 768], F32, kind="ExternalInput").ap()
    cbd = nc.dram_tensor("cb", [128, CB_W], BF16, kind="ExternalInput").ap()
    cfd = nc.dram_tensor("cf", [128, CF_W], F32, kind="ExternalInput").ap()
    tab = nc.dram_tensor("tab", [128, 2, S], F32, kind="ExternalInput").ap()
    mix = nc.dram_tensor("mix", [512, S], F32, kind="ExternalOutput").ap()
    sk = "ExternalOutput" if dbg else "Internal"
    SD = {}
    for n in ["q", "k", "v", "za", "R", "A", "B", "K", "V2"]:
        SD[n] = nc.dram_tensor("S_" + n, [256, S], BF16, kind=sk).ap()
    for n in ["gate", "bonus"]:
        SD[n] = nc.dram_tensor("S_" + n, [256, S], F32, kind=sk).ap()
    SD["pc"] = nc.dram_tensor("S_pc", [256, NCH], F32, kind=sk).ap()

    with contextlib.ExitStack() as st:
        P = Prog(nc, st)
        op = P.op
        prm_t = Tl(P, "prm", [128, 64], F32)
        cb = Tl(P, "cb", [128, CB_W], BF16)
        cf = Tl(P, "cf", [128, CF_W], F32)
        cmf = Tl(P, "cmf", [128, 768], F32)
        cmb = Tl(P, "cmb", [128, 768], BF16)
        op("sync", lambda e: e.dma_start(out=prm_t[:], in_=prm), writes=[prm_t], dma=prm_t)
        op("sync", lambda e: e.dma_start(out=cb[:], in_=cbd), writes=[cb], dma=cb)
        op("sync", lambda e: e.dma_start(out=cf[:], in_=cfd), writes=[cf], dma=cf)
        op("sync", lambda e: e.dma_start(out=cmf[:], in_=cm), writes=[cmf], dma=cmf)
        op("vector", lambda e: e.tensor_copy(out=cmb[:], in_=cmf[:]), reads=[cmf], writes=[cmb])
        ident = cb[:, CB_ID:CB_ID + 128]
        perm = cb[:, CB_PERM:CB_PERM + 128]
        bdm = cb[:, CB_BD:CB_BD + 128]
        rmask = cf[:, 64:64 + TT]

        def pcol(i):
            return prm_t[:, i:i + 1]

        with contextlib.ExitStack() as sa:
            WN = 1312

            def mk(name, shape, dt, psum=False):
                tl = Tl.__new__(Tl)
                tl.t = (P.ps if psum else P.sb)(name, shape, dt, sa)
                tl.k = P.tok(name)
                return tl

            Wb = mk("Wb", [128, 16, WN], BF16)
            ws = [mk("ws", [128, WN], F32) for _ in range(2)]
            xs = [mk("xs", [128, 2, TT], F32) for _ in range(2)]
            sq = [mk("sq", [128, 2, TT], BF16) for _ in range(2)]
            xb = [mk("xb", [128, 16, TT], BF16) for _ in range(2)]
            pj = [mk("pj", [128, TT], F32, True) for _ in range(3)]
            ssp = mk("ssp", [128, TT], F32, True)
            aux = [mk("aux", [128, TT], F32, True) for _ in range(3)]
            auxi = [0]

            def nxaux():
                auxi[0] += 1
                return aux[auxi[0] % 3]

            rstd = mk("rstd", [128, TT], F32)
            tss = mk("tss", [128, TT], F32)
            U = {ct: mk("U%d" % ct, [128, TT + 1], F32) for ct in [8, 9, 10, 11, 12, 13, 16, 17, 18]}
            lastc = {ct: mk("lc%d" % ct, [128, 1], F32) for ct in U}
            for ct in U:
                op("gpsimd", lambda e, ct=ct: e.memset(lastc[ct][:], 0.0), writes=[lastc[ct]])
            dtl = mk("dtl", [128, TT], F32)
            LA = mk("LA", [128, TT], BF16)
            GL0 = mk("GL0", [128, TT], BF16)
            GL1 = mk("GL1", [128, TT], BF16)
            lw = [mk("lw", [128, TT], F32) for _ in range(2)]
            aa = [mk("aa", [128, TT], F32) for _ in range(2)]
            gg = [mk("gg", [128, TT], F32) for _ in range(2)]
            zr = mk("zr", [128, TT], F32)
            kkr = mk("kkr", [128, TT], F32)
            sqb = mk("sqb", [128, TT], BF16)
            sn = mk("sn", [128, TT], F32)
            apt = mk("apt", [128, TT], F32)
            bpt = mk("bpt", [128, TT], F32)
            kpt = mk("kpt", [128, TT], F32)
            rkr = mk("rkr", [128, TT], BF16)
            cum = mk("cum", [128, TT], F32)
            cumx = mk("cumx", [128, TT], F32)
            eP = mk("eP", [128, TT], F32)
            eN = mk("eN", [128, TT], F32)
            ePx = mk("ePx", [128, TT], F32)
            pcs = mk("pcs", [128, 2, NCH], F32)
            stg = {n: mk("st_" + n, [128, TT], BF16) for n in ["R", "A", "B", "K", "V2", "q", "k", "v", "za"]}
            stg["gate"] = mk("st_gate", [128, TT], F32)
            stg["bonus"] = mk("st_bonus", [128, TT], F32)
            cst = mk("cst", [128, 2, TT], F32)
            qf_s = [mk("qf", [128, TT], F32) for _ in range(2)]
            qb_s = [mk("qb", [128, TT], BF16) for _ in range(2)]
            t1_s = [mk("t1", [128, TT], F32) for _ in range(2)]
            t2_s = [mk("t2", [128, TT], F32) for _ in range(2)]

            def store(name, c2, tt):
                dst = SD[name][c2 * 128:(c2 + 1) * 128, tt * TT:(tt + 1) * TT]
                op("sync", lambda e: e.dma_start(out=dst, in_=stg[name][:]), reads=[stg[name]],
                   writes=[P.dtok((name, c2, tt))], dma=stg[name])

            xTv = xT.rearrange("(kc p) t -> p kc t", p=128)
            xcnt = [0]

            for pas in passes:
                if pas == 0:
                    c0, ncol = 1024, 1312
                    cts = [16, 17, 18, 8, 10, 12, 14, 9, 11, 13, 15]
                else:
                    c0, ncol = 0, 1024
                    cts = [0, 1, 2, 3, 4, 5, 6, 7]
                for kc in range(16):
                    w = ws[kc % 2]
                    op("sync", lambda e, w=w, kc=kc: e.dma_start(out=w[:, 0:ncol], in_=wsl[kc * 128:(kc + 1) * 128, c0:c0 + ncol]),
                       writes=[w], dma=w)
                    op("vector", lambda e, w=w, kc=kc: e.tensor_scalar(out=Wb[:, kc, 0:ncol], in0=w[:, 0:ncol], scalar1=pcol(kc), scalar2=None, op0=ALU.mult),
                       reads=[w, prm_t], writes=[Wb])

                for tt in range(ntt):
                    tsl = slice(tt * TT, (tt + 1) * TT)
                    xbb = xb[tt % 2]
                    for j in range(8):
                        xi = xcnt[0] % 2
                        xcnt[0] += 1
                        op("sync", lambda e, xi=xi, j=j: e.dma_start(out=xs[xi][:], in_=xTv[:, 2 * j:2 * j + 2, tsl]), writes=[xs[xi]], dma=xs[xi])
                        op("vector", lambda e, xi=xi, j=j: e.tensor_copy(out=xbb[:, 2 * j:2 * j + 2, :], in_=xs[xi][:]), reads=[xs[xi]], writes=[xbb])
                        op("scalar", lambda e, xi=xi: e.activation(out=sq[xi][:], in_=xs[xi][:], func=AF.Square), reads=[xs[xi]], writes=[sq[xi]])

                        def ssmm(e, xi=xi, j=j):
                            for q in range(2):
                                ins = e.matmul(ssp[:], lhsT=cb[:, CB_ONES:CB_ONES + 128], rhs=sq[xi][:, q, :], start=(j == 0 and q == 0), stop=(j == 7 and q == 1))
                            return ins
                        op("tensor", ssmm, reads=[sq[xi], cb], writes=[ssp])
                    op("vector", lambda e: e.tensor_scalar(out=tss[:], in0=ssp[:], scalar1=1.0 / D, scalar2=1e-5, op0=ALU.mult, op1=ALU.add), reads=[ssp], writes=[tss])
                    op("scalar", lambda e: e.activation(out=tss[:], in_=tss[:], func=AF.Sqrt), reads=[tss], writes=[tss])
                    op("vector", lambda e: e.reciprocal(out=rstd[:], in_=tss[:]), reads=[tss], writes=[rstd])
                    if pas == 1:
                        op("sync", lambda e: e.dma_start(out=cst[:], in_=tab[:, :, tsl]), writes=[cst], dma=cst)

                    pend = []
                    for ci, ct in enumerate(cts):
                        pp = pj[ci % 3]
                        wc0 = ct * 128 - c0
                        wn = 32 if ct == 18 else 128

                        def proj(e, pp=pp, wc0=wc0, wn=wn):
                            for kc in range(16):
                                ins = e.matmul(pp[0:wn, :], lhsT=Wb[:, kc, wc0:wc0 + wn], rhs=xbb[:, kc, :], start=(kc == 0), stop=(kc == 15))
                            return ins
                        op("tensor", proj, reads=[Wb, xbb], writes=[pp])

                        def post_ct(ct, pp, tt, wn):
                            qf, qb, t1, t2 = qf_s[ct % 2], qb_s[ct % 2], t1_s[ct % 2], t2_s[ct % 2]
                            if ct in U:
                                u = U[ct]
                                lc = lastc[ct]
                                mucol = {8: 16, 9: 17, 10: 18, 11: 19, 12: 20, 13: 21, 16: 22, 17: 23, 18: 24}[ct]
                                op("vector", lambda e, u=u, pp=pp, wn=wn: e.tensor_tensor(out=u[0:wn, 1:TT + 1], in0=pp[0:wn, :], in1=rstd[0:wn, :], op=ALU.mult), reads=[pp, rstd], writes=[u])
                                op("gpsimd", lambda e, u=u, lc=lc, wn=wn: e.tensor_copy(out=u[0:wn, 0:1], in_=lc[0:wn, :]), reads=[lc], writes=[u])
                                op("gpsimd", lambda e, u=u, lc=lc, wn=wn: e.tensor_copy(out=lc[0:wn, :], in_=u[0:wn, TT:TT + 1]), reads=[u], writes=[lc])
                                op("gpsimd", lambda e, u=u, wn=wn: e.tensor_tensor(out=dtl[0:wn, :], in0=u[0:wn, 0:TT], in1=u[0:wn, 1:TT + 1], op=ALU.subtract), reads=[u], writes=[dtl])
                                op("vector", lambda e, u=u, wn=wn, mucol=mucol: e.scalar_tensor_tensor(out=u[0:wn, 1:TT + 1], in0=dtl[0:wn, :], scalar=prm_t[0:wn, mucol:mucol + 1], in1=u[0:wn, 1:TT + 1], op0=ALU.mult, op1=ALU.add), reads=[dtl, u, prm_t], writes=[u])

                            if ct == 16:
                                u = U[16]
                                op("scalar", lambda e, u=u: e.activation(out=LA[0:64, :], in_=u[0:64, 1:TT + 1], func=AF.Tanh), reads=[u], writes=[LA])
                                op("scalar", lambda e, u=u: e.activation(out=LA[64:128, :], in_=u[64:128, 1:TT + 1], func=AF.Copy), reads=[u], writes=[LA])
                                for c2 in range(2):
                                    yield
                                    yield
                                    a1 = nxaux()
                                    op("tensor", lambda e, a1=a1, c2=c2: e.matmul(a1[:], lhsT=cmb[0:64, c2 * 128:(c2 + 1) * 128], rhs=LA[0:64, :], start=True, stop=True), reads=[cmb, LA], writes=[a1])
                                    op("scalar", lambda e, a1=a1, c2=c2: e.activation(out=lw[c2][:], in_=a1[:], func=AF.Sigmoid, bias=pcol(25 + c2), scale=1.0), reads=[a1, prm_t], writes=[lw[c2]])
                                    op("gpsimd", lambda e, c2=c2: e.tensor_scalar(out=lw[c2][:], in0=lw[c2][:], scalar1=-DEC, scalar2=None, op0=ALU.mult), reads=[lw[c2]], writes=[lw[c2]])
                                    yield
                                    yield
                                    a2 = nxaux()
                                    op("tensor", lambda e, a2=a2, c2=c2: e.matmul(a2[:], lhsT=cmb[64:128, c2 * 128:(c2 + 1) * 128], rhs=LA[64:128, :], start=True, stop=True), reads=[cmb, LA], writes=[a2])
                                    op("scalar", lambda e, a2=a2, c2=c2: e.activation(out=aa[c2][:], in_=a2[:], func=AF.Sigmoid, bias=pcol(27 + c2), scale=1.0), reads=[a2, prm_t], writes=[aa[c2]])
                            if ct == 17:
                                op("scalar", lambda e: e.activation(out=GL0[:], in_=U[17][:, 1:TT + 1], func=AF.Sigmoid), reads=[U[17]], writes=[GL0])
                            if ct == 18:
                                op("scalar", lambda e: e.activation(out=GL1[0:32, :], in_=U[18][0:32, 1:TT + 1], func=AF.Sigmoid), reads=[U[18]], writes=[GL1])
                                for c2 in range(2):
                                    yield
                                    yield
                                    a1 = nxaux()

                                    def gmm(e, a1=a1, c2=c2):
                                        e.matmul(a1[:], lhsT=cmb[:, 256 + c2 * 128:256 + (c2 + 1) * 128], rhs=GL0[:], start=True, stop=False)
                                        return e.matmul(a1[:], lhsT=cmb[0:32, 512 + c2 * 128:512 + (c2 + 1) * 128], rhs=GL1[0:32, :], start=False, stop=True)
                                    op("tensor", gmm, reads=[cmb, GL0, GL1], writes=[a1])
                                    op("scalar", lambda e, a1=a1, c2=c2: e.activation(out=gg[c2][:], in_=a1[:], func=AF.Copy), reads=[a1], writes=[gg[c2]])
                            if ct in (14, 15):
                                c2 = ct - 14
                                op("vector", lambda e, pp=pp: e.tensor_tensor(out=zr[:], in0=pp[:], in1=rstd[:], op=ALU.mult), reads=[pp, rstd], writes=[zr])
                                op("scalar", lambda e: e.activation(out=zr[:], in_=zr[:], func=AF.Silu), reads=[zr], writes=[zr])
                                r_s = U[8 + c2]
                                k_s = U[10 + c2]
                                v_s = U[12 + c2]
                                RS = lambda t: t[:, 1:TT + 1]
                                op("gpsimd", lambda e, c2=c2: e.tensor_tensor(out=stg["gate"][:], in0=gg[c2][:], in1=zr[:], op=ALU.mult), reads=[gg[c2], zr], writes=[stg["gate"]])
                                store("gate", c2, tt)
                                op("vector", lambda e, c2=c2, k_s=k_s: e.tensor_scalar(out=kkr[:], in0=RS(k_s), scalar1=pcol(29 + c2), scalar2=None, op0=ALU.mult), reads=[k_s, prm_t], writes=[kkr])
                                op("scalar", lambda e: e.activation(out=sqb[:], in_=kkr[:], func=AF.Square), reads=[kkr], writes=[sqb])
                                yield
                                yield
                                a1 = nxaux()
                                op("tensor", lambda e, a1=a1: e.matmul(a1[:], lhsT=bdm, rhs=sqb[:], start=True, stop=True), reads=[cb, sqb], writes=[a1])
                                op("scalar", lambda e, a1=a1: e.activation(out=sn[:], in_=a1[:], func=AF.Sqrt), reads=[a1], writes=[sn])
                                op("vector", lambda e: e.tensor_scalar(out=sn[:], in0=sn[:], scalar1=1e-12, scalar2=None, op0=ALU.max), reads=[sn], writes=[sn])
                                op("vector", lambda e: e.reciprocal(out=sn[:], in_=sn[:]), reads=[sn], writes=[sn])
                                op("vector", lambda e: e.scalar_tensor_tensor(out=apt[:], in0=kkr[:], scalar=-1.0, in1=sn[:], op0=ALU.mult, op1=ALU.mult), reads=[kkr, sn], writes=[apt])
                                op("vector", lambda e, c2=c2: e.scalar_tensor_tensor(out=bpt[:], in0=apt[:], scalar=-1.0, in1=aa[c2][:], op0=ALU.mult, op1=ALU.mult), reads=[apt, aa[c2]], writes=[bpt])
                                op("vector", lambda e, c2=c2: e.tensor_scalar(out=kpt[:], in0=aa[c2][:], scalar1=-1.0, scalar2=pcol(31 + c2), op0=ALU.add, op1=ALU.mult), reads=[aa[c2], prm_t], writes=[kpt])
                                op("vector", lambda e, k_s=k_s: e.scalar_tensor_tensor(out=kpt[:], in0=kpt[:], scalar=1.0, in1=RS(k_s), op0=ALU.add, op1=ALU.mult), reads=[kpt, k_s], writes=[kpt])
                                op("vector", lambda e, c2=c2, r_s=r_s: e.scalar_tensor_tensor(out=rkr[:], in0=RS(r_s), scalar=pcol(33 + c2), in1=kpt[:], op0=ALU.mult, op1=ALU.mult), reads=[r_s, kpt, prm_t], writes=[rkr])
                                yield
                                yield
                                a2 = nxaux()
                                op("tensor", lambda e, a2=a2: e.matmul(a2[:], lhsT=bdm, rhs=rkr[:], start=True, stop=True), reads=[cb, rkr], writes=[a2])
                                op("vector", lambda e, a2=a2, v_s=v_s: e.tensor_tensor(out=stg["bonus"][:], in0=a2[:], in1=RS(v_s), op=ALU.mult), reads=[a2, v_s], writes=[stg["bonus"]])
                                store("bonus", c2, tt)
                                op("vector", lambda e, c2=c2: e.tensor_tensor_scan(out=cum[:], data0=rmask, data1=lw[c2][:], initial=0.0, op0=ALU.mult, op1=ALU.add), reads=[cf, lw[c2]], writes=[cum])
                                op("gpsimd", lambda e, c2=c2: e.tensor_tensor(out=cumx[:], in0=cum[:], in1=lw[c2][:], op=ALU.subtract), reads=[cum, lw[c2]], writes=[cumx])
                                op("scalar", lambda e: e.activation(out=eP[:], in_=cum[:], func=AF.Exp), reads=[cum], writes=[eP])
                                op("scalar", lambda e: e.activation(out=eN[:], in_=cum[:], func=AF.Exp, scale=-1.0), reads=[cum], writes=[eN])
                                op("scalar", lambda e: e.activation(out=ePx[:], in_=cumx[:], func=AF.Exp), reads=[cumx], writes=[ePx])
                                op("gpsimd", lambda e, c2=c2, tt=tt: e.tensor_copy(out=pcs[:, c2, tt * 4:(tt + 1) * 4], in_=eP[:, CH - 1:TT:CH]), reads=[eP], writes=[pcs])
                                op("vector", lambda e, r_s=r_s: e.tensor_tensor(out=stg["R"][:], in0=RS(r_s), in1=eP[:], op=ALU.mult), reads=[r_s, eP], writes=[stg["R"]])
                                store("R", c2, tt)
                                op("gpsimd", lambda e: e.tensor_tensor(out=stg["A"][:], in0=apt[:], in1=ePx[:], op=ALU.mult), reads=[apt, ePx], writes=[stg["A"]])
                                store("A", c2, tt)
                                op("vector", lambda e: e.tensor_tensor(out=stg["B"][:], in0=bpt[:], in1=eN[:], op=ALU.mult), reads=[bpt, eN], writes=[stg["B"]])
                                store("B", c2, tt)
                                op("gpsimd", lambda e: e.tensor_tensor(out=stg["K"][:], in0=kpt[:], in1=eN[:], op=ALU.mult), reads=[kpt, eN], writes=[stg["K"]])
                                store("K", c2, tt)
                                op("scalar", lambda e, v_s=v_s: e.activation(out=stg["V2"][:], in_=RS(v_s), func=AF.Copy), reads=[v_s], writes=[stg["V2"]])
                                store("V2", c2, tt)
                            if ct in (0, 1, 2, 3):
                                nm = "q" if ct < 2 else "k"
                                c2 = ct % 2
                                scl = 0.125 if ct < 2 else 1.0
                                op("vector", lambda e, pp=pp, scl=scl: e.scalar_tensor_tensor(out=qf[:], in0=pp[:], scalar=scl, in1=rstd[:], op0=ALU.mult, op1=ALU.mult), reads=[pp, rstd], writes=[qf])
                                op("scalar", lambda e: e.activation(out=qb[:], in_=qf[:], func=AF.Copy), reads=[qf], writes=[qb])
                                yield
                                yield
                                a1 = nxaux()
                                op("tensor", lambda e, a1=a1: e.matmul(a1[:], lhsT=perm, rhs=qb[:], start=True, stop=True), reads=[cb, qb], writes=[a1])
                                op("gpsimd", lambda e: e.tensor_tensor(out=t1[:], in0=qf[:], in1=cst[:, 0, :], op=ALU.mult), reads=[qf, cst], writes=[t1])
                                op("vector", lambda e, a1=a1: e.tensor_tensor(out=t2[:], in0=a1[:], in1=cst[:, 1, :], op=ALU.mult), reads=[a1, cst], writes=[t2])
                                op("gpsimd", lambda e, nm=nm: e.tensor_tensor(out=stg[nm][:], in0=t1[:], in1=t2[:], op=ALU.add), reads=[t1, t2], writes=[stg[nm]])
                                store(nm, c2, tt)
                            if ct in (4, 5):
                                op("vector", lambda e, pp=pp: e.tensor_tensor(out=stg["v"][:], in0=pp[:], in1=rstd[:], op=ALU.mult), reads=[pp, rstd], writes=[stg["v"]])
                                store("v", ct - 4, tt)
                            if ct in (6, 7):
                                op("vector", lambda e, pp=pp: e.tensor_tensor(out=qf[:], in0=pp[:], in1=rstd[:], op=ALU.mult), reads=[pp, rstd], writes=[qf])
                                op("scalar", lambda e: e.activation(out=stg["za"][:], in_=qf[:], func=AF.Silu), reads=[qf], writes=[stg["za"]])
                                store("za", ct - 6, tt)
                            yield
                        pend.append(post_ct(ct, pp, tt, wn))
                        for g in list(pend):
                            try:
                                next(g)
                            except StopIteration:
                                pend.remove(g)
                    while pend:
                        for g in list(pend):
                            try:
                                next(g)
                            except StopIteration:
                                pend.remove(g)
                if pas == 0:
                    for c2 in range(2):
                        op("sync", lambda e, c2=c2: e.dma_start(out=SD["pc"][c2 * 128:(c2 + 1) * 128, :], in_=pcs[:, c2, :]), reads=[pcs], writes=[P.dtok(("pc", c2))], dma=pcs)
                        op("sync", None, reads=[P.dtok(("pc", c2))])
            P.barrier()
            P.flush()
        if "C" in phases:
            build_attention(P, nc, SD, mix, cb, cf)
        if "D" in phases:
            build_rwkv(P, nc, SD, mix, cb, cf, prm_t)
        op("sync", None, writes=P.toks)
        P.flush()
    return nc


def _mk(P, sa, name, shape, dt, psum=False):
    tl = Tl.__new__(Tl)
    tl.t = (P.ps if psum else P.sb)(name, shape, dt, sa)
    tl.k = P.tok(name)
    return tl


def build_attention(P, nc, SD, mix, cb, cf):
    op = P.op
    NB = 4
    with contextlib.ExitStack() as sa:
        mk = lambda n, sh, dt, ps=False: _mk(P, sa, n, sh, dt, ps)
        QKVZ = [[mk(n, [64, S], BF16) for n in ("Q", "K", "V", "ZA")] for _ in range(1)]
        accs = [mk("acc", [65, S], F32) for _ in range(2)]
        vtok = mk("vtok", [128, 192, 65], BF16)
        UB = [mk("UB", [128, 512], F32, True) for _ in range(NB)]
        TB = [mk("TB", [128, 1024], BF16, True) for _ in range(2)]
        pB = mk("pB", [128, TT], F32, True)
        PT = [mk("PT", [128, 256], BF16) for _ in range(NB)]
        rec = [mk("rec", [64, TT], F32) for _ in range(2)]
        ost = [mk("ost", [64, TT], F32) for _ in range(2)]
        op("gpsimd", lambda e: e.memset(vtok[:], 1.0), writes=[vtok])
        mbias = cb[:, CB_AM:CB_AM + 256]
        ident = cb[:, CB_ID:CB_ID + 128]
        id64 = cb[0:64, CB_ID:CB_ID + 64]
        onesrow = cf[64:65, 64 + TT:64 + TT + 64]

        def load(h):
            rows = slice(h * 64, (h + 1) * 64)
            c2 = h // 2
            deps = [P.dtok((n, c2, tt)) for n in ["q", "k", "v", "za"] for tt in range(NTT)]
            for t, n in zip(QKVZ[0], ["q", "k", "v", "za"]):
                op("sync", lambda e, t=t, n=n, rows=rows: e.dma_start(out=t[:], in_=SD[n][rows, :]), reads=deps, writes=[t], dma=t)

        u = 0
        for h in range(4):
            rows = slice(h * 64, (h + 1) * 64)
            Q, K, V, ZA = QKVZ[0]
            load(h)
            for acc in accs:
                op("gpsimd", lambda e, acc=acc: e.memset(acc[:], 0.0), writes=[acc])
            blocks = []
            for d in (1, 4, 16):
                nb = (S // d) // 128
                for r in range(d):
                    for j in range(nb):
                        nq = 256 if j < nb - 1 else 128
                        k0 = r + 128 * j * d
                        blocks.append((slice(k0, k0 + 127 * d + 1, d), slice(k0, k0 + (nq - 1) * d + 1, d), nq))
            for g8 in range(len(blocks) // 8):
                tb = TB[g8 % 2]

                def tr8(e, g8=g8, tb=tb, V=V):
                    for q in range(8):
                        ins = e.transpose(tb[:, q * 64:(q + 1) * 64], V[:, blocks[g8 * 8 + q][0]], id64)
                    return ins
                op("tensor", tr8, reads=[V, cb], writes=[tb])
                src = tb[:, 0:512].rearrange("p (b c) -> p b c", c=64)
                if g8 % 2 == 0:
                    op("vector", lambda e, g8=g8, src=src: e.tensor_copy(out=vtok[:, g8 * 8:(g8 + 1) * 8, 0:64], in_=src), reads=[tb], writes=[vtok])
                else:
                    op("scalar", lambda e, g8=g8, src=src: e.activation(out=vtok[:, g8 * 8:(g8 + 1) * 8, 0:64], in_=src, func=AF.Copy), reads=[tb], writes=[vtok])

            def s_part(bi):
                ksl, qsl, nq = blocks[bi]
                i = (u + bi) % NB
                ub = UB[i]

                def smm(e, ub=ub, ksl=ksl, qsl=qsl, nq=nq, K=K, Q=Q):
                    e.matmul(ub[:, 0:nq], lhsT=K[:, ksl], rhs=Q[:, qsl], start=True, stop=False)
                    return e.matmul(ub[:, 0:nq], lhsT=ident, rhs=mbias[:, 0:nq], start=False, stop=True)
                op("tensor", smm, reads=[K, Q, cb], writes=[ub])
                op("scalar", lambda e, ub=ub, i=i, nq=nq: e.activation(out=PT[i][:, 0:nq], in_=ub[:, 0:nq], func=AF.Exp), reads=[ub], writes=[PT[i]])

            def o_part(bi):
                ksl, qsl, nq = blocks[bi]
                i = (u + bi) % NB
                ub = UB[i]
                op("tensor", lambda e, ub=ub, i=i, nq=nq, bi=bi: e.matmul(ub[0:65, 256:256 + nq], lhsT=vtok[:, bi, :], rhs=PT[i][:, 0:nq], start=True, stop=True), reads=[vtok, PT[i]], writes=[ub])
                acc = accs[bi % 2]
                op("vector", lambda e, ub=ub, nq=nq, qsl=qsl, acc=acc: e.tensor_tensor(out=acc[:, qsl], in0=ub[0:65, 256:256 + nq], in1=acc[:, qsl], op=ALU.add), reads=[ub, acc], writes=[acc])

            SK = 2
            nbk = len(blocks)
            for bi in range(nbk + SK):
                if bi < nbk:
                    s_part(bi)
                if bi >= SK:
                    o_part(bi - SK)
            u += nbk
            for tt in range(NTT):
                tsl = slice(tt * TT, (tt + 1) * TT)
                rc, os_ = rec[tt % 2], ost[tt % 2]
                def denmm(e, tsl=tsl):
                    e.matmul(pB[0:64, :], lhsT=onesrow, rhs=accs[0][64:65, tsl], start=True, stop=False)
                    return e.matmul(pB[0:64, :], lhsT=onesrow, rhs=accs[1][64:65, tsl], start=False, stop=True)
                op("tensor", denmm, reads=[cf, accs[0], accs[1]], writes=[pB])
                op("vector", lambda e, rc=rc: e.reciprocal(out=rc[:], in_=pB[0:64, :]), reads=[pB], writes=[rc])
                op("gpsimd", lambda e, os_=os_, tsl=tsl: e.tensor_tensor(out=os_[:], in0=accs[0][0:64, tsl], in1=accs[1][0:64, tsl], op=ALU.add), reads=[accs[0], accs[1]], writes=[os_])
                op("vector", lambda e, rc=rc, os_=os_: e.tensor_tensor(out=rc[:], in0=os_[:], in1=rc[:], op=ALU.mult), reads=[os_, rc], writes=[rc])
                op("gpsimd", lambda e, rc=rc, os_=os_, tsl=tsl, ZA=ZA: e.tensor_tensor(out=os_[:], in0=rc[:], in1=ZA[:, tsl], op=ALU.mult), reads=[rc, ZA], writes=[os_])
                op("sync", lambda e, os_=os_, tsl=tsl, rows=rows: e.dma_start(out=mix[rows, tsl], in_=os_[:]), reads=[os_], writes=[P.dtok(("mixa", h, tt))], dma=os_)
        P.barrier()
        P.flush()


def build_rwkv(P, nc, SD, mix, cb, cf, prm_t):
    op = P.op
    SEG = 512
    NSEG = S // SEG
    CPS = SEG // CH
    PD = 3
    NP = PD + 1
    with contextlib.ExitStack() as sa:
        mk = lambda n, sh, dt, ps=False: _mk(P, sa, n, sh, dt, ps)
        ident = cb[:, CB_ID:CB_ID + 128]
        id64 = cb[0:64, CB_ID:CB_ID + 64]
        mask4 = cb[:, CB_M4:CB_M4 + 512]
        maskts = cb[:, CB_MTS:CB_MTS + 128]
        ones64 = cf[0:64, 0:64]
        pool = [(P.ps("pb", [128, 512], F32, sa), P.tok("PB")) for _ in range(6)]
        b3 = P.ps("b3", [128, 1024], BF16, sa)
        t3 = P.tok("B3")
        pi = [0]

        def nxb():
            pi[0] += 1
            return pool[pi[0] % 6]

        HD = []
        for hh in range(4):
            d = {}
            for n in ["R", "A", "B", "K", "V2"]:
                d[n] = [mk("r" + n, [64, SEG], BF16) for _ in range(2)]
            d["y"] = [mk("ry", [64, SEG], F32) for _ in range(2)]
            for n in ["gate", "bonus"]:
                d[n] = mk("r" + n, [64, SEG], F32)
            d["pc"] = mk("rpc", [64, NCH], F32)
            d["H"] = mk("H", [64, 64], F32)
            d["Hb"] = mk("Hb", [64, 64], BF16)
            d["Ht"] = mk("Ht", [64, 64], F32)
            d["G4"] = [mk("G4", [128, 512], BF16) for _ in range(NP)]
            d["N1"] = [mk("N1", [128, 128], BF16) for _ in range(NP)]
            d["tok3"] = [mk("tok3", [128, 192], BF16) for _ in range(NP)]
            d["Sm"] = [[mk("Sm", [128, 128], BF16) for _ in range(2)] for _ in range(NP)]
            d["NM"] = [[mk("NM", [128, 256], BF16) for _ in range(2)] for _ in range(NP)]
            d["Sf"] = [None] * NP
            d["Zb"] = mk("Zb", [128, 64], BF16)
            d["Ub"] = mk("Ub", [128, 64], BF16)
            for n in ["ysq", "mt", "m2", "var", "yc", "ost"]:
                d[n] = mk("r" + n, [64, TT], F32)
            HD.append(d)

        def pre(d, h, gc):
            seg, c = gc // CPS, gc % CPS
            p = gc % NP
            cs = slice(c * CH, (c + 1) * CH)
            RT, AT, BT, KT, VT = [d[n][seg % 2] for n in ["R", "A", "B", "K", "V2"]]
            G4, N1, tok3 = d["G4"][p], d["N1"][p], d["tok3"][p]
            b0, t0 = nxb()

            def g4(e):
                for bi, (l, r) in enumerate([(BT, AT), (KT, AT), (BT, RT), (KT, RT)]):
                    ins = e.matmul(b0[:, bi * 128:(bi + 1) * 128], lhsT=l[:, cs], rhs=r[:, cs], start=True, stop=True)
                return ins
            op("tensor", g4, reads=[RT, AT, BT, KT], writes=[t0])
            op("vector", lambda e: e.tensor_tensor(out=G4[:], in0=b0[:], in1=mask4, op=ALU.mult), reads=[cb], writes=[G4, t0])
            b1, t1 = nxb()
            op("tensor", lambda e: e.matmul(b1[:, 0:128], lhsT=AT[:, cs], rhs=BT[:, cs], start=True, stop=True), reads=[AT, BT], writes=[t1])
            op("vector", lambda e: e.tensor_tensor(out=N1[:], in0=b1[:, 0:128], in1=maskts, op=ALU.mult), reads=[cb], writes=[N1, t1])
            bq = b3[:, h * 256:(h + 1) * 256]

            def tr(e):
                e.transpose(bq[:, 0:64], VT[:, cs], id64)
                e.transpose(bq[:, 64:128], BT[:, cs], id64)
                return e.transpose(bq[:, 128:192], KT[:, cs], id64)
            op("tensor", tr, reads=[VT, BT, KT, cb], writes=[t3])
            op("scalar", lambda e: e.activation(out=tok3[:], in_=bq[:, 0:192], func=AF.Copy), reads=[], writes=[tok3, t3])
            yield
            Sm = d["Sm"][p][0]
            op("gpsimd", lambda e: e.tensor_tensor(out=Sm[:], in0=G4[:, 0:128], in1=ident, op=ALU.add), reads=[G4, cb], writes=[Sm])
            np_ap, mp_ap = (lambda: N1[:]), (lambda: G4[:, 0:128])
            np_tok, mp_tok = N1, G4
            si = 0
            prevNM = None
            for lvl in range(7):
                bs, ts_ = nxb()
                NM = d["NM"][p][lvl % 2] if lvl < 6 else None
                So = d["Sm"][p][si]
                Sn = d["Sm"][p][1 - si]

                def lvl_mm(e, np_ap=np_ap, mp_ap=mp_ap, bs=bs, lvl=lvl, prevNM=prevNM, So=So):
                    ins = None
                    if lvl < 6:
                        e.matmul(bs[:, 0:128], lhsT=mp_ap(), rhs=np_ap(), start=True, stop=True)
                        ins = e.matmul(bs[:, 128:256], lhsT=np_ap(), rhs=mp_ap(), start=True, stop=True)
                    if lvl > 0:
                        ins = e.matmul(bs[:, 256:384], lhsT=prevNM[:, 0:128], rhs=So[:], start=True, stop=True)
                    return ins
                rd = [mp_tok, np_tok] + ([prevNM, So] if lvl > 0 else [])
                op("tensor", lvl_mm, reads=rd, writes=[ts_])
                if lvl < 6:
                    op("scalar", lambda e, NM=NM, bs=bs: e.activation(out=NM[:], in_=bs[:, 0:256], func=AF.Copy), reads=[], writes=[NM, ts_])
                if lvl > 0:
                    op("vector", lambda e, So=So, Sn=Sn, bs=bs: e.tensor_tensor(out=Sn[:], in0=bs[:, 256:384], in1=So[:], op=ALU.add), reads=[So], writes=[Sn, ts_])
                    si = 1 - si
                if lvl < 6:
                    np_ap, mp_ap = (lambda NM=NM: NM[:, 0:128]), (lambda NM=NM: NM[:, 128:256])
                    np_tok = mp_tok = NM
                    prevNM = NM
                yield
            d["Sf"][p] = d["Sm"][p][si]

        def chain(d, h, gc):
            seg, c = gc // CPS, gc % CPS
            p = gc % NP
            cs = slice(c * CH, (c + 1) * CH)
            RT, AT = d["R"][seg % 2], d["A"][seg % 2]
            G4, tok3, Sf = d["G4"][p], d["tok3"][p], d["Sf"][p]
            y = d["y"][seg % 2]
            Hb, H, Ht = d["Hb"], d["H"], d["Ht"]
            Zb, Ub = d["Zb"], d["Ub"]
            bz, tz = nxb()

            def zmm(e):
                e.matmul(bz[:, 0:64], lhsT=AT[:, cs], rhs=Hb[:], start=True, stop=False)
                return e.matmul(bz[:, 0:64], lhsT=G4[:, 128:256], rhs=tok3[:, 0:64], start=False, stop=True)
            op("tensor", zmm, reads=[AT, Hb, G4, tok3], writes=[tz])
            op("scalar", lambda e: e.activation(out=Zb[:], in_=bz[:, 0:64], func=AF.Copy), reads=[], writes=[Zb, tz])
            yield
            bu, tu = nxb()
            op("tensor", lambda e: e.matmul(bu[:, 0:64], lhsT=Sf[:], rhs=Zb[:], start=True, stop=True), reads=[Sf, Zb], writes=[tu])
            op("vector", lambda e: e.tensor_copy(out=Ub[:], in_=bu[:, 0:64]), reads=[], writes=[Ub, tu])
            yield
            by, ty = nxb()
            bh, th = nxb()

            def yhmm(e):
                e.matmul(bh[0:64, 0:64], lhsT=tok3[:, 64:128], rhs=Ub[:], start=True, stop=False)
                e.matmul(bh[0:64, 0:64], lhsT=tok3[:, 128:192], rhs=tok3[:, 0:64], start=False, stop=True)
                e.matmul(by[0:64, 0:128], lhsT=Hb[:], rhs=RT[:, cs], start=True, stop=False)
                e.matmul(by[0:64, 0:128], lhsT=Ub[:], rhs=G4[:, 256:384], start=False, stop=False)
                return e.matmul(by[0:64, 0:128], lhsT=tok3[:, 0:64], rhs=G4[:, 384:512], start=False, stop=True)
            op("tensor", yhmm, reads=[Hb, RT, Ub, G4, tok3], writes=[ty, th])
            op("vector", lambda e: e.tensor_tensor(out=Ht[:], in0=bh[0:64, 0:64], in1=H[:], op=ALU.add), reads=[H], writes=[Ht, th])
            op("scalar", lambda e: e.activation(out=y[:, cs], in_=by[0:64, 0:128], func=AF.Copy), reads=[], writes=[y, ty])
            op("vector", lambda e: e.tensor_scalar(out=H[:], in0=Ht[:], scalar1=d["pc"][:, gc:gc + 1], scalar2=None, op0=ALU.mult), reads=[Ht, d["pc"]], writes=[H])
            op("vector", lambda e: e.tensor_copy(out=Hb[:], in_=H[:]), reads=[H], writes=[Hb])
            yield

        def post(d, h, seg):
            y = d["y"][seg % 2]
            for q in range(SEG // TT):
                ts = slice(q * TT, (q + 1) * TT)
                gts = slice(seg * SEG + q * TT, seg * SEG + (q + 1) * TT)
                bm, tm = nxb()
                bq, tq = nxb()
                op("scalar", lambda e, ts=ts: e.activation(out=d["ysq"][:], in_=y[:, ts], func=AF.Square), reads=[y], writes=[d["ysq"]])
                op("tensor", lambda e, ts=ts, bm=bm: e.matmul(bm[0:64, :], lhsT=ones64, rhs=y[:, ts], start=True, stop=True), reads=[cf, y], writes=[tm])
                op("scalar", lambda e, bm=bm: e.activation(out=d["mt"][:], in_=bm[0:64, :], func=AF.Copy), reads=[], writes=[d["mt"], tm])
                op("tensor", lambda e, bq=bq: e.matmul(bq[0:64, :], lhsT=ones64, rhs=d["ysq"][:], start=True, stop=True), reads=[cf, d["ysq"]], writes=[tq])
                op("gpsimd", lambda e: e.tensor_tensor(out=d["m2"][:], in0=d["mt"][:], in1=d["mt"][:], op=ALU.mult), reads=[d["mt"]], writes=[d["m2"]])
                op("vector", lambda e, bq=bq: e.tensor_tensor(out=d["var"][:], in0=bq[0:64, :], in1=d["m2"][:], op=ALU.subtract), reads=[d["m2"]], writes=[d["var"], tq])
                yield
                op("gpsimd", lambda e: e.tensor_scalar(out=d["var"][:], in0=d["var"][:], scalar1=64e-5, scalar2=None, op0=ALU.add), reads=[d["var"]], writes=[d["var"]])
                op("scalar", lambda e: e.activation(out=d["var"][:], in_=d["var"][:], func=AF.Sqrt), reads=[d["var"]], writes=[d["var"]])
                op("vector", lambda e: e.reciprocal(out=d["var"][:], in_=d["var"][:]), reads=[d["var"]], writes=[d["var"]])
                op("gpsimd", lambda e, ts=ts: e.tensor_tensor(out=d["yc"][:], in0=y[:, ts], in1=d["mt"][:], op=ALU.subtract), reads=[y, d["mt"]], writes=[d["yc"]])
                yield
                op("gpsimd", lambda e: e.tensor_tensor(out=d["yc"][:], in0=d["yc"][:], in1=d["var"][:], op=ALU.mult), reads=[d["yc"], d["var"]], writes=[d["yc"]])
                op("vector", lambda e: e.tensor_scalar(out=d["yc"][:], in0=d["yc"][:], scalar1=prm_t[0:64, 40 + h:41 + h], scalar2=prm_t[0:64, 44 + h:45 + h], op0=ALU.mult, op1=ALU.add), reads=[d["yc"], prm_t], writes=[d["yc"]])
                op("gpsimd", lambda e, ts=ts: e.tensor_tensor(out=d["yc"][:], in0=d["yc"][:], in1=d["bonus"][:, ts], op=ALU.add), reads=[d["yc"], d["bonus"]], writes=[d["yc"]])
                op("gpsimd", lambda e, ts=ts: e.tensor_tensor(out=d["ost"][:], in0=d["yc"][:], in1=d["gate"][:, ts], op=ALU.mult), reads=[d["yc"], d["gate"]], writes=[d["ost"]])
                op("sync", lambda e, gts=gts: e.dma_start(out=mix[256 + h * 64:256 + (h + 1) * 64, gts], in_=d["ost"][:]), reads=[d["ost"]], writes=[P.dtok(("mixr", h, seg, q))], dma=d["ost"])
                yield

        def loads(seg, names, idx):
            for h in range(4):
                d = HD[h]
                c2 = h // 2
                for n in names:
                    dst = d[n][idx] if idx is not None else d[n]
                    deps = [P.dtok((n, c2, tt)) for tt in range(seg * (SEG // TT), (seg + 1) * (SEG // TT))]
                    op("sync", lambda e, dst=dst, n=n, h=h, seg=seg: e.dma_start(out=dst[:], in_=SD[n][h * 64:(h + 1) * 64, seg * SEG:(seg + 1) * SEG]), reads=deps, writes=[dst], dma=dst)

        for h in range(4):
            d = HD[h]
            op("gpsimd", lambda e, d=d: e.memset(d["H"][:], 0.0), writes=[d["H"]])
            op("gpsimd", lambda e, d=d: e.memset(d["Hb"][:], 0.0), writes=[d["Hb"]])
            op("sync", lambda e, d=d, h=h: e.dma_start(out=d["pc"][:], in_=SD["pc"][h * 64:(h + 1) * 64, :]), reads=[P.dtok(("pc", h // 2))], writes=[d["pc"]], dma=d["pc"])
        loads(0, ["R", "A", "B", "K", "V2"], 0)

        def run(gens):
            alive = True
            while alive:
                alive = False
                for g in gens:
                    try:
                        next(g)
                        alive = True
                    except StopIteration:
                        pass

        run([pre(HD[h], h, 0) for h in range(4)])
        active = []
        for gc in range(1, PD):
            active.append([gc, [pre(HD[h], h, gc) for h in range(4)]])
        posts = []
        for gc in range(NCH):
            seg, c = gc // CPS, gc % CPS
            if c == 0 and seg + 1 < NSEG:
                loads(seg + 1, ["R", "A", "B", "K", "V2"], (seg + 1) % 2)
            if gc + PD < NCH:
                active.append([gc + PD, [pre(HD[h], h, gc + PD) for h in range(4)]])
            must = [chain(HD[h], h, gc) for h in range(4)]
            while True:
                pend = False
                for g in list(must):
                    try:
                        next(g)
                        pend = True
                    except StopIteration:
                        must.remove(g)
                for ent in active:
                    for g in list(ent[1]):
                        try:
                            next(g)
                            if ent[0] == gc + 1:
                                pend = True
                        except StopIteration:
                            ent[1].remove(g)
                for g in list(posts):
                    try:
                        next(g)
                    except StopIteration:
                        posts.remove(g)
                active = [ent for ent in active if ent[1]]
                if not pend:
                    break
            if c == CPS - 1:
                loads(seg, ["gate", "bonus"], None)
                posts += [post(HD[h], h, seg) for h in range(4)]
        run(posts)
        P.barrier()
        P.flush()


def build_l2():
    nc = bass.Bass("TRN2", target_bir_lowering=False)
    NT = 2048
    mixT = nc.dram_tensor("mixT", [NT // 128, 128, 16 * 128], F32, kind="ExternalInput").ap()
    wout = nc.dram_tensor("wout", [D, D], F32, kind="ExternalInput").ap()
    xin = nc.dram_tensor("xin", [NT, D], F32, kind="ExternalInput").ap()
    gfin = nc.dram_tensor("gfin", [128, D], F32, kind="ExternalInput").ap()
    out = nc.dram_tensor("out", [NT, D], F32, kind="ExternalOutput").ap()
    with contextlib.ExitStack() as st:
        P = Prog(nc, st)
        op = P.op
        mk = lambda n, sh, dt, ps=False: _mk(P, st, n, sh, dt, ps)
        Wo = mk("Wo", [128, 16, D], BF16)
        wst = [mk("wst", [128, D], F32) for _ in range(2)]
        gf = mk("gf", [128, D], F32)
        op("sync", lambda e: e.dma_start(out=gf[:], in_=gfin), writes=[gf], dma=gf)
        for kc in range(16):
            w = wst[kc % 2]
            op("sync" if kc % 2 == 0 else "scalar", lambda e, w=w, kc=kc: e.dma_start(out=w[:], in_=wout[kc * 128:(kc + 1) * 128, :]), writes=[w], dma=w)
            op("vector" if kc % 2 == 0 else "gpsimd", lambda e, w=w, kc=kc: e.tensor_copy(out=Wo[:, kc, :], in_=w[:]), reads=[w], writes=[Wo])
        mf = [mk("mf", [128, 16, 128], F32) for _ in range(2)]
        mb = [mk("mb", [128, 16, 128], BF16) for _ in range(2)]
        xt = [mk("xt", [128, D], F32) for _ in range(2)]
        ys = [mk("ys", [128, D], F32) for _ in range(2)]
        junk = mk("junk", [128, D], F32)
        ss = mk("ss", [128, 1], F32)
        pp = [mk("pp", [128, 512], F32, True) for _ in range(4)]
        for t in range(NT // 128):
            i = t % 2
            tsl = slice(t * 128, (t + 1) * 128)
            op("sync", lambda e, i=i, t=t: e.dma_start(out=mf[i][:], in_=mixT[t].rearrange("p (kc t) -> p kc t", kc=16)), writes=[mf[i]], dma=mf[i])
            op("sync", lambda e, i=i, tsl=tsl: e.dma_start(out=xt[i][:], in_=xin[tsl, :]), writes=[xt[i]], dma=xt[i])
            op("gpsimd", lambda e, i=i: e.tensor_copy(out=mb[i][:], in_=mf[i][:]), reads=[mf[i]], writes=[mb[i]])
            for cg in range(4):
                def mm(e, i=i, cg=cg):
                    for kc in range(16):
                        ins = e.matmul(pp[cg][:], lhsT=mb[i][:, kc, :], rhs=Wo[:, kc, cg * 512:(cg + 1) * 512], start=(kc == 0), stop=(kc == 15))
                    return ins
                op("tensor", mm, reads=[mb[i], Wo], writes=[pp[cg]])
                op("vector", lambda e, i=i, cg=cg: e.tensor_tensor(out=ys[i][:, cg * 512:(cg + 1) * 512], in0=pp[cg][:], in1=xt[i][:, cg * 512:(cg + 1) * 512], op=ALU.add), reads=[pp[cg], xt[i]], writes=[ys[i]])
            op("scalar", lambda e, i=i: e.activation(out=junk[:], in_=ys[i][:], func=AF.Square, accum_out=ss[:]), reads=[ys[i]], writes=[junk, ss])
            op("vector", lambda e: e.tensor_scalar(out=ss[:], in0=ss[:], scalar1=1.0 / D, scalar2=1e-5, op0=ALU.mult, op1=ALU.add), reads=[ss], writes=[ss])
            op("scalar", lambda e: e.activation(out=ss[:], in_=ss[:], func=AF.Sqrt), reads=[ss], writes=[ss])
            op("vector", lambda e: e.reciprocal(out=ss[:], in_=ss[:]), reads=[ss], writes=[ss])
            op("vector", lambda e, i=i: e.scalar_tensor_tensor(out=ys[i][:], in0=ys[i][:], scalar=ss[:, 0:1], in1=gf[:], op0=ALU.mult, op1=ALU.mult), reads=[ys[i], ss, gf], writes=[ys[i]])
            op("sync", lambda e, i=i, tsl=tsl: e.dma_start(out=out[tsl, :], in_=ys[i][:]), reads=[ys[i]], writes=[P.dtok(("out", t))], dma=ys[i])
        op("sync", None, writes=P.toks)
        P.flush()
    return nc


def _group_cols(g):
    hs = np.arange(g * 256, (g + 1) * 256)
    base = 4 * A
    cols = [hs, A + hs, 2 * A + hs, 3 * A + hs,
            base + hs, base + R + hs, base + 2 * R + hs, base + SHIFT + hs,
            base + 3 * R + np.arange(128), base + 3 * R + 128 + np.arange(160)]
    return np.concatenate(cols)


def _prep_l1(inp, b, g, consts):
    cb, cf, tab = consts
    f = lambda n: np.asarray(inp[n], dtype=np.float32)
    hs = slice(g * 256, (g + 1) * 256)
    xT = np.ascontiguousarray(f("x")[b].T)
    wsl = np.ascontiguousarray(f("w_in")[0][:, _group_cols(g)])
    prm = np.zeros((128, 64), np.float32)
    prm[:, 0:16] = f("norm_g")[0].reshape(16, 128).T
    mu = f("shift_mu")[0]
    for i, off in enumerate([0, R, 2 * R]):
        prm[:, 16 + 2 * i:18 + 2 * i] = mu[off + g * 256: off + (g + 1) * 256].reshape(2, 128).T
    prm[:, 22] = mu[3 * R:3 * R + 128]
    prm[:, 23] = mu[3 * R + 128:3 * R + 256]
    prm[0:32, 24] = mu[3 * R + 256:3 * R + 288]
    for i, n in enumerate(["w0", "a0", "k_k", "k_a"]):
        prm[:, 25 + 2 * i:27 + 2 * i] = f(n)[0][hs].reshape(2, 128).T
    prm[:, 33:35] = f("r_k")[0].reshape(-1)[hs].reshape(2, 128).T
    lg = f("lnx_g")[0][hs].reshape(4, 64)
    lb = f("lnx_b")[0][hs].reshape(4, 64)
    prm[0:64, 40:44] = lg.T
    prm[0:64, 44:48] = lb.T
    cm = np.zeros((128, 768), np.float32)
    cm[0:64, 0:256] = f("w2")[0][:, hs]
    cm[64:128, 0:256] = f("a2")[0][:, hs]
    cm[:, 256:512] = f("g2")[0][0:128, hs]
    cm[0:32, 512:768] = f("g2")[0][128:160, hs]
    return {"xT": xT, "wsl": wsl, "prm": prm, "cm": cm, "cb": cb, "cf": cf, "tab": tab}


_CACHE = {}


def kernel(**inputs):
    consts = _consts()
    if "l1" not in _CACHE:
        _CACHE["l1"] = build_l1()
        _CACHE["l2"] = build_l2()
    in1 = [_prep_l1(inputs, c // 4, c % 4, consts) for c in range(8)]
    r1 = run_bass_kernel_spmd(_CACHE["l1"], in1, core_ids=list(range(8))).results
    x = np.asarray(inputs["x"], dtype=np.float32)
    wout = np.asarray(inputs["w_out"], dtype=np.float32)[0]
    gfin = np.ascontiguousarray(np.broadcast_to(np.asarray(inputs["final_g"], dtype=np.float32)[None, :], (128, D)))
    in2 = []
    for c in range(8):
        b, qd = c // 4, c % 4
        ts = slice(qd * 2048, (qd + 1) * 2048)
        mt = np.empty((D, 2048), np.float32)
        for g in range(4):
            m = np.asarray(r1[b * 4 + g]["mix"])
            mt[g * 256:(g + 1) * 256] = m[0:256, ts]
            mt[A + g * 256:A + (g + 1) * 256] = m[256:512, ts]
        mt = np.ascontiguousarray(mt.reshape(16, 128, 16, 128).transpose(2, 1, 0, 3)).reshape(16, 128, 16 * 128)
        in2.append({"mixT": mt, "wout": wout, "xin": np.ascontiguousarray(x[b, ts]), "gfin": gfin})
    r2 = run_bass_kernel_spmd(_CACHE["l2"], in2, core_ids=list(range(8))).results
    out = np.empty((2, S, D), np.float32)
    for c in range(8):
        b, qd = c // 4, c % 4
        out[b, qd * 2048:(qd + 1) * 2048] = np.asarray(r2[c]["out"])
    return out
```
